# Optimizing a Trainium2 kernel written in Bass

```python
import math
import jax, jax.numpy as jnp
from jax import lax
import numpy as np

D_MODEL = 1024
BATCH = 16
SEQ = 2048
DEPTH = 2

D_RNN = D_MODEL
RNN_BLOCKS = 8
RNN_BLOCK_W = D_RNN // RNN_BLOCKS
CONV_RNN = 4
RG_LRU_C = 8.0
DIFF_HEADS = 4
DIFF_HEAD_DIM = 64
DIFF_QK = DIFF_HEADS * 2 * DIFF_HEAD_DIM
DIFF_WIDTH = DIFF_HEADS * 2 * DIFF_HEAD_DIM
FOX_HEADS = 8
FOX_HEAD_DIM = 64
FOX_WIDTH = FOX_HEADS * FOX_HEAD_DIM
N_BUCKETS = 32
MAX_EXACT = N_BUCKETS // 2
MAX_DISTANCE = 128
Q_BLOCK = 128
D_FF = ((8 * D_MODEL // 3 + 127) // 128) * 128
CONV_FFN = 3
N_BRANCH = 3
EPS = 1e-6

IN_WIDTHS = (D_RNN, D_RNN, DIFF_QK, DIFF_QK, DIFF_WIDTH, FOX_WIDTH, FOX_WIDTH, FOX_WIDTH, FOX_HEADS, N_BRANCH * D_MODEL)
N_IN = int(sum(IN_WIDTHS))
IN_SPLITS = tuple(int(v) for v in np.cumsum(IN_WIDTHS)[:-1])

kernel_name = "hybrid_rglru_diffattn_fox_gated_trunk"


def rmsnorm(x, g):
    xf = x.astype(jnp.float32)
    y = xf * lax.rsqrt(jnp.mean(xf * xf, axis=-1, keepdims=True) + EPS)
    return (y * g.astype(jnp.float32)).astype(x.dtype)


def causal_dwconv(x, w, b):
    k = w.shape[0]
    s = x.shape[1]
    xp = jnp.pad(x, ((0, 0), (k - 1, 0), (0, 0)))
    y = b + xp[:, 0:s] * w[0]
    for j in range(1, k):
        y = y + xp[:, j:j + s] * w[j]
    return y


def _to_blocks(t):
    b, s = t.shape[:2]
    return jnp.moveaxis(t.reshape(b, s // Q_BLOCK, Q_BLOCK, *t.shape[2:]), 1, 0)


def _from_blocks(t):
    nb, b = t.shape[:2]
    t = jnp.moveaxis(t, 0, 1)
    return t.reshape(b, nb * Q_BLOCK, *t.shape[3:])


def t5_bucket(dist):
    n = jnp.maximum(dist, 0)
    nf = jnp.maximum(n, 1).astype(jnp.float32)
    large = MAX_EXACT + (jnp.log(nf / MAX_EXACT) / math.log(MAX_DISTANCE / MAX_EXACT)
                         * (N_BUCKETS - MAX_EXACT)).astype(jnp.int32)
    large = jnp.minimum(large, N_BUCKETS - 1)
    return jnp.where(n < MAX_EXACT, n, large)


def rg_lru(x, w_r, b_r, w_i, b_i, a_param):
    bsz, s, c = x.shape
    xb = x.reshape(bsz, s, RNN_BLOCKS, RNN_BLOCK_W)
    r = jax.nn.sigmoid(jnp.einsum('bsnc,ncd->bsnd', xb, w_r).reshape(bsz, s, c) + b_r)
    i = jax.nn.sigmoid(jnp.einsum('bsnc,ncd->bsnd', xb, w_i).reshape(bsz, s, c) + b_i)
    log_a = (-RG_LRU_C * r.astype(jnp.float32)) * jax.nn.softplus(-a_param.astype(jnp.float32))
    a = jnp.exp(log_a)
    u = jnp.sqrt(-jnp.expm1(2.0 * log_a)) * (i * x).astype(jnp.float32)

    def combine(left, right):
        a1, b1 = left
        a2, b2 = right
        return a1 * a2, a2 * b1 + b2

    _, h = lax.associative_scan(combine, (a, u), axis=1)
    return h.astype(x.dtype)


def diff_attention(q, k, v, lam, rel_table):
    s_len = q.shape[1]
    scale = DIFF_HEAD_DIM ** -0.5
    k_pos = jnp.arange(s_len)

    def one_block(args):
        qb, blk = args
        q_pos = blk * Q_BLOCK + jnp.arange(Q_BLOCK)
        dist = q_pos[:, None] - k_pos[None, :]
        bias = jnp.moveaxis(rel_table[t5_bucket(dist)], -1, 0)
        sc = jnp.einsum('bqhcd,bkhcd->bchqk', qb, k).astype(jnp.float32) * scale + bias.astype(jnp.float32)
        sc = jnp.where(dist >= 0, sc, -jnp.inf)
        p = jax.nn.softmax(sc, axis=-1)
        w = p[:, 0] - lam * p[:, 1]
        return jnp.einsum('bhqk,bkhd->bqhd', w.astype(v.dtype), v)

    out = lax.map(one_block, (_to_blocks(q), jnp.arange(s_len // Q_BLOCK)))
    return _from_blocks(out)


def forgetting_attention(q, k, v, log_f):
    s_len = q.shape[1]
    scale = FOX_HEAD_DIM ** -0.5
    k_pos = jnp.arange(s_len)
    c = jnp.cumsum(log_f.astype(jnp.float32), axis=1)
    ck = jnp.moveaxis(c, -1, 1)

    def one_block(args):
        qb, cq, blk = args
        q_pos = blk * Q_BLOCK + jnp.arange(Q_BLOCK)
        dist = q_pos[:, None] - k_pos[None, :]
        decay = jnp.moveaxis(cq, -1, 1)[..., None] - ck[:, :, None, :]
        sc = jnp.einsum('bqhd,bkhd->bhqk', qb, k).astype(jnp.float32) * scale + decay
        sc = jnp.where(dist >= 0, sc, -jnp.inf)
        p = jax.nn.softmax(sc, axis=-1)
        return jnp.einsum('bhqk,bkhd->bqhd', p.astype(v.dtype), v)

    out = lax.map(one_block, (_to_blocks(q), _to_blocks(c), jnp.arange(s_len // Q_BLOCK)))
    return _from_blocks(out)


def setup_inputs(seed: int = 0) -> dict:
    key = jax.random.key(seed)
    ks = jax.random.split(key, 32)
    L = DEPTH

    def nrm(k, shape, scale):
        return jax.random.normal(k, shape, jnp.float32) * scale

    a_target = jax.random.uniform(ks[9], (L, D_RNN), jnp.float32, 0.9, 0.999)
    sig = a_target ** (1.0 / RG_LRU_C)
    return {
        "x": nrm(ks[0], (BATCH, SEQ, D_MODEL), 1.0),
        "norm1_g": 1.0 + nrm(ks[1], (L, D_MODEL), 0.02),
        "w_in": nrm(ks[2], (L, D_MODEL, N_IN), D_MODEL ** -0.5),
        "rnn_conv_w": nrm(ks[3], (L, CONV_RNN, D_RNN), CONV_RNN ** -0.5),
        "rnn_conv_b": nrm(ks[4], (L, D_RNN), 0.02),
        "rg_w_r": nrm(ks[5], (L, RNN_BLOCKS, RNN_BLOCK_W, RNN_BLOCK_W), RNN_BLOCK_W ** -0.5),
        "rg_b_r": nrm(ks[6], (L, D_RNN), 0.02),
        "rg_w_i": nrm(ks[7], (L, RNN_BLOCKS, RNN_BLOCK_W, RNN_BLOCK_W), RNN_BLOCK_W ** -0.5),
        "rg_b_i": nrm(ks[8], (L, D_RNN), 0.02),
        "rg_a": jnp.log(sig) - jnp.log1p(-sig),
        "diff_lq1": nrm(ks[10], (L, DIFF_HEAD_DIM), 0.1),
        "diff_lk1": nrm(ks[11], (L, DIFF_HEAD_DIM), 0.1),
        "diff_lq2": nrm(ks[12], (L, DIFF_HEAD_DIM), 0.1),
        "diff_lk2": nrm(ks[13], (L, DIFF_HEAD_DIM), 0.1),
        "diff_subln_g": 1.0 + nrm(ks[14], (L, 2 * DIFF_HEAD_DIM), 0.02),
        "rel_bias": nrm(ks[15], (N_BUCKETS, DIFF_HEADS), 0.5),
        "fox_b_f": 3.0 + nrm(ks[16], (L, FOX_HEADS), 0.5),
        "gate_b": nrm(ks[17], (L, N_BRANCH, D_MODEL), 0.02),
        "w_br_rnn": nrm(ks[18], (L, D_RNN, D_MODEL), D_RNN ** -0.5),
        "w_br_diff": nrm(ks[19], (L, DIFF_WIDTH, D_MODEL), DIFF_WIDTH ** -0.5),
        "w_br_fox": nrm(ks[20], (L, FOX_WIDTH, D_MODEL), FOX_WIDTH ** -0.5),
        "w_out": nrm(ks[21], (L, D_MODEL, D_MODEL), D_MODEL ** -0.5),
        "norm2_g": 1.0 + nrm(ks[22], (L, D_MODEL), 0.02),
        "ffn_up": nrm(ks[23], (L, D_MODEL, 2 * D_FF), D_MODEL ** -0.5),
        "ffn_conv_w": nrm(ks[24], (L, CONV_FFN, 2 * D_FF), CONV_FFN ** -0.5),
        "ffn_conv_b": nrm(ks[25], (L, 2 * D_FF), 0.02),
        "ffn_down": nrm(ks[26], (L, D_FF, D_MODEL), D_FF ** -0.5),
        "final_g": 1.0 + nrm(ks[27], (D_MODEL,), 0.02),
    }


def reference(x, norm1_g, w_in, rnn_conv_w, rnn_conv_b, rg_w_r, rg_b_r, rg_w_i, rg_b_i, rg_a,
              diff_lq1, diff_lk1, diff_lq2, diff_lk2, diff_subln_g, rel_bias, fox_b_f, gate_b,
              w_br_rnn, w_br_diff, w_br_fox, w_out, norm2_g, ffn_up, ffn_conv_w, ffn_conv_b,
              ffn_down, final_g):
    bsz, s_len, _ = x.shape
    for l in range(DEPTH):
        h = rmsnorm(x, norm1_g[l])
        proj = h @ w_in[l]
        (x_rnn, g_rnn, dq, dk, dv, fq, fk, fv, f_logit, gates) = jnp.split(proj, IN_SPLITS, axis=-1)

        x_rnn = causal_dwconv(x_rnn, rnn_conv_w[l], rnn_conv_b[l])
        y_rnn = jax.nn.gelu(g_rnn) * rg_lru(x_rnn, rg_w_r[l], rg_b_r[l], rg_w_i[l], rg_b_i[l], rg_a[l])

        lam_init = 0.8 - 0.6 * math.exp(-0.3 * l)
        lam = (jnp.exp(jnp.sum(diff_lq1[l] * diff_lk1[l]).astype(jnp.float32))
               - jnp.exp(jnp.sum(diff_lq2[l] * diff_lk2[l]).astype(jnp.float32)) + lam_init)
        o_diff = diff_attention(dq.reshape(bsz, s_len, DIFF_HEADS, 2, DIFF_HEAD_DIM),
                                dk.reshape(bsz, s_len, DIFF_HEADS, 2, DIFF_HEAD_DIM),
                                dv.reshape(bsz, s_len, DIFF_HEADS, 2 * DIFF_HEAD_DIM), lam, rel_bias)
        y_diff = (rmsnorm(o_diff, diff_subln_g[l]) * (1.0 - lam_init)).reshape(bsz, s_len, DIFF_WIDTH)

        log_f = jax.nn.log_sigmoid((f_logit + fox_b_f[l]).astype(jnp.float32))
        o_fox = forgetting_attention(fq.reshape(bsz, s_len, FOX_HEADS, FOX_HEAD_DIM),
                                     fk.reshape(bsz, s_len, FOX_HEADS, FOX_HEAD_DIM),
                                     fv.reshape(bsz, s_len, FOX_HEADS, FOX_HEAD_DIM), log_f)
        y_fox = o_fox.reshape(bsz, s_len, FOX_WIDTH)

        g = jax.nn.sigmoid(gates.reshape(bsz, s_len, N_BRANCH, D_MODEL) + gate_b[l])
        m = (g[:, :, 0] * (y_rnn @ w_br_rnn[l])
             + g[:, :, 1] * (y_diff @ w_br_diff[l])
             + g[:, :, 2] * (y_fox @ w_br_fox[l]))
        x = x + m @ w_out[l]

        h = rmsnorm(x, norm2_g[l])
        u = causal_dwconv(h @ ffn_up[l], ffn_conv_w[l], ffn_conv_b[l])
        u_gate, u_val = jnp.split(u, 2, axis=-1)
        x = x + (jax.nn.gelu(u_gate) * u_val) @ ffn_down[l]
    return rmsnorm(x, final_g)
```

```python
import contextlib
import math
import numpy as np
import concourse.bass as bass
import concourse.mybir as mybir
from concourse.bass_utils import run_bass_kernel_spmd

F32 = mybir.dt.float32
BF16 = mybir.dt.bfloat16
AF = mybir.ActivationFunctionType
ALU = mybir.AluOpType

D = 1024
KC = 8
NIN = 8200
DFF = 2816
NFC = 22
EPS = 1e-6
NEG = -30000.0
O_XR, O_GR, O_DQ, O_DK, O_DV, O_FQ, O_FK, O_FV, O_FL, O_GT = 0, 1024, 2048, 2560, 3072, 3584, 4096, 4608, 5120, 5128


class _Op:
    __slots__ = ("eng", "fn", "waits", "signal", "idx", "count", "dma", "dsem", "dcount", "clock")

    def __init__(self, eng, fn, dma):
        self.eng = eng; self.fn = fn; self.waits = []; self.signal = False; self.idx = -1
        self.count = 0; self.dma = dma; self.dsem = None; self.dcount = 0; self.clock = None


class Sched:
    ENG = ("pe", "act", "dve", "pool", "sp")
    NDSEM = 24

    def __init__(self, nc):
        self.nc = nc
        self.ops = {e: [] for e in self.ENG}
        self.lastw = {}
        self.readers = {}
        self.seen = {e: {f: -1 for f in self.ENG} for e in self.ENG}
        self.seen_d = {e: [0] * self.NDSEM for e in self.ENG}
        self.ndma = 0
        self.dma_ops = []

    def op(self, eng, fn, r=(), w=(), dma=False):
        o = _Op(eng, fn, dma)
        o.idx = len(self.ops[eng])
        r = tuple(r) + ("PHASE",)
        deps = []
        for k in r:
            x = self.lastw.get(k)
            if x is not None:
                deps.append(x)
        for k in w:
            x = self.lastw.get(k)
            if x is not None:
                deps.append(x)
            deps.extend(self.readers.get(k, ()))
        if dma:
            j = self.ndma
            self.ndma += 1
            o.dsem = j % self.NDSEM
            o.dcount = 16 * (j // self.NDSEM + 1)
            if j >= self.NDSEM:
                deps.append(self.dma_ops[j - self.NDSEM])
            self.dma_ops.append(o)
        seen = self.seen[eng]
        sd = self.seen_d[eng]
        best_e = {}
        best_d = {}
        for d in deps:
            if d is o:
                continue
            if d.dma:
                if d.dcount > sd[d.dsem] and (d.dsem not in best_d or d.dcount > best_d[d.dsem].dcount):
                    best_d[d.dsem] = d
            else:
                if d.eng == eng and eng == "pe":
                    continue
                if d.idx > seen[d.eng] and (d.eng not in best_e or d.idx > best_e[d.eng].idx):
                    best_e[d.eng] = d
        for d in best_e.values():
            if seen[d.eng] >= d.idx:
                continue
            d.signal = True
            o.waits.append(d)
            seen[d.eng] = d.idx
            if d.clock is not None:
                for f, v in d.clock.items():
                    if f != eng and v > seen[f]:
                        seen[f] = v
        for d in best_d.values():
            sd[d.dsem] = d.dcount
            o.waits.append(d)
        if not dma:
            o.clock = dict(seen)
            o.clock[eng] = o.idx
        for k in r:
            self.readers.setdefault(k, []).append(o)
        for k in w:
            self.lastw[k] = o
            self.readers[k] = []
        self.ops[eng].append(o)
        return o

    def emit(self):
        nc = self.nc
        with contextlib.ExitStack() as st:
            esem = {e: st.enter_context(nc.semaphore("s_" + e)) for e in self.ENG}
            dsem = [st.enter_context(nc.semaphore("d%d" % i)) for i in range(self.NDSEM)]
            for e in self.ENG:
                c = 0
                for o in self.ops[e]:
                    if o.signal and not o.dma:
                        c += 1
                        o.count = c
            last = {}
            for o in self.dma_ops:
                last[o.dsem] = o
            final = list(last.values())
            block = st.enter_context(nc.Block())

            def run(e, eng):
                for o in self.ops[e]:
                    for d in o.waits:
                        if d.dma:
                            eng.wait_ge(dsem[d.dsem], d.dcount)
                        else:
                            eng.wait_ge(esem[d.eng], d.count)
                    ins = o.fn(eng)
                    if o.dma:
                        ins.then_inc(dsem[o.dsem], 16)
                    elif o.signal:
                        ins.then_inc(esem[e], 1)
                if e == "sp":
                    for o in final:
                        eng.wait_ge(dsem[o.dsem], o.dcount)

            @block.tensor
            def _(eng):
                run("pe", eng)

            @block.scalar
            def _(eng):
                run("act", eng)

            @block.vector
            def _(eng):
                run("dve", eng)

            @block.gpsimd
            def _(eng):
                run("pool", eng)

            @block.sync
            def _(eng):
                run("sp", eng)


def _col_layout(L):
    off = {}
    n = 0
    for name, w in (("n1", L * 8), ("n2", L * 8), ("fin", 8), ("cw", L * 4 * 8), ("cb", L * 8), ("br", L * 8),
                    ("bi", L * 8), ("ra", L * 8), ("gb", L * 3 * 8), ("fw", L * 3 * 44), ("fb", L * 44), ("sg", L)):
        off[name] = n
        n += w
    return off, n


def _row_layout(L):
    off = {}
    n = 0
    for name, w in (("lq", L * 4 * 64), ("bf", L * 128), ("ch", 4)):
        off[name] = n
        n += w
    return off, n


def _t5_bucket_np(dist):
    n = np.maximum(dist, 0)
    nf = np.maximum(n, 1).astype(np.float32)
    large = 16 + (np.log(nf / np.float32(16)) / np.float32(math.log(128 / 16)) * np.float32(16)).astype(np.int32)
    large = np.minimum(large, 31)
    return np.where(n < 16, n, large)


def _host_params(inp, L):
    co, nc_ = _col_layout(L)
    ro, nr = _row_layout(L)
    pc = np.zeros((128, nc_), np.float32)

    def cols(v):
        v = np.asarray(v, np.float32)
        lead = int(np.prod(v.shape[:-1])) if v.ndim > 1 else 1
        k = v.shape[-1] // 128
        return v.reshape(lead, k, 128).transpose(2, 0, 1).reshape(128, lead * k)

    pc[:, co["n1"]:co["n1"] + L * 8] = cols(inp["norm1_g"][:L])
    pc[:, co["n2"]:co["n2"] + L * 8] = cols(inp["norm2_g"][:L])
    pc[:, co["fin"]:co["fin"] + 8] = cols(inp["final_g"])
    pc[:, co["cw"]:co["cw"] + L * 32] = cols(inp["rnn_conv_w"][:L])
    pc[:, co["cb"]:co["cb"] + L * 8] = cols(inp["rnn_conv_b"][:L])
    pc[:, co["br"]:co["br"] + L * 8] = cols(inp["rg_b_r"][:L])
    pc[:, co["bi"]:co["bi"] + L * 8] = cols(inp["rg_b_i"][:L])
    pc[:, co["ra"]:co["ra"] + L * 8] = cols(inp["rg_a"][:L])
    pc[:, co["gb"]:co["gb"] + L * 24] = cols(inp["gate_b"][:L])
    pc[:, co["fw"]:co["fw"] + L * 132] = cols(inp["ffn_conv_w"][:L])
    pc[:, co["fb"]:co["fb"] + L * 44] = cols(inp["ffn_conv_b"][:L])
    pc[:, co["sg"]:co["sg"] + L] = np.asarray(inp["diff_subln_g"][:L], np.float32).T
    pr = np.zeros((128, nr), np.float32)
    lq = np.stack([np.asarray(inp[k][:L], np.float32) for k in ("diff_lq1", "diff_lk1", "diff_lq2", "diff_lk2")], 1)
    pr[:, ro["lq"]:ro["lq"] + L * 256] = np.broadcast_to(lq.reshape(1, L * 256), (128, L * 256))
    bf = np.tile(np.asarray(inp["fox_b_f"][:L], np.float32).reshape(L, 1, 8), (1, 16, 1)).reshape(1, L * 128)
    pr[:, ro["bf"]:ro["bf"] + L * 128] = np.broadcast_to(bf, (128, L * 128))
    rel = np.asarray(inp["rel_bias"], np.float32)
    pr[:, ro["ch"]:ro["ch"] + 4] = np.broadcast_to(rel[31:32, :], (128, 4))
    kl = np.arange(128)[:, None]
    sx = np.arange(256)[None, :]
    dist = sx - kl
    bidx = _t5_bucket_np(dist)
    strips = np.zeros((128, 4 * 256), np.float32)
    for h in range(4):
        g = rel[bidx, h]
        strips[:, h * 256:(h + 1) * 256] = np.where(dist >= 0, g, np.float32(NEG))
    cst = np.zeros((128, 4 * 128), np.float32)
    cst[:, 0:128] = np.eye(128, dtype=np.float32)
    cst[:, 128:256] = (np.arange(128)[:, None] <= np.arange(128)[None, :]).astype(np.float32)
    cst[:, 256:384] = np.where(np.arange(128)[:, None] <= np.arange(128)[None, :], 0.0, NEG)
    cst[:, 384:512] = (np.arange(128)[:, None] == (np.arange(128)[None, :] + 64) % 128).astype(np.float32)
    return pc, pr, strips, cst


def build_nc(S=2048, L=2, NSEQ=2):
    assert S % 1024 == 0
    NT = S // 512
    NB = S // 128
    HT = S // 2
    TPH = NT // 2
    co, ncol = _col_layout(L)
    ro, nrow = _row_layout(L)

    nc = bass.Bass("TRN2", target_bir_lowering=False)

    def din(name, shape):
        return nc.dram_tensor(name, shape, F32, kind="ExternalInput").ap()

    x_d = din("x", [NSEQ, S, D])
    w_in = din("w_in", [L, D, NIN])
    rg_wr = din("rg_w_r", [L, 8, 128, 128])
    rg_wi = din("rg_w_i", [L, 8, 128, 128])
    w_br = [din("w_br_rnn", [L, D, D]), din("w_br_diff", [L, 512, D]), din("w_br_fox", [L, 512, D])]
    w_out = din("w_out", [L, D, D])
    ffn_up = din("ffn_up", [L, D, 2 * DFF])
    ffn_dn = din("ffn_down", [L, DFF, D])
    pcol_d = din("pcols", [128, ncol])
    prow_d = din("prows", [128, nrow])
    strip_d = din("strips", [128, 1024])
    cst_d = din("consts", [128, 512])
    out_d = nc.dram_tensor("out", [NSEQ, S, D], F32, kind="ExternalOutput").ap()
    yscr = nc.dram_tensor("yscr", [16, 128, S], BF16, kind="Internal").ap()

    with contextlib.ExitStack() as st:
        def sb(name, shape, dt=F32):
            return st.enter_context(nc.sbuf_tensor(name, shape, dt))

        S_ = Sched(nc)
        xT = sb("xT", [128, KC, S])
        hT = sb("hT", [128, KC, S], BF16)
        ARN = 17424
        arena = sb("arena", [128, ARN])
        NSTG, NWB = 2, 8
        stg = [sb("stg%d" % i, [128, 8, 128]) for i in range(NSTG)]
        wbs = [sb("wb%d" % i, [128, 8, 128], BF16) for i in range(NWB)]
        pcol = sb("pcol_sb", [128, ncol])
        prow = sb("prow_sb", [128, nrow])
        cst = sb("cst_sb", [128, 512])
        strips = sb("strips_sb", [128, 1024])
        dcol = sb("dcol", [128, 16 * L + 8])
        hcol = sb("hcol", [128, L * 48 + 8])
        ones32 = sb("ones32", [128, 128])
        onesb = sb("onesb", [128, 128], BF16)
        phz = sb("phz", [128, 1])
        banks = [st.enter_context(nc.psum_tensor("bank%d" % i, [128, 512], F32)) for i in range(8)]
        ident = cst[:, 0:128]
        tri = cst[:, 128:256]
        cmask = cst[:, 256:384]
        shm = cst[:, 384:512]

        def MM(out, lhsT, rhs, start, stop, r, w):
            S_.op("pe", lambda e: e.matmul(out, lhsT=lhsT, rhs=rhs, start=start, stop=stop), r=r, w=w)

        def TR(out, in_, r, w):
            S_.op("pe", lambda e: e.transpose(out=out, in_=in_, identity=ident), r=tuple(r) + ("cst",), w=w)

        def ACT(out, in_, func, r, w, bias=None, scale=None):
            kw = {}
            if bias is not None:
                kw["bias"] = bias
            if scale is not None:
                kw["scale"] = scale
            S_.op("act", lambda e: e.activation(out=out, in_=in_, func=func, **kw), r=r, w=w)

        def TT(eng, out, in0, in1, op, r, w):
            S_.op(eng, lambda e: e.tensor_tensor(out=out, in0=in0, in1=in1, op=op), r=r, w=w)

        def TS(eng, out, in0, s1, s2, op0, op1, r, w):
            if s2 is None:
                S_.op(eng, lambda e: e.tensor_scalar(out=out, in0=in0, scalar1=s1, scalar2=None, op0=op0), r=r, w=w)
            else:
                S_.op(eng, lambda e: e.tensor_scalar(out=out, in0=in0, scalar1=s1, scalar2=s2, op0=op0, op1=op1), r=r, w=w)

        def STT(out, in0, scalar, in1, op0, op1, r, w):
            S_.op("dve", lambda e: e.scalar_tensor_tensor(out=out, in0=in0, scalar=scalar, in1=in1, op0=op0, op1=op1), r=r, w=w)

        def CP(eng, out, in_, r, w):
            S_.op(eng, lambda e: e.tensor_copy(out=out, in_=in_), r=r, w=w)

        def RECIP(out, in_, r, w):
            S_.op("dve", lambda e: e.reciprocal(out=out, in_=in_), r=r, w=w)

        def MEMSET(eng, ap, val, w):
            S_.op(eng, lambda e: e.memset(ap, val), w=w)

        def DMA(out, in_, r, w):
            S_.op("sp", lambda e: e.dma_start(out=out, in_=in_), r=r, w=w, dma=True)

        def barrier():
            S_.op("dve", lambda e: e.memset(phz[:], 0.0), w=("PHASE", "phz"))

        apos = [0]

        def areset():
            barrier()
            apos[0] = 0

        def aalloc(nelem, dt=F32):
            n32 = nelem if dt == F32 else (nelem + 1) // 2
            n32 = (n32 + 7) // 8 * 8
            a0 = apos[0]
            apos[0] += n32
            assert apos[0] <= ARN, ("arena overflow", apos[0])
            v = arena[:, a0:a0 + n32]
            if dt != F32:
                v = v.bitcast(dt)[:, 0:nelem]
            else:
                v = v[:, 0:nelem]
            return v

        roles = {"mm": [0, 1], "sc": [2, 3], "acc": [4, 5, 6, 7], "pj": [0, 4, 5, 6, 7]}
        rpos = {"mm": 0, "sc": 0, "acc": 0, "pj": 0}

        def bank(role):
            lst = roles[role]
            i = lst[rpos[role] % len(lst)]
            rpos[role] += 1
            return banks[i], "bank%d" % i

        wcnt = [0, 0]

        def LW(src, kc, ncols, dst=None, dkey=None):
            i = wcnt[0] % NSTG
            wcnt[0] += 1
            DMA(stg[i][:, 0:kc, 0:ncols], src, r=(), w=("stg%d" % i,))
            if dst is None:
                j = wcnt[1] % NWB
                wcnt[1] += 1
                dst = wbs[j][:, 0:kc, 0:ncols]
                dkey = "wb%d" % j
                ret = wbs[j]
            else:
                ret = None
            if wcnt[0] % 3 == 0:
                ACT(dst, stg[i][:, 0:kc, 0:ncols], AF.Copy, r=("stg%d" % i,), w=(dkey,))
            else:
                CP("pool", dst, stg[i][:, 0:kc, 0:ncols], r=("stg%d" % i,), w=(dkey,))
            return ret, dkey

        def wview(ap2d, c0, ncols):
            return ap2d.rearrange("(k p) n -> p k n", p=128)[:, :, c0:c0 + ncols]

        DMA(pcol[:], pcol_d, r=(), w=("pcol",))
        DMA(prow[:], prow_d, r=(), w=("prow",))
        DMA(cst[:], cst_d, r=(), w=("cst",))
        DMA(strips[:], strip_d, r=(), w=("strips",))
        MEMSET("dve", ones32[:], 1.0, w=("ones32",))
        MEMSET("dve", onesb[:], 1.0, w=("onesb",))
        DC_EPS, DC_ONE = 16 * L, 16 * L + 1
        MEMSET("dve", dcol[:, DC_EPS:DC_EPS + 1], EPS, w=("dcol",))
        MEMSET("dve", dcol[:, DC_ONE:DC_ONE + 1], 1.0, w=("dcol",))
        epsc = dcol[:, DC_EPS:DC_EPS + 1]
        onec = dcol[:, DC_ONE:DC_ONE + 1]
        for h in range(4):
            TS("dve", strips[:, h * 256:(h + 1) * 256], strips[:, h * 256:(h + 1) * 256],
               prow[:, ro["ch"] + h:ro["ch"] + h + 1], None, ALU.subtract, None, r=("strips", "prow"), w=("strips",))
        HB_R, HB_I, HB_G, HB_A, HB_C = 0, L * 8, L * 16, L * 40, L * 48
        TS("dve", hcol[:, HB_R:HB_R + L * 16], pcol[:, co["br"]:co["br"] + L * 16], 0.5, None, ALU.mult, None, r=("pcol",), w=("hcol",))
        TS("dve", hcol[:, HB_G:HB_G + L * 24], pcol[:, co["gb"]:co["gb"] + L * 24], 0.5, None, ALU.mult, None, r=("pcol",), w=("hcol",))
        MEMSET("dve", hcol[:, HB_C:HB_C + 1], 1.0 / 16, w=("hcol",))
        sixteenth = hcol[:, HB_C:HB_C + 1]
        lam_init = [0.8 - 0.6 * math.exp(-0.3 * l) for l in range(L)]
        for l in range(L):
            b0 = l * 16
            ACT(dcol[:, b0:b0 + 8], pcol[:, co["ra"] + l * 8:co["ra"] + l * 8 + 8], AF.Exp, r=("pcol",), w=("dcol",), scale=-1.0)
            ACT(dcol[:, b0:b0 + 8], dcol[:, b0:b0 + 8], AF.Ln, r=("dcol",), w=("dcol",), bias=onec)
            TS("dve", dcol[:, b0:b0 + 8], dcol[:, b0:b0 + 8], -8.0, None, ALU.mult, None, r=("dcol",), w=("dcol",))
            TS("dve", hcol[:, HB_A + l * 8:HB_A + l * 8 + 8], dcol[:, b0:b0 + 8], 0.5, None, ALU.mult, None, r=("dcol",), w=("hcol",))
            q0 = ro["lq"] + l * 256
            for pair in range(2):
                tmp = arena[:, 0:64]
                TT("dve", tmp, prow[:, q0 + pair * 128:q0 + pair * 128 + 64], prow[:, q0 + pair * 128 + 64:q0 + pair * 128 + 128],
                   ALU.mult, r=("prow",), w=("lamtmp",))
                S_.op("dve", lambda e, o_=dcol[:, b0 + 10 + pair:b0 + 11 + pair], i_=tmp: e.tensor_reduce(
                    out=o_, in_=i_, axis=mybir.AxisListType.X, op=ALU.add), r=("lamtmp",), w=("dcol",))
            ACT(dcol[:, b0 + 10:b0 + 12], dcol[:, b0 + 10:b0 + 12], AF.Exp, r=("dcol",), w=("dcol",))
            TT("dve", dcol[:, b0 + 8:b0 + 9], dcol[:, b0 + 11:b0 + 12], dcol[:, b0 + 10:b0 + 11], ALU.subtract, r=("dcol",), w=("dcol",))
            TS("dve", dcol[:, b0 + 8:b0 + 9], dcol[:, b0 + 8:b0 + 9], -lam_init[l], None, ALU.add, None, r=("dcol",), w=("dcol",))
            TS("dve", dcol[:, b0 + 9:b0 + 10], pcol[:, co["sg"] + l:co["sg"] + l + 1], 1.0 - lam_init[l], None, ALU.mult, None,
               r=("pcol",), w=("dcol",))

        def rstd_all(sq2, ms):
            for t in range(NT):
                pn, pk = bank("mm")
                for f in range(KC):
                    sq = sq2[f % 2]
                    ACT(sq[0], xT[:, f, t * 512:(t + 1) * 512], AF.Square, r=("xT%d_%d" % (f, t),), w=(sq[1],))
                    MM(pn[:], ones32[:], sq[0], f == 0, f == KC - 1, r=(sq[1], "ones32"), w=(pk,))
                ACT(ms[:, t * 512:(t + 1) * 512], pn[:], AF.Copy, r=(pk,), w=("ms",), scale=1.0 / D)
            ACT(ms, ms, AF.Sqrt, r=("ms", "dcol"), w=("ms",), bias=epsc)
            RECIP(ms, ms, r=("ms",), w=("ms",))

        def norm_to_hT(gbase):
            areset()
            sq2 = [(aalloc(512), "nsq0"), (aalloc(512), "nsq1")]
            ms = aalloc(S)
            rstd_all(sq2, ms)
            for t in range(NT):
                for f in range(KC):
                    STT(hT[:, f, t * 512:(t + 1) * 512], xT[:, f, t * 512:(t + 1) * 512], pcol[:, gbase + f:gbase + f + 1],
                        ms[:, t * 512:(t + 1) * 512], ALU.mult, ALU.mult, r=("xT%d_%d" % (f, t), "pcol", "ms"), w=("hT%d_%d" % (f, t),))

        def proj_fm(dst_fn, wt, wkey, kc, rhs_fn, rkeys_fn, ntiles, M=128, tiles=None, role="mm"):
            for t in (tiles if tiles is not None else range(ntiles)):
                p, pk = bank(role)
                for k in range(kc):
                    MM(p[0:M, :], wt[:, k, 0:M], rhs_fn(k, t), k == 0, k == kc - 1, r=(wkey,) + tuple(rkeys_fn(k, t)), w=(pk,))
                dst_fn(t, p, pk)

        def hT_rhs(k, t):
            return hT[:, k, t * 512:(t + 1) * 512]

        def hT_keys(k, t):
            return ("hT%d_%d" % (k, t),)

        def gelu2_from(dst, src, srckeys, tmp, tmpkey, dkey):
            ACT(tmp, src, AF.Square, r=srckeys, w=(tmpkey,), scale=0.21145921595661512)
            STT(tmp, tmp, 1.0, src, ALU.add, ALU.mult, r=(tmpkey,) + tuple(srckeys), w=(tmpkey,))
            ACT(tmp, tmp, AF.Tanh, r=(tmpkey,), w=(tmpkey,), scale=0.7978845608028654)
            STT(dst, tmp, 1.0, src, ALU.add, ALU.mult, r=(tmpkey,) + tuple(srckeys), w=(dkey,))

        for s in range(NSEQ):
            areset()
            xin = [(aalloc(1024), "xin0"), (aalloc(1024), "xin1")]
            for b in range(NB):
                xi = xin[b % 2]
                DMA(xi[0], x_d[s, b * 128:(b + 1) * 128, :], r=(), w=(xi[1],))
                for g in range(2):
                    p, pk = bank("mm")
                    for i in range(4):
                        f = g * 4 + i
                        TR(p[:, i * 128:(i + 1) * 128], xi[0][:, f * 128:(f + 1) * 128], r=(xi[1],), w=(pk,))
                    t = b // 4
                    wk = tuple("xT%d_%d" % (g * 4 + i, t) for i in range(4))
                    dstx = xT[:, g * 4:g * 4 + 4, b * 128:(b + 1) * 128]
                    srcx = p[:].rearrange("p (a b) -> p a b", a=4)
                    if g == 0:
                        CP("dve", dstx, srcx, r=(pk,), w=wk)
                    else:
                        ACT(dstx, srcx, AF.Copy, r=(pk,), w=wk)

            for l in range(L):
                dc0 = l * 16
                norm_to_hT(co["n1"] + l * 8)

                areset()
                roles["mm"] = [0, 1, 2, 3]
                xr = aalloc(3 + S)
                gg = [aalloc(S), aalloc(S)]
                xc = aalloc(S)
                xcb = aalloc(S, BF16)
                rr = aalloc(S)
                ii = aalloc(S)
                a2 = aalloc(S)
                hh = a2
                gtmp = [(aalloc(512), "gtmp0"), (aalloc(512), "gtmp1")]
                ybuf = aalloc(S, BF16)
                carry = aalloc(8)
                MEMSET("dve", xr[:, 0:3], 0.0, w=("xrh",))

                def sl(t):
                    return slice(t * 512, (t + 1) * 512)

                def rnn_W(n):
                    wx_ = LW(wview(w_in[l], O_XR + n * 128, 128), 8, 128)
                    wg_ = LW(wview(w_in[l], O_GR + n * 128, 128), 8, 128)
                    wr_ = LW(rg_wr[l, n].rearrange("(k p) n -> p k n", p=128), 1, 128)
                    wi_ = LW(rg_wi[l, n].rearrange("(k p) n -> p k n", p=128), 1, 128)
                    return wx_, wr_, wi_, wg_

                def rnn_Pg(n, wg_, t):
                    g_ = gg[n % 2]

                    def ev(t_, p, pk):
                        gt = gtmp[t_ % 2]
                        gelu2_from(g_[:, sl(t_)], p[:], (pk,), gt[0], gt[1], "gg%d_%d" % (n % 2, t_))
                    proj_fm(ev, wg_[0], wg_[1], 8, hT_rhs, hT_keys, NT, tiles=[t])

                def rnn_Px(n, wx_, t):
                    wx, wxk = wx_
                    cw = co["cw"] + (l * 4) * 8 + n
                    w3c = pcol[:, cw + 24:cw + 25]
                    cbc = pcol[:, co["cb"] + l * 8 + n:co["cb"] + l * 8 + n + 1]

                    def ev(t_, p, pk):
                        ACT(xr[:, 3 + t_ * 512:3 + (t_ + 1) * 512], p[:], AF.Copy, r=(pk,), w=("xr%d" % t_,))
                        ACT(xc[:, sl(t_)], p[:], AF.Identity, r=(pk, "pcol"), w=("xc%d" % t_,), bias=cbc, scale=w3c)
                    proj_fm(ev, wx, wxk, 8, hT_rhs, hT_keys, NT, tiles=[t])

                def rnn_R1(n, wr_, wi_, t, after_conv):
                    wr, wrk = wr_
                    wi, wik = wi_
                    cw = co["cw"] + (l * 4) * 8 + n
                    xk = ("xr%d" % t, "xr%d" % (t - 1) if t > 0 else "xrh", "pcol", "xc%d" % t)
                    for j in range(3):
                        STT(xc[:, sl(t)], xr[:, j + t * 512:j + (t + 1) * 512], pcol[:, cw + 8 * j:cw + 8 * j + 1], xc[:, sl(t)],
                            ALU.mult, ALU.add, r=xk, w=("xc%d" % t,))
                    ACT(xcb[:, sl(t)], xc[:, sl(t)], AF.Copy, r=("xc%d" % t,), w=("xcb%d" % t,))
                    after_conv()
                    hbr = hcol[:, HB_R + l * 8 + n:HB_R + l * 8 + n + 1]
                    hbi = hcol[:, HB_I + l * 8 + n:HB_I + l * 8 + n + 1]
                    hsa = hcol[:, HB_A + l * 8 + n:HB_A + l * 8 + n + 1]
                    proj_fm(lambda t_, p, pk: ACT(rr[:, sl(t)], p[:], AF.Tanh, r=(pk, "hcol"), w=("rr%d" % t,), bias=hbr, scale=0.5),
                            wr, wrk, 1, lambda k, t_: xcb[:, sl(t)], lambda k, t_: ("xcb%d" % t,), NT, tiles=[t])
                    proj_fm(lambda t_, p, pk: ACT(ii[:, sl(t)], p[:], AF.Tanh, r=(pk, "hcol"), w=("ii%d" % t,), bias=hbi, scale=0.5),
                            wi, wik, 1, lambda k, t_: xcb[:, sl(t)], lambda k, t_: ("xcb%d" % t,), NT, tiles=[t])
                    ACT(rr[:, sl(t)], rr[:, sl(t)], AF.Exp, r=("rr%d" % t, "hcol"), w=("rr%d" % t,), bias=hsa, scale=hsa)
                    TT("pool", a2[:, sl(t)], rr[:, sl(t)], rr[:, sl(t)], ALU.mult, r=("rr%d" % t,), w=("a2_%d" % t,))
                    STT(ii[:, sl(t)], ii[:, sl(t)], 1.0, xc[:, sl(t)], ALU.add, ALU.mult, r=("ii%d" % t, "xc%d" % t), w=("ii%d" % t,))

                def rnn_R2(n, t):
                    g_ = gg[n % 2]
                    TT("pool", ii[:, sl(t)], ii[:, sl(t)], a2[:, sl(t)], ALU.mult, r=("ii%d" % t, "a2_%d" % t), w=("ii%d" % t,))
                    init = 0.0 if t == 0 else carry[:, t - 1:t]
                    rk = ("rr%d" % t, "ii%d" % t, "a2_%d" % t) + (("carry%d" % (t - 1),) if t > 0 else ())
                    S_.op("dve", lambda e, o_=hh[:, sl(t)], d0=rr[:, sl(t)], d1=ii[:, sl(t)], init=init: e.tensor_tensor_scan(
                        out=o_, data0=d0, data1=d1, initial=init, op0=ALU.mult, op1=ALU.add), r=rk, w=("a2_%d" % t,))
                    if t + 1 < NT:
                        CP("pool", carry[:, t:t + 1], hh[:, (t + 1) * 512 - 1:(t + 1) * 512], r=("a2_%d" % t,), w=("carry%d" % t,))
                    TT("pool", ybuf[:, sl(t)], g_[:, sl(t)], hh[:, sl(t)], ALU.mult, r=("gg%d_%d" % (n % 2, t), "a2_%d" % t), w=("ybuf%d" % t,))

                def rnn_sqrt(n):
                    ak = tuple("a2_%d" % t for t in range(NT))
                    ACT(a2, a2, AF.Sqrt, r=ak + ("hcol",), w=ak, bias=sixteenth, scale=-1.0 / 16)

                def rnn_out(n):
                    DMA(yscr[n], ybuf, r=tuple("ybuf%d" % t for t in range(NT)), w=("yscr%d" % n,))

                W = {0: rnn_W(0)}
                for t in range(NT):
                    rnn_Pg(0, W[0][3], t)
                    rnn_Px(0, W[0][0], t)
                for n in range(8):
                    if n + 1 < 8:
                        W[n + 1] = rnn_W(n + 1)
                    for t in range(NT):
                        if n > 0:
                            rnn_R2(n - 1, t)
                        if n + 1 < 8:
                            rnn_Pg(n + 1, W[n + 1][3], t)
                        if n + 1 < 8 and t > 0:
                            rnn_R1(n, W[n][1], W[n][2], t, lambda t=t: rnn_Px(n + 1, W[n + 1][0], t - 1))
                        else:
                            rnn_R1(n, W[n][1], W[n][2], t, lambda: None)
                    if n > 0:
                        rnn_out(n - 1)
                    if n + 1 < 8:
                        rnn_Px(n + 1, W[n + 1][0], NT - 1)
                    rnn_sqrt(n)
                for t in range(NT):
                    rnn_R2(7, t)
                rnn_out(7)

                areset()
                roles["mm"] = [0]
                roles["sc"] = [1, 2, 3]
                Vt = aalloc(NB * 576, BF16).rearrange("p (a b) -> p a b", a=NB)
                MEMSET("pool", Vt[:, :, 512:576], 1.0, w=("Vt",))
                Wv = aalloc(8 * 512, BF16).rearrange("p (a b) -> p a b", a=8)
                qk = [aalloc(S, BF16) for _ in range(3)]
                Et = [(aalloc(512, BF16), "E%d" % i) for i in range(4)]
                rz = aalloc(512)
                oo = [aalloc(512), aalloc(512)]
                sq_ = aalloc(512)
                rs_ = aalloc(512)
                yt = [(aalloc(512, BF16), "yt%d" % i) for i in range(2)]
                ecnt = [0]

                def build_V(colbase):
                    for i in range(4):
                        LW(wview(w_in[l], colbase + i * 128, 128), 8, 128, dst=Wv[:, :, i * 128:(i + 1) * 128], dkey="Wv")
                    for b in range(NB):
                        p, pk = bank("pj")
                        for k in range(KC):
                            MM(p[:], hT[:, k, b * 128:(b + 1) * 128], Wv[:, k, :], k == 0, k == KC - 1,
                               r=("hT%d_%d" % (k, b // 4), "Wv"), w=(pk,))
                        if b % 2 == 0:
                            CP("dve", Vt[:, b, 0:512], p[:], r=(pk,), w=("Vt",))
                        else:
                            ACT(Vt[:, b, 0:512], p[:], AF.Copy, r=(pk,), w=("Vt",))

                MEMSET("dve", qk[0][64:128, :], 0.0, w=("qk0",))
                MEMSET("dve", qk[1][0:64, :], 0.0, w=("qk1",))

                def load_qk(colq, colk):
                    return LW(wview(w_in[l], colq, 128), 8, 128), LW(wview(w_in[l], colk, 128), 8, 128)

                def build_qk3(wq_, wk_):
                    def evq(t, p, pk):
                        ACT(qk[0][0:64, t * 512:(t + 1) * 512], p[0:64, :], AF.Copy, r=(pk,), w=("qk0",), scale=0.125)
                        ACT(qk[1][64:128, t * 512:(t + 1) * 512], p[64:128, :], AF.Copy, r=(pk,), w=("qk1",), scale=0.125)
                    proj_fm(evq, wq_[0], wq_[1], 8, hT_rhs, hT_keys, NT, role="pj")

                    def evk(t, p, pk):
                        CP("dve", qk[2][:, t * 512:(t + 1) * 512], p[:], r=(pk,), w=("qk2",))
                    proj_fm(evk, wk_[0], wk_[1], 8, hT_rhs, hT_keys, NT, role="pj")

                NE = len(Et)

                def attn_tasks(qT, qkey, kT, kkey, j, v_fn, use_pz, bias_fn, fix_fn, fin_fn):
                    nk = 4 * (j + 1)
                    st_ = {}
                    tasks = []
                    for kb in range(nk):
                        def A(kb=kb):
                            m = kb - 4 * j
                            c0 = max(0, 128 * m)
                            ps, psk = bank("sc")
                            MM(ps[:, c0:512], kT[:, kb * 128:(kb + 1) * 128], qT[:, j * 512 + c0:(j + 1) * 512], True, True,
                               r=(qkey, kkey), w=(psk,))
                            fix_fn(ps, psk, m)
                            E = Et[ecnt[0] % NE]
                            ecnt[0] += 1
                            bcol, bkeys = bias_fn(kb)
                            ACT(E[0][:, c0:512], ps[:, c0:512], AF.Exp, r=(psk,) + tuple(bkeys), w=(E[1],), bias=bcol)
                            st_[kb] = (E, c0)

                        def B(kb=kb):
                            if kb == 0:
                                st_["po"] = bank("acc")
                                st_["pz"] = bank("acc") if use_pz else (None, None)
                            po, pok = st_["po"]
                            pz, pzk = st_["pz"]
                            E, c0 = st_.pop(kb)
                            MM(po[:, c0:512], v_fn(kb), E[0][:, c0:512], kb == 0, kb == nk - 1, r=("Vt", E[1]), w=(pok,))
                            if use_pz:
                                MM(pz[:, c0:512], onesb[:], E[0][:, c0:512], kb == 0, kb == nk - 1, r=("onesb", E[1]), w=(pzk,))
                            if kb == nk - 1:
                                fin_fn(po, pok, pz, pzk)
                        tasks.append((A, B))
                    return tasks

                def run_tasks(tasks, LA=3):
                    n = len(tasks)
                    for i in range(n + LA):
                        if i < n:
                            tasks[i][0]()
                        if i - LA >= 0:
                            tasks[i - LA][1]()

                build_V(O_DV)
                wpre = load_qk(O_DQ, O_DK)
                for h in range(4):
                    build_qk3(*wpre)
                    wpre = load_qk(O_DQ + (h + 1) * 128, O_DK + (h + 1) * 128) if h + 1 < 4 else load_qk(O_FQ, O_FK)
                    G = strips[:, h * 256:(h + 1) * 256]
                    chc = prow[:, ro["ch"] + h:ro["ch"] + h + 1]

                    def fix_diff(ps, psk, m, G=G):
                        if m == -1:
                            TT("dve", ps[:, 0:128], ps[:, 0:128], G[:, 128:256], ALU.add, r=(psk, "strips"), w=(psk,))
                        elif m >= 0:
                            a_ = 128 * m
                            b_ = min(a_ + 256, 512)
                            TT("dve", ps[:, a_:b_], ps[:, a_:b_], G[:, 0:b_ - a_], ALU.add, r=(psk, "strips"), w=(psk,))

                    tasks = []
                    for j in range(NT):
                        for c in range(2):
                            def fin_d(po, pok, pz, pzk, j=j, c=c, h=h):
                                RECIP(rz, pz[:], r=(pzk,), w=("rz",))
                                TT("dve", oo[c], po[:], rz, ALU.mult, r=(pok, "rz"), w=("oo%d" % c,))
                                if c == 1:
                                    STT(oo[0], oo[1], dcol[:, dc0 + 8:dc0 + 9], oo[0], ALU.mult, ALU.add, r=("oo0", "oo1", "dcol"), w=("oo0",))
                                    ACT(sq_, oo[0], AF.Square, r=("oo0",), w=("sq_",))
                                    pn, pnk = bank("mm")
                                    MM(pn[:], ones32[:], sq_, True, True, r=("sq_", "ones32"), w=(pnk,))
                                    ACT(rs_, pn[:], AF.Sqrt, r=(pnk, "dcol"), w=("rs_",), bias=epsc, scale=1.0 / 128)
                                    RECIP(rs_, rs_, r=("rs_",), w=("rs_",))
                                    y_ = yt[j % 2]
                                    STT(y_[0], oo[0], dcol[:, dc0 + 9:dc0 + 10], rs_, ALU.mult, ALU.mult, r=("oo0", "dcol", "rs_"), w=(y_[1],))
                                    DMA(yscr[8 + h][:, j * 512:(j + 1) * 512], y_[0], r=(y_[1],), w=("yscr%d" % (8 + h),))
                            tasks += attn_tasks(qk[c], "qk%d" % c, qk[2], "qk2", j,
                                                lambda kb, h=h: Vt[:, kb, h * 128:(h + 1) * 128], True,
                                                lambda kb, chc=chc: (chc, ("prow",)), fix_diff, fin_d)
                    run_tasks(tasks)

                build_V(O_FV)
                wf, wfk = LW(wview(w_in[l], O_FL, 8), 8, 8)
                lf = aalloc(128)
                Dk = aalloc(128)
                pref = aalloc(128 + 8)
                ball = aalloc(NT * NB * 8)
                pf, pfk = bank("mm")
                for b in range(NB):
                    for k in range(KC):
                        MM(pf[:, b * 8:(b + 1) * 8], hT[:, k, b * 128:(b + 1) * 128], wf[:, k, 0:8], k == 0, k == KC - 1,
                           r=("hT%d_%d" % (k, b // 4), wfk), w=(pfk,))
                nb8 = NB * 8
                bf0 = ro["bf"] + l * 128
                TT("dve", lf[:, 0:nb8], pf[:, 0:nb8], prow[:, bf0:bf0 + nb8], ALU.add, r=(pfk, "prow"), w=("lf",))
                ACT(lf[:, 0:nb8], lf[:, 0:nb8], AF.Exp, r=("lf",), w=("lf",), scale=-1.0)
                ACT(lf[:, 0:nb8], lf[:, 0:nb8], AF.Ln, r=("lf", "dcol"), w=("lf",), bias=onec)
                pc_, pck = bank("sc")
                MM(pc_[:, 0:nb8], tri, lf[:, 0:nb8], True, True, r=("cst", "lf"), w=(pck,))
                ptot, ptk = bank("sc")
                MM(ptot[:, 0:nb8], ones32[:], lf[:, 0:nb8], True, True, r=("ones32", "lf"), w=(ptk,))
                MEMSET("dve", pref[:, 0:8], 0.0, w=("pref",))
                for b in range(1, NB + 1):
                    TT("dve", pref[:, b * 8:(b + 1) * 8], pref[:, (b - 1) * 8:b * 8], ptot[:, (b - 1) * 8:b * 8], ALU.add,
                       r=("pref", ptk), w=("pref",))
                TT("dve", Dk[:, 0:nb8], pc_[:, 0:nb8], pref[:, 0:nb8], ALU.add, r=(pck, "pref"), w=("Dk",))
                for j in range(NT):
                    rb = 4 * j + 2
                    for kb in range(4 * (j + 1)):
                        TT("dve", ball[:, (j * NB + kb) * 8:(j * NB + kb) * 8 + 8], Dk[:, kb * 8:kb * 8 + 8], pref[:, rb * 8:rb * 8 + 8],
                           ALU.subtract, r=("Dk", "pref"), w=("ball",))

                def fix_fox(ps, psk, m):
                    if m >= 0:
                        a_ = 128 * m
                        TT("dve", ps[:, a_:a_ + 128], ps[:, a_:a_ + 128], cmask, ALU.add, r=(psk, "cst"), w=(psk,))

                for pr_ in range(4):
                    build_qk3(*wpre)
                    if pr_ + 1 < 4:
                        wpre = load_qk(O_FQ + (pr_ + 1) * 128, O_FK + (pr_ + 1) * 128)
                    tasks = []
                    for j in range(NT):
                        for hx in range(2):
                            hd = 2 * pr_ + hx

                            def fin_f(po, pok, pz, pzk, j=j, hx=hx, pr_=pr_):
                                y_ = yt[j % 2]
                                lo, hi = hx * 64, hx * 64 + 64
                                RECIP(rz[lo:hi, :], pz[lo:hi, :], r=(pzk,), w=("rz",))
                                TT("dve", y_[0][lo:hi, :], po[lo:hi, :], rz[lo:hi, :], ALU.mult, r=(pok, "rz"), w=(y_[1],))
                                if hx == 1:
                                    DMA(yscr[12 + pr_][:, j * 512:(j + 1) * 512], y_[0], r=(y_[1],), w=("yscr%d" % (12 + pr_),))
                            tasks += attn_tasks(
                                qk[hx], "qk%d" % hx, qk[2], "qk2", j,
                                lambda kb, pr_=pr_: Vt[:, kb, pr_ * 128:(pr_ + 1) * 128], True,
                                lambda kb, hd=hd, j=j: (ball[:, (j * NB + kb) * 8 + hd:(j * NB + kb) * 8 + hd + 1], ("ball",)), fix_fox, fin_f)
                    run_tasks(tasks)

                areset()
                roles["mm"] = [0, 1, 2, 3, 4, 5]
                ymh = aalloc(16 * HT, BF16).rearrange("p (a b) -> p a b", a=16)
                mT = aalloc(8 * HT, BF16).rearrange("p (a b) -> p a b", a=8)
                gt_ = [(aalloc(512), "gt0"), (aalloc(512), "gt1")]
                acc = aalloc(512)
                tmpm = aalloc(512)
                broff = [0, 8, 12]
                brkc = [8, 4, 4]
                for hf in range(2):
                    for c in range(16):
                        DMA(ymh[:, c, :], yscr[c][:, hf * HT:(hf + 1) * HT], r=("yscr%d" % c,), w=("ymh%d" % c,))
                    for f in range(8):
                        wg_ = [LW(wview(w_in[l], O_GT + b * 1024 + f * 128, 128), 8, 128) for b in range(3)]
                        wb_ = [LW(wview(w_br[b][l], f * 128, 128), brkc[b], 128) for b in range(3)]
                        for tt in range(TPH):
                            t = hf * TPH + tt
                            for b in range(3):
                                g_ = gt_[b % 2]
                                gbc = pcol[:, co["gb"] + (l * 3 + b) * 8 + f:co["gb"] + (l * 3 + b) * 8 + f + 1]
                                hgb = hcol[:, HB_G + (l * 3 + b) * 8 + f:HB_G + (l * 3 + b) * 8 + f + 1]
                                proj_fm(lambda t_, p, pk: ACT(g_[0], p[:], AF.Tanh, r=(pk, "hcol"), w=(g_[1],), bias=hgb, scale=0.5),
                                        wg_[b][0], wg_[b][1], 8, hT_rhs, hT_keys, NT, tiles=[t])
                                dst = acc if b == 0 else tmpm
                                dkey = "acc" if b == 0 else "tmpm"
                                proj_fm(lambda t_, p, pk: STT(dst, g_[0], 1.0, p[:], ALU.add, ALU.mult, r=(pk, g_[1]), w=(dkey,)),
                                        wb_[b][0], wb_[b][1], brkc[b],
                                        lambda k, t_, b=b, tt=tt: ymh[:, broff[b] + k, tt * 512:(tt + 1) * 512],
                                        lambda k, t_, b=b: ("ymh%d" % (broff[b] + k),), NT, tiles=[t])
                                if b == 1:
                                    TT("dve", acc, acc, tmpm, ALU.add, r=("acc", "tmpm"), w=("acc",))
                                elif b == 2:
                                    TT("dve", mT[:, f, tt * 512:(tt + 1) * 512], acc, tmpm, ALU.add, r=("acc", "tmpm"), w=("mT%d" % f,))
                    for f in range(8):
                        wo, wok = LW(wview(w_out[l], f * 128, 128), 8, 128)
                        for tt in range(TPH):
                            t = hf * TPH + tt
                            xk = "xT%d_%d" % (f, t)
                            proj_fm(lambda t_, p, pk: STT(xT[:, f, t * 512:(t + 1) * 512], p[:], 0.5, xT[:, f, t * 512:(t + 1) * 512], ALU.mult, ALU.add,
                                                          r=(pk, xk), w=(xk,)),
                                    wo, wok, 8, lambda k, t_, tt=tt: mT[:, k, tt * 512:(tt + 1) * 512], lambda k, t_: ("mT%d" % k,), NT, tiles=[t])

                norm_to_hT(co["n2"] + l * 8)
                areset()
                zT = aalloc(NFC * HT, BF16).rearrange("p (a b) -> p a b", a=NFC)
                urs = [[(aalloc(2 + 512), "ur%d_%d" % (s_, gv)) for gv in range(2)] for s_ in range(2)]
                ucs = [[(aalloc(512), "uc%d_%d" % (s_, gv)) for gv in range(2)] for s_ in range(2)]
                fts = [(aalloc(512), "ft%d" % s_) for s_ in range(2)]
                fcnt = 0
                for hf in range(2):
                    for c in range(NFC):
                        wts = [LW(wview(ffn_up[l], gv * DFF + c * 128, 128), 8, 128) for gv in range(2)]
                        for tl in range(TPH):
                            t = hf * TPH + tl
                            st_ = fcnt % 2
                            prev = (fcnt - 1) % 2
                            fcnt += 1
                            for gv in range(2):
                                wt, wk = wts[gv]
                                u_, uk = urs[st_][gv]
                                o_, ok = ucs[st_][gv]
                                fc = gv * NFC + c
                                w0 = co["fw"] + (l * 3) * 44 + fc
                                w2c = pcol[:, w0 + 88:w0 + 89]
                                bbc = pcol[:, co["fb"] + l * 44 + fc:co["fb"] + l * 44 + fc + 1]
                                if tl > 0:
                                    pu_, puk = urs[prev][gv]
                                    ACT(u_[:, 0:2], pu_[:, 512:514], AF.Copy, r=(puk,), w=(uk,))
                                elif hf == 0:
                                    MEMSET("dve", u_[:, 0:2], 0.0, w=(uk,))
                                else:
                                    p, pk = bank("mm")
                                    t0 = hf * HT
                                    for k in range(KC):
                                        MM(p[:, 0:2], wt[:, k, :], hT[:, k, t0 - 2:t0], k == 0, k == KC - 1,
                                           r=(wk, "hT%d_%d" % (k, (t0 - 2) // 512)), w=(pk,))
                                    ACT(u_[:, 0:2], p[:, 0:2], AF.Copy, r=(pk,), w=(uk,))

                                def ffn_evac(t_, p, pk, u_=u_, uk=uk, o_=o_, ok=ok, w2c=w2c, bbc=bbc):
                                    ACT(u_[:, 2:514], p[:], AF.Copy, r=(pk,), w=(uk,))
                                    ACT(o_, p[:], AF.Identity, r=(pk, "pcol"), w=(ok,), bias=bbc, scale=w2c)
                                proj_fm(ffn_evac, wt, wk, 8, hT_rhs, hT_keys, NT, tiles=[t])
                                for j in range(2):
                                    STT(o_, u_[:, j:j + 512], pcol[:, w0 + 44 * j:w0 + 44 * j + 1], o_, ALU.mult, ALU.add, r=(uk, "pcol", ok), w=(ok,))
                            ft, fk = fts[st_]
                            gelu2_from(ft, ucs[st_][0][0], (ucs[st_][0][1],), ft, fk, fk)
                            STT(zT[:, c, tl * 512:(tl + 1) * 512], ft, 0.5, ucs[st_][1][0], ALU.mult, ALU.mult, r=(fk, ucs[st_][1][1]), w=("zT%d" % c,))
                    for f in range(8):
                        wd = [LW(ffn_dn[l][k0 * 128:(k0 + n_) * 128, :].rearrange("(k p) n -> p k n", p=128)[:, :, f * 128:(f + 1) * 128], n_, 128)
                              for (k0, n_) in ((0, 8), (8, 8), (16, 6))]
                        for tt in range(TPH):
                            t = hf * TPH + tt
                            xk = "xT%d_%d" % (f, t)
                            p, pk = bank("mm")
                            for kk in range(NFC):
                                wt_, wk_ = wd[kk // 8]
                                MM(p[:], wt_[:, kk % 8, :], zT[:, kk, tt * 512:(tt + 1) * 512], kk == 0, kk == NFC - 1, r=(wk_, "zT%d" % kk), w=(pk,))
                            TT("dve", xT[:, f, t * 512:(t + 1) * 512], p[:], xT[:, f, t * 512:(t + 1) * 512], ALU.add, r=(pk, xk), w=(xk,))
                roles["mm"] = [0, 1]

            areset()
            sq2 = [(aalloc(512), "nsq0"), (aalloc(512), "nsq1")]
            ms = aalloc(S)
            of = aalloc(8 * 512).rearrange("p (a b) -> p a b", a=8)
            ot = [(aalloc(1024), "ot0"), (aalloc(1024), "ot1")]
            ocnt = 0
            rstd_all(sq2, ms)
            for t in range(NT):
                for f in range(KC):
                    STT(of[:, f, :], xT[:, f, t * 512:(t + 1) * 512], pcol[:, co["fin"] + f:co["fin"] + f + 1], ms[:, t * 512:(t + 1) * 512],
                        ALU.mult, ALU.mult, r=("xT%d_%d" % (f, t), "pcol", "ms"), w=("of%d" % f,))
                for blk in range(4):
                    o_, ok = ot[ocnt % 2]
                    ocnt += 1
                    for g in range(2):
                        p, pk = bank("mm")
                        for i in range(4):
                            f = g * 4 + i
                            TR(p[:, i * 128:(i + 1) * 128], of[:, f, blk * 128:(blk + 1) * 128], r=("of%d" % f,), w=(pk,))
                        if g == 0:
                            CP("dve", o_[:, 0:512], p[:], r=(pk,), w=(ok,))
                        else:
                            ACT(o_[:, 512:1024], p[:], AF.Copy, r=(pk,), w=(ok,))
                    r0 = t * 512 + blk * 128
                    DMA(out_d[s, r0:r0 + 128, :], o_, r=(ok,), w=("out",))
        S_.emit()
    return nc


_NC_CACHE = {}


def _make_in_maps(inputs, L, nseq, ncores, S):
    pc, pr, strips, cst = _host_params(inputs, L)
    shared = {
        "w_in": np.ascontiguousarray(inputs["w_in"][:L], np.float32),
        "rg_w_r": np.ascontiguousarray(inputs["rg_w_r"][:L], np.float32),
        "rg_w_i": np.ascontiguousarray(inputs["rg_w_i"][:L], np.float32),
        "w_br_rnn": np.ascontiguousarray(inputs["w_br_rnn"][:L], np.float32),
        "w_br_diff": np.ascontiguousarray(inputs["w_br_diff"][:L], np.float32),
        "w_br_fox": np.ascontiguousarray(inputs["w_br_fox"][:L], np.float32),
        "w_out": np.ascontiguousarray(inputs["w_out"][:L], np.float32),
        "ffn_up": np.ascontiguousarray(inputs["ffn_up"][:L], np.float32),
        "ffn_down": np.ascontiguousarray(inputs["ffn_down"][:L], np.float32),
        "pcols": pc, "prows": pr, "strips": strips, "consts": cst,
    }
    x = np.asarray(inputs["x"], np.float32)
    maps = []
    for c in range(ncores):
        m = dict(shared)
        m["x"] = np.ascontiguousarray(x[c * nseq:(c + 1) * nseq, :S])
        maps.append(m)
    return maps


def kernel(**inputs):
    inputs = {k: np.asarray(v) for k, v in inputs.items()}
    B, S, _ = inputs["x"].shape
    L = inputs["w_in"].shape[0]
    ncores = 8
    nseq = B // ncores
    key = (S, L, nseq)
    if key not in _NC_CACHE:
        _NC_CACHE[key] = build_nc(S=S, L=L, NSEQ=nseq)
    nc = _NC_CACHE[key]
    maps = _make_in_maps(inputs, L, nseq, ncores, S)
    res = run_bass_kernel_spmd(nc, maps, core_ids=list(range(ncores)))
    out = np.concatenate([np.asarray(r["out"]) for r in res.results], axis=0)
    return out.astype(np.float32)
```

```python
import contextlib
import math
import numpy as np
import concourse.bass as bass
import concourse.mybir as mybir
from concourse.bass_utils import run_bass_kernel_spmd

F32 = mybir.dt.float32
BF16 = mybir.dt.bfloat16
AF = mybir.ActivationFunctionType
ALU = mybir.AluOpType

D = 1024
KC = 8
NIN = 8200
DFF = 2816
NFC = 22
EPS = 1e-6
NEG = -30000.0
O_XR, O_GR, O_DQ, O_DK, O_DV, O_FQ, O_FK, O_FV, O_FL, O_GT = 0, 1024, 2048, 2560, 3072, 3584, 4096, 4608, 5120, 5128


class _Op:
    __slots__ = ("eng", "fn", "waits", "signal", "idx", "count", "dma", "dsem", "dcount", "clock")

    def __init__(self, eng, fn, dma):
        self.eng = eng; self.fn = fn; self.waits = []; self.signal = False; self.idx = -1
        self.count = 0; self.dma = dma; self.dsem = None; self.dcount = 0; self.clock = None


class Sched:
    ENG = ("pe", "act", "dve", "pool", "sp")
    NDSEM = 24

    def __init__(self, nc):
        self.nc = nc
        self.ops = {e: [] for e in self.ENG}
        self.lastw = {}
        self.readers = {}
        self.seen = {e: {f: -1 for f in self.ENG} for e in self.ENG}
        self.seen_d = {e: [0] * self.NDSEM for e in self.ENG}
        self.ndma = 0
        self.dma_ops = []

    def op(self, eng, fn, r=(), w=(), dma=False):
        o = _Op(eng, fn, dma)
        o.idx = len(self.ops[eng])
        r = tuple(r) + ("PHASE",)
        deps = []
        for k in r:
            x = self.lastw.get(k)
            if x is not None:
                deps.append(x)
        for k in w:
            x = self.lastw.get(k)
            if x is not None:
                deps.append(x)
            deps.extend(self.readers.get(k, ()))
        if dma:
            j = self.ndma
            self.ndma += 1
            o.dsem = j % self.NDSEM
            o.dcount = 16 * (j // self.NDSEM + 1)
            if j >= self.NDSEM:
                deps.append(self.dma_ops[j - self.NDSEM])
            self.dma_ops.append(o)
        seen = self.seen[eng]
        sd = self.seen_d[eng]
        best_e = {}
        best_d = {}
        for d in deps:
            if d is o:
                continue
            if d.dma:
                if d.dcount > sd[d.dsem] and (d.dsem not in best_d or d.dcount > best_d[d.dsem].dcount):
                    best_d[d.dsem] = d
            else:
                if d.eng == eng and eng == "pe":
                    continue
                if d.idx > seen[d.eng] and (d.eng not in best_e or d.idx > best_e[d.eng].idx):
                    best_e[d.eng] = d
        for d in best_e.values():
            if seen[d.eng] >= d.idx:
                continue
            d.signal = True
            o.waits.append(d)
            seen[d.eng] = d.idx
            if d.clock is not None:
                for f, v in d.clock.items():
                    if f != eng and v > seen[f]:
                        seen[f] = v
        for d in best_d.values():
            sd[d.dsem] = d.dcount
            o.waits.append(d)
        if not dma:
            o.clock = dict(seen)
            o.clock[eng] = o.idx
        for k in r:
            self.readers.setdefault(k, []).append(o)
        for k in w:
            self.lastw[k] = o
            self.readers[k] = []
        self.ops[eng].append(o)
        return o

    def emit(self):
        nc = self.nc
        with contextlib.ExitStack() as st:
            esem = {e: st.enter_context(nc.semaphore("s_" + e)) for e in self.ENG}
            dsem = [st.enter_context(nc.semaphore("d%d" % i)) for i in range(self.NDSEM)]
            for e in self.ENG:
                c = 0
                for o in self.ops[e]:
                    if o.signal and not o.dma:
                        c += 1
                        o.count = c
            last = {}
            for o in self.dma_ops:
                last[o.dsem] = o
            final = list(last.values())
            block = st.enter_context(nc.Block())

            def run(e, eng):
                for o in self.ops[e]:
                    for d in o.waits:
                        if d.dma:
                            eng.wait_ge(dsem[d.dsem], d.dcount)
                        else:
                            eng.wait_ge(esem[d.eng], d.count)
                    ins = o.fn(eng)
                    if o.dma:
                        ins.then_inc(dsem[o.dsem], 16)
                    elif o.signal:
                        ins.then_inc(esem[e], 1)
                if e == "sp":
                    for o in final:
                        eng.wait_ge(dsem[o.dsem], o.dcount)

            @block.tensor
            def _(eng):
                run("pe", eng)

            @block.scalar
            def _(eng):
                run("act", eng)

            @block.vector
            def _(eng):
                run("dve", eng)

            @block.gpsimd
            def _(eng):
                run("pool", eng)

            @block.sync
            def _(eng):
                run("sp", eng)


def _col_layout(L):
    off = {}
    n = 0
    for name, w in (("n1", L * 8), ("n2", L * 8), ("fin", 8), ("cw", L * 4 * 8), ("cb", L * 8), ("br", L * 8),
                    ("bi", L * 8), ("ra", L * 8), ("gb", L * 3 * 8), ("fw", L * 3 * 44), ("fb", L * 44), ("sg", L)):
        off[name] = n
        n += w
    return off, n


def _row_layout(L):
    off = {}
    n = 0
    for name, w in (("lq", L * 4 * 64), ("bf", L * 128), ("ch", 4)):
        off[name] = n
        n += w
    return off, n


def _t5_bucket_np(dist):
    n = np.maximum(dist, 0)
    nf = np.maximum(n, 1).astype(np.float32)
    large = 16 + (np.log(nf / np.float32(16)) / np.float32(math.log(128 / 16)) * np.float32(16)).astype(np.int32)
    large = np.minimum(large, 31)
    return np.where(n < 16, n, large)


def _host_params(inp, L):
    co, nc_ = _col_layout(L)
    ro, nr = _row_layout(L)
    pc = np.zeros((128, nc_), np.float32)

    def cols(v):
        v = np.asarray(v, np.float32)
        lead = int(np.prod(v.shape[:-1])) if v.ndim > 1 else 1
        k = v.shape[-1] // 128
        return v.reshape(lead, k, 128).transpose(2, 0, 1).reshape(128, lead * k)

    pc[:, co["n1"]:co["n1"] + L * 8] = cols(inp["norm1_g"][:L])
    pc[:, co["n2"]:co["n2"] + L * 8] = cols(inp["norm2_g"][:L])
    pc[:, co["fin"]:co["fin"] + 8] = cols(inp["final_g"])
    pc[:, co["cw"]:co["cw"] + L * 32] = cols(inp["rnn_conv_w"][:L])
    pc[:, co["cb"]:co["cb"] + L * 8] = cols(inp["rnn_conv_b"][:L])
    pc[:, co["br"]:co["br"] + L * 8] = cols(inp["rg_b_r"][:L])
    pc[:, co["bi"]:co["bi"] + L * 8] = cols(inp["rg_b_i"][:L])
    pc[:, co["ra"]:co["ra"] + L * 8] = cols(inp["rg_a"][:L])
    pc[:, co["gb"]:co["gb"] + L * 24] = cols(inp["gate_b"][:L])
    pc[:, co["fw"]:co["fw"] + L * 132] = cols(inp["ffn_conv_w"][:L])
    pc[:, co["fb"]:co["fb"] + L * 44] = cols(inp["ffn_conv_b"][:L])
    pc[:, co["sg"]:co["sg"] + L] = np.asarray(inp["diff_subln_g"][:L], np.float32).T
    pr = np.zeros((128, nr), np.float32)
    lq = np.stack([np.asarray(inp[k][:L], np.float32) for k in ("diff_lq1", "diff_lk1", "diff_lq2", "diff_lk2")], 1)
    pr[:, ro["lq"]:ro["lq"] + L * 256] = np.broadcast_to(lq.reshape(1, L * 256), (128, L * 256))
    bf = np.tile(np.asarray(inp["fox_b_f"][:L], np.float32).reshape(L, 1, 8), (1, 16, 1)).reshape(1, L * 128)
    pr[:, ro["bf"]:ro["bf"] + L * 128] = np.broadcast_to(bf, (128, L * 128))
    rel = np.asarray(inp["rel_bias"], np.float32)
    pr[:, ro["ch"]:ro["ch"] + 4] = np.broadcast_to(rel[31:32, :], (128, 4))
    kl = np.arange(128)[:, None]
    sx = np.arange(256)[None, :]
    dist = sx - kl
    bidx = _t5_bucket_np(dist)
    strips = np.zeros((128, 4 * 256), np.float32)
    for h in range(4):
        g = rel[bidx, h]
        strips[:, h * 256:(h + 1) * 256] = np.where(dist >= 0, g, np.float32(NEG))
    cst = np.zeros((128, 4 * 128), np.float32)
    cst[:, 0:128] = np.eye(128, dtype=np.float32)
    cst[:, 128:256] = (np.arange(128)[:, None] <= np.arange(128)[None, :]).astype(np.float32)
    cst[:, 256:384] = np.where(np.arange(128)[:, None] <= np.arange(128)[None, :], 0.0, NEG)
    cst[:, 384:512] = (np.arange(128)[:, None] == (np.arange(128)[None, :] + 64) % 128).astype(np.float32)
    return pc, pr, strips, cst


def build_nc(S=2048, L=2, NSEQ=2):
    assert S % 1024 == 0
    NT = S // 512
    NB = S // 128
    HT = S // 2
    TPH = NT // 2
    co, ncol = _col_layout(L)
    ro, nrow = _row_layout(L)

    nc = bass.Bass("TRN2", target_bir_lowering=False)

    def din(name, shape):
        return nc.dram_tensor(name, shape, F32, kind="ExternalInput").ap()

    x_d = din("x", [NSEQ, S, D])
    w_in = din("w_in", [L, D, NIN])
    rg_wr = din("rg_w_r", [L, 8, 128, 128])
    rg_wi = din("rg_w_i", [L, 8, 128, 128])
    w_br = [din("w_br_rnn", [L, D, D]), din("w_br_diff", [L, 512, D]), din("w_br_fox", [L, 512, D])]
    w_out = din("w_out", [L, D, D])
    ffn_up = din("ffn_up", [L, D, 2 * DFF])
    ffn_dn = din("ffn_down", [L, DFF, D])
    pcol_d = din("pcols", [128, ncol])
    prow_d = din("prows", [128, nrow])
    strip_d = din("strips", [128, 1024])
    cst_d = din("consts", [128, 512])
    out_d = nc.dram_tensor("out", [NSEQ, S, D], F32, kind="ExternalOutput").ap()
    yscr = nc.dram_tensor("yscr", [16, 128, S], BF16, kind="Internal").ap()

    with contextlib.ExitStack() as st:
        def sb(name, shape, dt=F32):
            return st.enter_context(nc.sbuf_tensor(name, shape, dt))

        S_ = Sched(nc)
        xT = sb("xT", [128, KC, S])
        hT = sb("hT", [128, KC, S], BF16)
        ARN = 17424
        arena = sb("arena", [128, ARN])
        NSTG, NWB = 2, 8
        stg = [sb("stg%d" % i, [128, 8, 128]) for i in range(NSTG)]
        wbs = [sb("wb%d" % i, [128, 8, 128], BF16) for i in range(NWB)]
        pcol = sb("pcol_sb", [128, ncol])
        prow = sb("prow_sb", [128, nrow])
        cst = sb("cst_sb", [128, 512])
        strips = sb("strips_sb", [128, 1024])
        dcol = sb("dcol", [128, 16 * L + 8])
        hcol = sb("hcol", [128, L * 48 + 8])
        ones32 = sb("ones32", [128, 128])
        onesb = sb("onesb", [128, 128], BF16)
        phz = sb("phz", [128, 1])
        banks = [st.enter_context(nc.psum_tensor("bank%d" % i, [128, 512], F32)) for i in range(8)]
        ident = cst[:, 0:128]
        tri = cst[:, 128:256]
        cmask = cst[:, 256:384]
        shm = cst[:, 384:512]

        def MM(out, lhsT, rhs, start, stop, r, w):
            S_.op("pe", lambda e: e.matmul(out, lhsT=lhsT, rhs=rhs, start=start, stop=stop), r=r, w=w)

        def TR(out, in_, r, w):
            S_.op("pe", lambda e: e.transpose(out=out, in_=in_, identity=ident), r=tuple(r) + ("cst",), w=w)

        def ACT(out, in_, func, r, w, bias=None, scale=None):
            kw = {}
            if bias is not None:
                kw["bias"] = bias
            if scale is not None:
                kw["scale"] = scale
            S_.op("act", lambda e: e.activation(out=out, in_=in_, func=func, **kw), r=r, w=w)

        def TT(eng, out, in0, in1, op, r, w):
            S_.op(eng, lambda e: e.tensor_tensor(out=out, in0=in0, in1=in1, op=op), r=r, w=w)

        def TS(eng, out, in0, s1, s2, op0, op1, r, w):
            if s2 is None:
                S_.op(eng, lambda e: e.tensor_scalar(out=out, in0=in0, scalar1=s1, scalar2=None, op0=op0), r=r, w=w)
            else:
                S_.op(eng, lambda e: e.tensor_scalar(out=out, in0=in0, scalar1=s1, scalar2=s2, op0=op0, op1=op1), r=r, w=w)

        def STT(out, in0, scalar, in1, op0, op1, r, w):
            S_.op("dve", lambda e: e.scalar_tensor_tensor(out=out, in0=in0, scalar=scalar, in1=in1, op0=op0, op1=op1), r=r, w=w)

        def CP(eng, out, in_, r, w):
            S_.op(eng, lambda e: e.tensor_copy(out=out, in_=in_), r=r, w=w)

        def RECIP(out, in_, r, w):
            S_.op("dve", lambda e: e.reciprocal(out=out, in_=in_), r=r, w=w)

        def MEMSET(eng, ap, val, w):
            S_.op(eng, lambda e: e.memset(ap, val), w=w)

        def DMA(out, in_, r, w):
            S_.op("sp", lambda e: e.dma_start(out=out, in_=in_), r=r, w=w, dma=True)

        def barrier():
            S_.op("dve", lambda e: e.memset(phz[:], 0.0), w=("PHASE", "phz"))

        apos = [0]

        def areset():
            barrier()
            apos[0] = 0

        def aalloc(nelem, dt=F32):
            n32 = nelem if dt == F32 else (nelem + 1) // 2
            n32 = (n32 + 7) // 8 * 8
            a0 = apos[0]
            apos[0] += n32
            assert apos[0] <= ARN, ("arena overflow", apos[0])
            v = arena[:, a0:a0 + n32]
            if dt != F32:
                v = v.bitcast(dt)[:, 0:nelem]
            else:
                v = v[:, 0:nelem]
            return v

        roles = {"mm": [0, 1], "sc": [2, 3], "acc": [4, 5, 6, 7], "pj": [0, 4, 5, 6, 7]}
        rpos = {"mm": 0, "sc": 0, "acc": 0, "pj": 0}

        def bank(role):
            lst = roles[role]
            i = lst[rpos[role] % len(lst)]
            rpos[role] += 1
            return banks[i], "bank%d" % i

        wcnt = [0, 0]

        def LW(src, kc, ncols, dst=None, dkey=None):
            i = wcnt[0] % NSTG
            wcnt[0] += 1
            DMA(stg[i][:, 0:kc, 0:ncols], src, r=(), w=("stg%d" % i,))
            if dst is None:
                j = wcnt[1] % NWB
                wcnt[1] += 1
                dst = wbs[j][:, 0:kc, 0:ncols]
                dkey = "wb%d" % j
                ret = wbs[j]
            else:
                ret = None
            if wcnt[0] % 3 == 0:
                ACT(dst, stg[i][:, 0:kc, 0:ncols], AF.Copy, r=("stg%d" % i,), w=(dkey,))
            else:
                CP("pool", dst, stg[i][:, 0:kc, 0:ncols], r=("stg%d" % i,), w=(dkey,))
            return ret, dkey

        def wview(ap2d, c0, ncols):
            return ap2d.rearrange("(k p) n -> p k n", p=128)[:, :, c0:c0 + ncols]

        DMA(pcol[:], pcol_d, r=(), w=("pcol",))
        DMA(prow[:], prow_d, r=(), w=("prow",))
        DMA(cst[:], cst_d, r=(), w=("cst",))
        DMA(strips[:], strip_d, r=(), w=("strips",))
        MEMSET("dve", ones32[:], 1.0, w=("ones32",))
        MEMSET("dve", onesb[:], 1.0, w=("onesb",))
        DC_EPS, DC_ONE = 16 * L, 16 * L + 1
        MEMSET("dve", dcol[:, DC_EPS:DC_EPS + 1], EPS, w=("dcol",))
        MEMSET("dve", dcol[:, DC_ONE:DC_ONE + 1], 1.0, w=("dcol",))
        epsc = dcol[:, DC_EPS:DC_EPS + 1]
        onec = dcol[:, DC_ONE:DC_ONE + 1]
        for h in range(4):
            TS("dve", strips[:, h * 256:(h + 1) * 256], strips[:, h * 256:(h + 1) * 256],
               prow[:, ro["ch"] + h:ro["ch"] + h + 1], None, ALU.subtract, None, r=("strips", "prow"), w=("strips",))
        HB_R, HB_I, HB_G, HB_A, HB_C = 0, L * 8, L * 16, L * 40, L * 48
        TS("dve", hcol[:, HB_R:HB_R + L * 16], pcol[:, co["br"]:co["br"] + L * 16], 0.5, None, ALU.mult, None, r=("pcol",), w=("hcol",))
        TS("dve", hcol[:, HB_G:HB_G + L * 24], pcol[:, co["gb"]:co["gb"] + L * 24], 0.5, None, ALU.mult, None, r=("pcol",), w=("hcol",))
        MEMSET("dve", hcol[:, HB_C:HB_C + 1], 1.0 / 16, w=("hcol",))
        sixteenth = hcol[:, HB_C:HB_C + 1]
        lam_init = [0.8 - 0.6 * math.exp(-0.3 * l) for l in range(L)]
        for l in range(L):
            b0 = l * 16
            ACT(dcol[:, b0:b0 + 8], pcol[:, co["ra"] + l * 8:co["ra"] + l * 8 + 8], AF.Exp, r=("pcol",), w=("dcol",), scale=-1.0)
            ACT(dcol[:, b0:b0 + 8], dcol[:, b0:b0 + 8], AF.Ln, r=("dcol",), w=("dcol",), bias=onec)
            TS("dve", dcol[:, b0:b0 + 8], dcol[:, b0:b0 + 8], -8.0, None, ALU.mult, None, r=("dcol",), w=("dcol",))
            TS("dve", hcol[:, HB_A + l * 8:HB_A + l * 8 + 8], dcol[:, b0:b0 + 8], 0.5, None, ALU.mult, None, r=("dcol",), w=("hcol",))
            q0 = ro["lq"] + l * 256
            for pair in range(2):
                tmp = arena[:, 0:64]
                TT("dve", tmp, prow[:, q0 + pair * 128:q0 + pair * 128 + 64], prow[:, q0 + pair * 128 + 64:q0 + pair * 128 + 128],
                   ALU.mult, r=("prow",), w=("lamtmp",))
                S_.op("dve", lambda e, o_=dcol[:, b0 + 10 + pair:b0 + 11 + pair], i_=tmp: e.tensor_reduce(
                    out=o_, in_=i_, axis=mybir.AxisListType.X, op=ALU.add), r=("lamtmp",), w=("dcol",))
            ACT(dcol[:, b0 + 10:b0 + 12], dcol[:, b0 + 10:b0 + 12], AF.Exp, r=("dcol",), w=("dcol",))
            TT("dve", dcol[:, b0 + 8:b0 + 9], dcol[:, b0 + 11:b0 + 12], dcol[:, b0 + 10:b0 + 11], ALU.subtract, r=("dcol",), w=("dcol",))
            TS("dve", dcol[:, b0 + 8:b0 + 9], dcol[:, b0 + 8:b0 + 9], -lam_init[l], None, ALU.add, None, r=("dcol",), w=("dcol",))
            TS("dve", dcol[:, b0 + 9:b0 + 10], pcol[:, co["sg"] + l:co["sg"] + l + 1], 1.0 - lam_init[l], None, ALU.mult, None,
               r=("pcol",), w=("dcol",))

        def rstd_all(sq2, ms):
            for t in range(NT):
                pn, pk = bank("mm")
                for f in range(KC):
                    sq = sq2[f % 2]
                    ACT(sq[0], xT[:, f, t * 512:(t + 1) * 512], AF.Square, r=("xT%d_%d" % (f, t),), w=(sq[1],))
                    MM(pn[:], ones32[:], sq[0], f == 0, f == KC - 1, r=(sq[1], "ones32"), w=(pk,))
                ACT(ms[:, t * 512:(t + 1) * 512], pn[:], AF.Copy, r=(pk,), w=("ms",), scale=1.0 / D)
            ACT(ms, ms, AF.Sqrt, r=("ms", "dcol"), w=("ms",), bias=epsc)
            RECIP(ms, ms, r=("ms",), w=("ms",))

        def norm_to_hT(gbase):
            areset()
            sq2 = [(aalloc(512), "nsq0"), (aalloc(512), "nsq1")]
            ms = aalloc(S)
            rstd_all(sq2, ms)
            for t in range(NT):
                for f in range(KC):
                    STT(hT[:, f, t * 512:(t + 1) * 512], xT[:, f, t * 512:(t + 1) * 512], pcol[:, gbase + f:gbase + f + 1],
                        ms[:, t * 512:(t + 1) * 512], ALU.mult, ALU.mult, r=("xT%d_%d" % (f, t), "pcol", "ms"), w=("hT%d_%d" % (f, t),))

        def proj_fm(dst_fn, wt, wkey, kc, rhs_fn, rkeys_fn, ntiles, M=128, tiles=None, role="mm"):
            for t in (tiles if tiles is not None else range(ntiles)):
                p, pk = bank(role)
                for k in range(kc):
                    MM(p[0:M, :], wt[:, k, 0:M], rhs_fn(k, t), k == 0, k == kc - 1, r=(wkey,) + tuple(rkeys_fn(k, t)), w=(pk,))
                dst_fn(t, p, pk)

        def hT_rhs(k, t):
            return hT[:, k, t * 512:(t + 1) * 512]

        def hT_keys(k, t):
            return ("hT%d_%d" % (k, t),)

        def gelu2_from(dst, src, srckeys, tmp, tmpkey, dkey):
            ACT(tmp, src, AF.Square, r=srckeys, w=(tmpkey,), scale=0.21145921595661512)
            STT(tmp, tmp, 1.0, src, ALU.add, ALU.mult, r=(tmpkey,) + tuple(srckeys), w=(tmpkey,))
            ACT(tmp, tmp, AF.Tanh, r=(tmpkey,), w=(tmpkey,), scale=0.7978845608028654)
            STT(dst, tmp, 1.0, src, ALU.add, ALU.mult, r=(tmpkey,) + tuple(srckeys), w=(dkey,))

        for s in range(NSEQ):
            areset()
            xin = [(aalloc(1024), "xin0"), (aalloc(1024), "xin1")]
            for b in range(NB):
                xi = xin[b % 2]
                DMA(xi[0], x_d[s, b * 128:(b + 1) * 128, :], r=(), w=(xi[1],))
                for g in range(2):
                    p, pk = bank("mm")
                    for i in range(4):
                        f = g * 4 + i
                        TR(p[:, i * 128:(i + 1) * 128], xi[0][:, f * 128:(f + 1) * 128], r=(xi[1],), w=(pk,))
                    t = b // 4
                    wk = tuple("xT%d_%d" % (g * 4 + i, t) for i in range(4))
                    dstx = xT[:, g * 4:g * 4 + 4, b * 128:(b + 1) * 128]
                    srcx = p[:].rearrange("p (a b) -> p a b", a=4)
                    if g == 0:
                        CP("dve", dstx, srcx, r=(pk,), w=wk)
                    else:
                        ACT(dstx, srcx, AF.Copy, r=(pk,), w=wk)

            for l in range(L):
                dc0 = l * 16
                norm_to_hT(co["n1"] + l * 8)

                areset()
                roles["mm"] = [0, 1, 2, 3]
                xr = aalloc(3 + S)
                gg = [aalloc(S), aalloc(S)]
                xc = aalloc(S)
                xcb = aalloc(S, BF16)
                rr = aalloc(S)
                ii = aalloc(S)
                a2 = aalloc(S)
                hh = a2
                gtmp = [(aalloc(512), "gtmp0"), (aalloc(512), "gtmp1")]
                ybuf = aalloc(S, BF16)
                carry = aalloc(8)
                MEMSET("dve", xr[:, 0:3], 0.0, w=("xrh",))

                def sl(t):
                    return slice(t * 512, (t + 1) * 512)

                def rnn_W(n):
                    wx_ = LW(wview(w_in[l], O_XR + n * 128, 128), 8, 128)
                    wg_ = LW(wview(w_in[l], O_GR + n * 128, 128), 8, 128)
                    wr_ = LW(rg_wr[l, n].rearrange("(k p) n -> p k n", p=128), 1, 128)
                    wi_ = LW(rg_wi[l, n].rearrange("(k p) n -> p k n", p=128), 1, 128)
                    return wx_, wr_, wi_, wg_

                def rnn_Pg(n, wg_, t):
                    g_ = gg[n % 2]

                    def ev(t_, p, pk):
                        gt = gtmp[t_ % 2]
                        gelu2_from(g_[:, sl(t_)], p[:], (pk,), gt[0], gt[1], "gg%d_%d" % (n % 2, t_))
                    proj_fm(ev, wg_[0], wg_[1], 8, hT_rhs, hT_keys, NT, tiles=[t])

                def rnn_Px(n, wx_, t):
                    wx, wxk = wx_
                    cw = co["cw"] + (l * 4) * 8 + n
                    w3c = pcol[:, cw + 24:cw + 25]
                    cbc = pcol[:, co["cb"] + l * 8 + n:co["cb"] + l * 8 + n + 1]

                    def ev(t_, p, pk):
                        ACT(xr[:, 3 + t_ * 512:3 + (t_ + 1) * 512], p[:], AF.Copy, r=(pk,), w=("xr%d" % t_,))
                        ACT(xc[:, sl(t_)], p[:], AF.Identity, r=(pk, "pcol"), w=("xc%d" % t_,), bias=cbc, scale=w3c)
                    proj_fm(ev, wx, wxk, 8, hT_rhs, hT_keys, NT, tiles=[t])

                def rnn_R1(n, wr_, wi_, t, after_conv):
                    wr, wrk = wr_
                    wi, wik = wi_
                    cw = co["cw"] + (l * 4) * 8 + n
                    xk = ("xr%d" % t, "xr%d" % (t - 1) if t > 0 else "xrh", "pcol", "xc%d" % t)
                    for j in range(3):
                        STT(xc[:, sl(t)], xr[:, j + t * 512:j + (t + 1) * 512], pcol[:, cw + 8 * j:cw + 8 * j + 1], xc[:, sl(t)],
                            ALU.mult, ALU.add, r=xk, w=("xc%d" % t,))
                    ACT(xcb[:, sl(t)], xc[:, sl(t)], AF.Copy, r=("xc%d" % t,), w=("xcb%d" % t,))
                    after_conv()
                    hbr = hcol[:, HB_R + l * 8 + n:HB_R + l * 8 + n + 1]
                    hbi = hcol[:, HB_I + l * 8 + n:HB_I + l * 8 + n + 1]
                    hsa = hcol[:, HB_A + l * 8 + n:HB_A + l * 8 + n + 1]
                    proj_fm(lambda t_, p, pk: ACT(rr[:, sl(t)], p[:], AF.Tanh, r=(pk, "hcol"), w=("rr%d" % t,), bias=hbr, scale=0.5),
                            wr, wrk, 1, lambda k, t_: xcb[:, sl(t)], lambda k, t_: ("xcb%d" % t,), NT, tiles=[t])
                    proj_fm(lambda t_, p, pk: ACT(ii[:, sl(t)], p[:], AF.Tanh, r=(pk, "hcol"), w=("ii%d" % t,), bias=hbi, scale=0.5),
                            wi, wik, 1, lambda k, t_: xcb[:, sl(t)], lambda k, t_: ("xcb%d" % t,), NT, tiles=[t])
                    ACT(rr[:, sl(t)], rr[:, sl(t)], AF.Exp, r=("rr%d" % t, "hcol"), w=("rr%d" % t,), bias=hsa, scale=hsa)
                    TT("pool", a2[:, sl(t)], rr[:, sl(t)], rr[:, sl(t)], ALU.mult, r=("rr%d" % t,), w=("a2_%d" % t,))
                    STT(ii[:, sl(t)], ii[:, sl(t)], 1.0, xc[:, sl(t)], ALU.add, ALU.mult, r=("ii%d" % t, "xc%d" % t), w=("ii%d" % t,))

                def rnn_R2(n, t):
                    g_ = gg[n % 2]
                    TT("pool", ii[:, sl(t)], ii[:, sl(t)], a2[:, sl(t)], ALU.mult, r=("ii%d" % t, "a2_%d" % t), w=("ii%d" % t,))
                    init = 0.0 if t == 0 else carry[:, t - 1:t]
                    rk = ("rr%d" % t, "ii%d" % t, "a2_%d" % t) + (("carry%d" % (t - 1),) if t > 0 else ())
                    S_.op("dve", lambda e, o_=hh[:, sl(t)], d0=rr[:, sl(t)], d1=ii[:, sl(t)], init=init: e.tensor_tensor_scan(
                        out=o_, data0=d0, data1=d1, initial=init, op0=ALU.mult, op1=ALU.add), r=rk, w=("a2_%d" % t,))
                    if t + 1 < NT:
                        CP("pool", carry[:, t:t + 1], hh[:, (t + 1) * 512 - 1:(t + 1) * 512], r=("a2_%d" % t,), w=("carry%d" % t,))
                    TT("pool", ybuf[:, sl(t)], g_[:, sl(t)], hh[:, sl(t)], ALU.mult, r=("gg%d_%d" % (n % 2, t), "a2_%d" % t), w=("ybuf%d" % t,))

                def rnn_sqrt(n):
                    ak = tuple("a2_%d" % t for t in range(NT))
                    ACT(a2, a2, AF.Sqrt, r=ak + ("hcol",), w=ak, bias=sixteenth, scale=-1.0 / 16)

                def rnn_out(n):
                    DMA(yscr[n], ybuf, r=tuple("ybuf%d" % t for t in range(NT)), w=("yscr%d" % n,))

                W = {0: rnn_W(0)}
                for t in range(NT):
                    rnn_Pg(0, W[0][3], t)
                    rnn_Px(0, W[0][0], t)
                for n in range(8):
                    if n + 1 < 8:
                        W[n + 1] = rnn_W(n + 1)
                    for t in range(NT):
                        if n > 0:
                            rnn_R2(n - 1, t)
                        if n + 1 < 8:
                            rnn_Pg(n + 1, W[n + 1][3], t)
                        if n + 1 < 8 and t > 0:
                            rnn_R1(n, W[n][1], W[n][2], t, lambda t=t: rnn_Px(n + 1, W[n + 1][0], t - 1))
                        else:
                            rnn_R1(n, W[n][1], W[n][2], t, lambda: None)
                    if n > 0:
                        rnn_out(n - 1)
                    if n + 1 < 8:
                        rnn_Px(n + 1, W[n + 1][0], NT - 1)
                    rnn_sqrt(n)
                for t in range(NT):
                    rnn_R2(7, t)
                rnn_out(7)

                areset()
                roles["mm"] = [0]
                roles["sc"] = [1, 2, 3]
                Vt = aalloc(NB * 576, BF16).rearrange("p (a b) -> p a b", a=NB)
                MEMSET("pool", Vt[:, :, 512:576], 1.0, w=("Vt",))
                Wv = aalloc(8 * 512, BF16).rearrange("p (a b) -> p a b", a=8)
                qk = [aalloc(S, BF16) for _ in range(3)]
                Et = [(aalloc(512, BF16), "E%d" % i) for i in range(4)]
                rz = aalloc(512)
                oo = [aalloc(512), aalloc(512)]
                sq_ = aalloc(512)
                rs_ = aalloc(512)
                yt = [(aalloc(512, BF16), "yt%d" % i) for i in range(2)]
                ecnt = [0]

                def build_V(colbase):
                    for i in range(4):
                        LW(wview(w_in[l], colbase + i * 128, 128), 8, 128, dst=Wv[:, :, i * 128:(i + 1) * 128], dkey="Wv")
                    for b in range(NB):
                        p, pk = bank("pj")
                        for k in range(KC):
                            MM(p[:], hT[:, k, b * 128:(b + 1) * 128], Wv[:, k, :], k == 0, k == KC - 1,
                               r=("hT%d_%d" % (k, b // 4), "Wv"), w=(pk,))
                        if b % 2 == 0:
                            CP("dve", Vt[:, b, 0:512], p[:], r=(pk,), w=("Vt",))
                        else:
                            ACT(Vt[:, b, 0:512], p[:], AF.Copy, r=(pk,), w=("Vt",))

                MEMSET("dve", qk[0][64:128, :], 0.0, w=("qk0",))
                MEMSET("dve", qk[1][0:64, :], 0.0, w=("qk1",))

                def load_qk(colq, colk):
                    return LW(wview(w_in[l], colq, 128), 8, 128), LW(wview(w_in[l], colk, 128), 8, 128)

                def build_qk3(wq_, wk_):
                    def evq(t, p, pk):
                        ACT(qk[0][0:64, t * 512:(t + 1) * 512], p[0:64, :], AF.Copy, r=(pk,), w=("qk0",), scale=0.125)
                        ACT(qk[1][64:128, t * 512:(t + 1) * 512], p[64:128, :], AF.Copy, r=(pk,), w=("qk1",), scale=0.125)
                    proj_fm(evq, wq_[0], wq_[1], 8, hT_rhs, hT_keys, NT, role="pj")

                    def evk(t, p, pk):
                        CP("dve", qk[2][:, t * 512:(t + 1) * 512], p[:], r=(pk,), w=("qk2",))
                    proj_fm(evk, wk_[0], wk_[1], 8, hT_rhs, hT_keys, NT, role="pj")

                NE = len(Et)

                def attn_tasks(qT, qkey, kT, kkey, j, v_fn, use_pz, bias_fn, fix_fn, fin_fn):
                    nk = 4 * (j + 1)
                    st_ = {}
                    tasks = []
                    for kb in range(nk):
                        def A(kb=kb):
                            m = kb - 4 * j
                            c0 = max(0, 128 * m)
                            ps, psk = bank("sc")
                            MM(ps[:, c0:512], kT[:, kb * 128:(kb + 1) * 128], qT[:, j * 512 + c0:(j + 1) * 512], True, True,
                               r=(qkey, kkey), w=(psk,))
                            fix_fn(ps, psk, m)
                            E = Et[ecnt[0] % NE]
                            ecnt[0] += 1
                            bcol, bkeys = bias_fn(kb)
                            ACT(E[0][:, c0:512], ps[:, c0:512], AF.Exp, r=(psk,) + tuple(bkeys), w=(E[1],), bias=bcol)
                            st_[kb] = (E, c0)

                        def B(kb=kb):
                            if kb == 0:
                                st_["po"] = bank("acc")
                                st_["pz"] = bank("acc") if use_pz else (None, None)
                            po, pok = st_["po"]
                            pz, pzk = st_["pz"]
                            E, c0 = st_.pop(kb)
                            MM(po[:, c0:512], v_fn(kb), E[0][:, c0:512], kb == 0, kb == nk - 1, r=("Vt", E[1]), w=(pok,))
                            if use_pz:
                                MM(pz[:, c0:512], onesb[:], E[0][:, c0:512], kb == 0, kb == nk - 1, r=("onesb", E[1]), w=(pzk,))
                            if kb == nk - 1:
                                fin_fn(po, pok, pz, pzk)
                        tasks.append((A, B))
                    return tasks

                def run_tasks(tasks, LA=3):
                    n = len(tasks)
                    for i in range(n + LA):
                        if i < n:
                            tasks[i][0]()
                        if i - LA >= 0:
                            tasks[i - LA][1]()

                build_V(O_DV)
                wpre = load_qk(O_DQ, O_DK)
                for h in range(4):
                    build_qk3(*wpre)
                    wpre = load_qk(O_DQ + (h + 1) * 128, O_DK + (h + 1) * 128) if h + 1 < 4 else load_qk(O_FQ, O_FK)
                    G = strips[:, h * 256:(h + 1) * 256]
                    chc = prow[:, ro["ch"] + h:ro["ch"] + h + 1]

                    def fix_diff(ps, psk, m, G=G):
                        if m == -1:
                            TT("dve", ps[:, 0:128], ps[:, 0:128], G[:, 128:256], ALU.add, r=(psk, "strips"), w=(psk,))
                        elif m >= 0:
                            a_ = 128 * m
                            b_ = min(a_ + 256, 512)
                            TT("dve", ps[:, a_:b_], ps[:, a_:b_], G[:, 0:b_ - a_], ALU.add, r=(psk, "strips"), w=(psk,))

                    tasks = []
                    for j in range(NT):
                        for c in range(2):
                            def fin_d(po, pok, pz, pzk, j=j, c=c, h=h):
                                RECIP(rz, pz[:], r=(pzk,), w=("rz",))
                                TT("dve", oo[c], po[:], rz, ALU.mult, r=(pok, "rz"), w=("oo%d" % c,))
                                if c == 1:
                                    STT(oo[0], oo[1], dcol[:, dc0 + 8:dc0 + 9], oo[0], ALU.mult, ALU.add, r=("oo0", "oo1", "dcol"), w=("oo0",))
                                    ACT(sq_, oo[0], AF.Square, r=("oo0",), w=("sq_",))
                                    pn, pnk = bank("mm")
                                    MM(pn[:], ones32[:], sq_, True, True, r=("sq_", "ones32"), w=(pnk,))
                                    ACT(rs_, pn[:], AF.Sqrt, r=(pnk, "dcol"), w=("rs_",), bias=epsc, scale=1.0 / 128)
                                    RECIP(rs_, rs_, r=("rs_",), w=("rs_",))
                                    y_ = yt[j % 2]
                                    STT(y_[0], oo[0], dcol[:, dc0 + 9:dc0 + 10], rs_, ALU.mult, ALU.mult, r=("oo0", "dcol", "rs_"), w=(y_[1],))
                                    DMA(yscr[8 + h][:, j * 512:(j + 1) * 512], y_[0], r=(y_[1],), w=("yscr%d" % (8 + h),))
                            tasks += attn_tasks(qk[c], "qk%d" % c, qk[2], "qk2", j,
                                                lambda kb, h=h: Vt[:, kb, h * 128:(h + 1) * 128], True,
                                                lambda kb, chc=chc: (chc, ("prow",)), fix_diff, fin_d)
                    run_tasks(tasks)

                build_V(O_FV)
                wf, wfk = LW(wview(w_in[l], O_FL, 8), 8, 8)
                lf = aalloc(128)
                Dk = aalloc(128)
                pref = aalloc(128 + 8)
                ball = aalloc(NT * NB * 8)
                pf, pfk = bank("mm")
                for b in range(NB):
                    for k in range(KC):
                        MM(pf[:, b * 8:(b + 1) * 8], hT[:, k, b * 128:(b + 1) * 128], wf[:, k, 0:8], k == 0, k == KC - 1,
                           r=("hT%d_%d" % (k, b // 4), wfk), w=(pfk,))
                nb8 = NB * 8
                bf0 = ro["bf"] + l * 128
                TT("dve", lf[:, 0:nb8], pf[:, 0:nb8], prow[:, bf0:bf0 + nb8], ALU.add, r=(pfk, "prow"), w=("lf",))
                ACT(lf[:, 0:nb8], lf[:, 0:nb8], AF.Exp, r=("lf",), w=("lf",), scale=-1.0)
                ACT(lf[:, 0:nb8], lf[:, 0:nb8], AF.Ln, r=("lf", "dcol"), w=("lf",), bias=onec)
                pc_, pck = bank("sc")
                MM(pc_[:, 0:nb8], tri, lf[:, 0:nb8], True, True, r=("cst", "lf"), w=(pck,))
                ptot, ptk = bank("sc")
                MM(ptot[:, 0:nb8], ones32[:], lf[:, 0:nb8], True, True, r=("ones32", "lf"), w=(ptk,))
                MEMSET("dve", pref[:, 0:8], 0.0, w=("pref",))
                for b in range(1, NB + 1):
                    TT("dve", pref[:, b * 8:(b + 1) * 8], pref[:, (b - 1) * 8:b * 8], ptot[:, (b - 1) * 8:b * 8], ALU.add,
                       r=("pref", ptk), w=("pref",))
                TT("dve", Dk[:, 0:nb8], pc_[:, 0:nb8], pref[:, 0:nb8], ALU.add, r=(pck, "pref"), w=("Dk",))
                for j in range(NT):
                    rb = 4 * j + 2
                    for kb in range(4 * (j + 1)):
                        TT("dve", ball[:, (j * NB + kb) * 8:(j * NB + kb) * 8 + 8], Dk[:, kb * 8:kb * 8 + 8], pref[:, rb * 8:rb * 8 + 8],
                           ALU.subtract, r=("Dk", "pref"), w=("ball",))

                def fix_fox(ps, psk, m):
                    if m >= 0:
                        a_ = 128 * m
                        TT("dve", ps[:, a_:a_ + 128], ps[:, a_:a_ + 128], cmask, ALU.add, r=(psk, "cst"), w=(psk,))

                for pr_ in range(4):
                    build_qk3(*wpre)
                    if pr_ + 1 < 4:
                        wpre = load_qk(O_FQ + (pr_ + 1) * 128, O_FK + (pr_ + 1) * 128)
                    tasks = []
                    for j in range(NT):
                        for hx in range(2):
                            hd = 2 * pr_ + hx

                            def fin_f(po, pok, pz, pzk, j=j, hx=hx, pr_=pr_):
                                y_ = yt[j % 2]
                                lo, hi = hx * 64, hx * 64 + 64
                                RECIP(rz[lo:hi, :], pz[lo:hi, :], r=(pzk,), w=("rz",))
                                TT("dve", y_[0][lo:hi, :], po[lo:hi, :], rz[lo:hi, :], ALU.mult, r=(pok, "rz"), w=(y_[1],))
                                if hx == 1:
                                    DMA(yscr[12 + pr_][:, j * 512:(j + 1) * 512], y_[0], r=(y_[1],), w=("yscr%d" % (12 + pr_),))
                            tasks += attn_tasks(
                                qk[hx], "qk%d" % hx, qk[2], "qk2", j,
                                lambda kb, pr_=pr_: Vt[:, kb, pr_ * 128:(pr_ + 1) * 128], True,
                                lambda kb, hd=hd, j=j: (ball[:, (j * NB + kb) * 8 + hd:(j * NB + kb) * 8 + hd + 1], ("ball",)), fix_fox, fin_f)
                    run_tasks(tasks)

                areset()
                roles["mm"] = [0, 1, 2, 3, 4, 5, 6, 7]
                ymh = aalloc(16 * HT, BF16).rearrange("p (a b) -> p a b", a=16)
                mT = aalloc(8 * HT, BF16).rearrange("p (a b) -> p a b", a=8)
                gt_ = [(aalloc(512), "gt0"), (aalloc(512), "gt1")]
                acc = aalloc(512)
                tmpm = aalloc(512)
                broff = [0, 8, 12]
                brkc = [8, 4, 4]
                for hf in range(2):
                    for c in range(16):
                        DMA(ymh[:, c, :], yscr[c][:, hf * HT:(hf + 1) * HT], r=("yscr%d" % c,), w=("ymh%d" % c,))
                    for f in range(8):
                        wg_ = [LW(wview(w_in[l], O_GT + b * 1024 + f * 128, 128), 8, 128) for b in range(3)]
                        wb_ = [LW(wview(w_br[b][l], f * 128, 128), brkc[b], 128) for b in range(3)]
                        for tt in range(TPH):
                            t = hf * TPH + tt
                            for b in range(3):
                                g_ = gt_[b % 2]
                                gbc = pcol[:, co["gb"] + (l * 3 + b) * 8 + f:co["gb"] + (l * 3 + b) * 8 + f + 1]
                                hgb = hcol[:, HB_G + (l * 3 + b) * 8 + f:HB_G + (l * 3 + b) * 8 + f + 1]
                                proj_fm(lambda t_, p, pk: ACT(g_[0], p[:], AF.Tanh, r=(pk, "hcol"), w=(g_[1],), bias=hgb, scale=0.5),
                                        wg_[b][0], wg_[b][1], 8, hT_rhs, hT_keys, NT, tiles=[t])
                                dst = acc if b == 0 else tmpm
                                dkey = "acc" if b == 0 else "tmpm"
                                proj_fm(lambda t_, p, pk: STT(dst, g_[0], 1.0, p[:], ALU.add, ALU.mult, r=(pk, g_[1]), w=(dkey,)),
                                        wb_[b][0], wb_[b][1], brkc[b],
                                        lambda k, t_, b=b, tt=tt: ymh[:, broff[b] + k, tt * 512:(tt + 1) * 512],
                                        lambda k, t_, b=b: ("ymh%d" % (broff[b] + k),), NT, tiles=[t])
                                if b == 1:
                                    TT("dve", acc, acc, tmpm, ALU.add, r=("acc", "tmpm"), w=("acc",))
                                elif b == 2:
                                    TT("dve", mT[:, f, tt * 512:(tt + 1) * 512], acc, tmpm, ALU.add, r=("acc", "tmpm"), w=("mT%d" % f,))
                    for f in range(8):
                        wo, wok = LW(wview(w_out[l], f * 128, 128), 8, 128)
                        for tt in range(TPH):
                            t = hf * TPH + tt
                            xk = "xT%d_%d" % (f, t)
                            proj_fm(lambda t_, p, pk: STT(xT[:, f, t * 512:(t + 1) * 512], p[:], 0.5, xT[:, f, t * 512:(t + 1) * 512], ALU.mult, ALU.add,
                                                          r=(pk, xk), w=(xk,)),
                                    wo, wok, 8, lambda k, t_, tt=tt: mT[:, k, tt * 512:(tt + 1) * 512], lambda k, t_: ("mT%d" % k,), NT, tiles=[t])

                norm_to_hT(co["n2"] + l * 8)
                areset()
                zT = aalloc(NFC * HT, BF16).rearrange("p (a b) -> p a b", a=NFC)
                ur = [(aalloc(2 + HT), "urg"), (aalloc(2 + HT), "urv")]
                uc = [(aalloc(HT), "ucg"), (aalloc(HT), "ucv")]
                ftmp = aalloc(HT)
                for hf in range(2):
                    for c in range(NFC):
                        for gv in range(2):
                            col = gv * DFF + c * 128
                            wt, wk = LW(wview(ffn_up[l], col, 128), 8, 128)
                            u_, uk = ur[gv]
                            if hf == 0:
                                MEMSET("dve", u_[:, 0:2], 0.0, w=(uk,))
                            else:
                                p, pk = bank("mm")
                                t0 = hf * HT
                                for k in range(KC):
                                    MM(p[:, 0:2], wt[:, k, :], hT[:, k, t0 - 2:t0], k == 0, k == KC - 1,
                                       r=(wk, "hT%d_%d" % (k, (t0 - 2) // 512)), w=(pk,))
                                ACT(u_[:, 0:2], p[:, 0:2], AF.Copy, r=(pk,), w=(uk,))
                            fc = gv * NFC + c
                            w0 = co["fw"] + (l * 3) * 44 + fc
                            o_, ok = uc[gv]
                            w2c = pcol[:, w0 + 88:w0 + 89]
                            bbc = pcol[:, co["fb"] + l * 44 + fc:co["fb"] + l * 44 + fc + 1]

                            def ffn_evac(t, p, pk, u_=u_, uk=uk, o_=o_, ok=ok, w2c=w2c, bbc=bbc):
                                tl = t - hf * TPH
                                ACT(u_[:, 2 + tl * 512:2 + (tl + 1) * 512], p[:], AF.Copy, r=(pk,), w=(uk,))
                                ACT(o_[:, tl * 512:(tl + 1) * 512], p[:], AF.Identity, r=(pk, "pcol"), w=(ok,), bias=bbc, scale=w2c)
                            proj_fm(ffn_evac, wt, wk, 8, hT_rhs, hT_keys, NT, tiles=[hf * TPH + tt for tt in range(TPH)])
                            for j in range(2):
                                STT(o_, u_[:, j:j + HT], pcol[:, w0 + 44 * j:w0 + 44 * j + 1], o_, ALU.mult, ALU.add, r=(uk, "pcol", ok), w=(ok,))
                        gelu2_from(ftmp, uc[0][0], ("ucg",), ftmp, "ftmp", "ftmp")
                        STT(zT[:, c, :], ftmp, 0.5, uc[1][0], ALU.mult, ALU.mult, r=("ftmp", "ucv"), w=("zT%d" % c,))
                    for f in range(8):
                        wd = [LW(ffn_dn[l][k0 * 128:(k0 + n_) * 128, :].rearrange("(k p) n -> p k n", p=128)[:, :, f * 128:(f + 1) * 128], n_, 128)
                              for (k0, n_) in ((0, 8), (8, 8), (16, 6))]
                        for tt in range(TPH):
                            t = hf * TPH + tt
                            xk = "xT%d_%d" % (f, t)
                            p, pk = bank("mm")
                            for kk in range(NFC):
                                wt_, wk_ = wd[kk // 8]
                                MM(p[:], wt_[:, kk % 8, :], zT[:, kk, tt * 512:(tt + 1) * 512], kk == 0, kk == NFC - 1, r=(wk_, "zT%d" % kk), w=(pk,))
                            TT("dve", xT[:, f, t * 512:(t + 1) * 512], p[:], xT[:, f, t * 512:(t + 1) * 512], ALU.add, r=(pk, xk), w=(xk,))
                roles["mm"] = [0, 1]

            areset()
            sq2 = [(aalloc(512), "nsq0"), (aalloc(512), "nsq1")]
            ms = aalloc(S)
            of = aalloc(8 * 512).rearrange("p (a b) -> p a b", a=8)
            ot = [(aalloc(1024), "ot0"), (aalloc(1024), "ot1")]
            ocnt = 0
            rstd_all(sq2, ms)
            for t in range(NT):
                for f in range(KC):
                    STT(of[:, f, :], xT[:, f, t * 512:(t + 1) * 512], pcol[:, co["fin"] + f:co["fin"] + f + 1], ms[:, t * 512:(t + 1) * 512],
                        ALU.mult, ALU.mult, r=("xT%d_%d" % (f, t), "pcol", "ms"), w=("of%d" % f,))
                for blk in range(4):
                    o_, ok = ot[ocnt % 2]
                    ocnt += 1
                    for g in range(2):
                        p, pk = bank("mm")
                        for i in range(4):
                            f = g * 4 + i
                            TR(p[:, i * 128:(i + 1) * 128], of[:, f, blk * 128:(blk + 1) * 128], r=("of%d" % f,), w=(pk,))
                        if g == 0:
                            CP("dve", o_[:, 0:512], p[:], r=(pk,), w=(ok,))
                        else:
                            ACT(o_[:, 512:1024], p[:], AF.Copy, r=(pk,), w=(ok,))
                    r0 = t * 512 + blk * 128
                    DMA(out_d[s, r0:r0 + 128, :], o_, r=(ok,), w=("out",))
        S_.emit()
    return nc


_NC_CACHE = {}


def _make_in_maps(inputs, L, nseq, ncores, S):
    pc, pr, strips, cst = _host_params(inputs, L)
    shared = {
        "w_in": np.ascontiguousarray(inputs["w_in"][:L], np.float32),
        "rg_w_r": np.ascontiguousarray(inputs["rg_w_r"][:L], np.float32),
        "rg_w_i": np.ascontiguousarray(inputs["rg_w_i"][:L], np.float32),
        "w_br_rnn": np.ascontiguousarray(inputs["w_br_rnn"][:L], np.float32),
        "w_br_diff": np.ascontiguousarray(inputs["w_br_diff"][:L], np.float32),
        "w_br_fox": np.ascontiguousarray(inputs["w_br_fox"][:L], np.float32),
        "w_out": np.ascontiguousarray(inputs["w_out"][:L], np.float32),
        "ffn_up": np.ascontiguousarray(inputs["ffn_up"][:L], np.float32),
        "ffn_down": np.ascontiguousarray(inputs["ffn_down"][:L], np.float32),
        "pcols": pc, "prows": pr, "strips": strips, "consts": cst,
    }
    x = np.asarray(inputs["x"], np.float32)
    maps = []
    for c in range(ncores):
        m = dict(shared)
        m["x"] = np.ascontiguousarray(x[c * nseq:(c + 1) * nseq, :S])
        maps.append(m)
    return maps


def kernel(**inputs):
    inputs = {k: np.asarray(v) for k, v in inputs.items()}
    B, S, _ = inputs["x"].shape
    L = inputs["w_in"].shape[0]
    ncores = 8
    nseq = B // ncores
    key = (S, L, nseq)
    if key not in _NC_CACHE:
        _NC_CACHE[key] = build_nc(S=S, L=L, NSEQ=nseq)
    nc = _NC_CACHE[key]
    maps = _make_in_maps(inputs, L, nseq, ncores, S)
    res = run_bass_kernel_spmd(nc, maps, core_ids=list(range(ncores)))
    out = np.concatenate([np.asarray(r["out"]) for r in res.results], axis=0)
    return out.astype(np.float32)
```

```python
import contextlib
import math
import numpy as np
import concourse.bass as bass
import concourse.mybir as mybir
from concourse.bass_utils import run_bass_kernel_spmd

F32 = mybir.dt.float32
BF16 = mybir.dt.bfloat16
AF = mybir.ActivationFunctionType
ALU = mybir.AluOpType

D = 1024
KC = 8
NIN = 8200
DFF = 2816
NFC = 22
EPS = 1e-6
NEG = -30000.0
O_XR, O_GR, O_DQ, O_DK, O_DV, O_FQ, O_FK, O_FV, O_FL, O_GT = 0, 1024, 2048, 2560, 3072, 3584, 4096, 4608, 5120, 5128


class _Op:
    __slots__ = ("eng", "fn", "waits", "signal", "idx", "count", "dma", "dsem", "dcount", "clock")

    def __init__(self, eng, fn, dma):
        self.eng = eng; self.fn = fn; self.waits = []; self.signal = False; self.idx = -1
        self.count = 0; self.dma = dma; self.dsem = None; self.dcount = 0; self.clock = None


class Sched:
    ENG = ("pe", "act", "dve", "pool", "sp")
    NDSEM = 24

    def __init__(self, nc):
        self.nc = nc
        self.ops = {e: [] for e in self.ENG}
        self.lastw = {}
        self.readers = {}
        self.seen = {e: {f: -1 for f in self.ENG} for e in self.ENG}
        self.seen_d = {e: [0] * self.NDSEM for e in self.ENG}
        self.ndma = 0
        self.dma_ops = []

    def op(self, eng, fn, r=(), w=(), dma=False):
        o = _Op(eng, fn, dma)
        o.idx = len(self.ops[eng])
        r = tuple(r) + ("PHASE",)
        deps = []
        for k in r:
            x = self.lastw.get(k)
            if x is not None:
                deps.append(x)
        for k in w:
            x = self.lastw.get(k)
            if x is not None:
                deps.append(x)
            deps.extend(self.readers.get(k, ()))
        if dma:
            j = self.ndma
            self.ndma += 1
            o.dsem = j % self.NDSEM
            o.dcount = 16 * (j // self.NDSEM + 1)
            if j >= self.NDSEM:
                deps.append(self.dma_ops[j - self.NDSEM])
            self.dma_ops.append(o)
        seen = self.seen[eng]
        sd = self.seen_d[eng]
        best_e = {}
        best_d = {}
        for d in deps:
            if d is o:
                continue
            if d.dma:
                if d.dcount > sd[d.dsem] and (d.dsem not in best_d or d.dcount > best_d[d.dsem].dcount):
                    best_d[d.dsem] = d
            else:
                if d.eng == eng and eng == "pe":
                    continue
                if d.idx > seen[d.eng] and (d.eng not in best_e or d.idx > best_e[d.eng].idx):
                    best_e[d.eng] = d
        for d in best_e.values():
            if seen[d.eng] >= d.idx:
                continue
            d.signal = True
            o.waits.append(d)
            seen[d.eng] = d.idx
            if d.clock is not None:
                for f, v in d.clock.items():
                    if f != eng and v > seen[f]:
                        seen[f] = v
        for d in best_d.values():
            sd[d.dsem] = d.dcount
            o.waits.append(d)
        if not dma:
            o.clock = dict(seen)
            o.clock[eng] = o.idx
        for k in r:
            self.readers.setdefault(k, []).append(o)
        for k in w:
            self.lastw[k] = o
            self.readers[k] = []
        self.ops[eng].append(o)
        return o

    def emit(self):
        nc = self.nc
        with contextlib.ExitStack() as st:
            esem = {e: st.enter_context(nc.semaphore("s_" + e)) for e in self.ENG}
            dsem = [st.enter_context(nc.semaphore("d%d" % i)) for i in range(self.NDSEM)]
            for e in self.ENG:
                c = 0
                for o in self.ops[e]:
                    if o.signal and not o.dma:
                        c += 1
                        o.count = c
            last = {}
            for o in self.dma_ops:
                last[o.dsem] = o
            final = list(last.values())
            block = st.enter_context(nc.Block())

            def run(e, eng):
                for o in self.ops[e]:
                    for d in o.waits:
                        if d.dma:
                            eng.wait_ge(dsem[d.dsem], d.dcount)
                        else:
                            eng.wait_ge(esem[d.eng], d.count)
                    ins = o.fn(eng)
                    if o.dma:
                        ins.then_inc(dsem[o.dsem], 16)
                    elif o.signal:
                        ins.then_inc(esem[e], 1)
                if e == "sp":
                    for o in final:
                        eng.wait_ge(dsem[o.dsem], o.dcount)

            @block.tensor
            def _(eng):
                run("pe", eng)

            @block.scalar
            def _(eng):
                run("act", eng)

            @block.vector
            def _(eng):
                run("dve", eng)

            @block.gpsimd
            def _(eng):
                run("pool", eng)

            @block.sync
            def _(eng):
                run("sp", eng)


def _col_layout(L):
    off = {}
    n = 0
    for name, w in (("n1", L * 8), ("n2", L * 8), ("fin", 8), ("cw", L * 4 * 8), ("cb", L * 8), ("br", L * 8),
                    ("bi", L * 8), ("ra", L * 8), ("gb", L * 3 * 8), ("fw", L * 3 * 44), ("fb", L * 44), ("sg", L)):
        off[name] = n
        n += w
    return off, n


def _row_layout(L):
    off = {}
    n = 0
    for name, w in (("lq", L * 4 * 64), ("bf", L * 128), ("ch", 4)):
        off[name] = n
        n += w
    return off, n


def _t5_bucket_np(dist):
    n = np.maximum(dist, 0)
    nf = np.maximum(n, 1).astype(np.float32)
    large = 16 + (np.log(nf / np.float32(16)) / np.float32(math.log(128 / 16)) * np.float32(16)).astype(np.int32)
    large = np.minimum(large, 31)
    return np.where(n < 16, n, large)


def _host_params(inp, L):
    co, nc_ = _col_layout(L)
    ro, nr = _row_layout(L)
    pc = np.zeros((128, nc_), np.float32)

    def cols(v):
        v = np.asarray(v, np.float32)
        lead = int(np.prod(v.shape[:-1])) if v.ndim > 1 else 1
        k = v.shape[-1] // 128
        return v.reshape(lead, k, 128).transpose(2, 0, 1).reshape(128, lead * k)

    pc[:, co["n1"]:co["n1"] + L * 8] = cols(inp["norm1_g"][:L])
    pc[:, co["n2"]:co["n2"] + L * 8] = cols(inp["norm2_g"][:L])
    pc[:, co["fin"]:co["fin"] + 8] = cols(inp["final_g"])
    pc[:, co["cw"]:co["cw"] + L * 32] = cols(inp["rnn_conv_w"][:L])
    pc[:, co["cb"]:co["cb"] + L * 8] = cols(inp["rnn_conv_b"][:L])
    pc[:, co["br"]:co["br"] + L * 8] = cols(inp["rg_b_r"][:L])
    pc[:, co["bi"]:co["bi"] + L * 8] = cols(inp["rg_b_i"][:L])
    pc[:, co["ra"]:co["ra"] + L * 8] = cols(inp["rg_a"][:L])
    pc[:, co["gb"]:co["gb"] + L * 24] = cols(inp["gate_b"][:L])
    pc[:, co["fw"]:co["fw"] + L * 132] = cols(inp["ffn_conv_w"][:L])
    pc[:, co["fb"]:co["fb"] + L * 44] = cols(inp["ffn_conv_b"][:L])
    pc[:, co["sg"]:co["sg"] + L] = np.asarray(inp["diff_subln_g"][:L], np.float32).T
    pr = np.zeros((128, nr), np.float32)
    lq = np.stack([np.asarray(inp[k][:L], np.float32) for k in ("diff_lq1", "diff_lk1", "diff_lq2", "diff_lk2")], 1)
    pr[:, ro["lq"]:ro["lq"] + L * 256] = np.broadcast_to(lq.reshape(1, L * 256), (128, L * 256))
    bf = np.tile(np.asarray(inp["fox_b_f"][:L], np.float32).reshape(L, 1, 8), (1, 16, 1)).reshape(1, L * 128)
    pr[:, ro["bf"]:ro["bf"] + L * 128] = np.broadcast_to(bf, (128, L * 128))
    rel = np.asarray(inp["rel_bias"], np.float32)
    pr[:, ro["ch"]:ro["ch"] + 4] = np.broadcast_to(rel[31:32, :], (128, 4))
    kl = np.arange(128)[:, None]
    sx = np.arange(256)[None, :]
    dist = sx - kl
    bidx = _t5_bucket_np(dist)
    strips = np.zeros((128, 4 * 256), np.float32)
    for h in range(4):
        g = rel[bidx, h]
        strips[:, h * 256:(h + 1) * 256] = np.where(dist >= 0, g, np.float32(NEG))
    cst = np.zeros((128, 4 * 128), np.float32)
    cst[:, 0:128] = np.eye(128, dtype=np.float32)
    cst[:, 128:256] = (np.arange(128)[:, None] <= np.arange(128)[None, :]).astype(np.float32)
    cst[:, 256:384] = np.where(np.arange(128)[:, None] <= np.arange(128)[None, :], 0.0, NEG)
    cst[:, 384:512] = (np.arange(128)[:, None] == (np.arange(128)[None, :] + 64) % 128).astype(np.float32)
    return pc, pr, strips, cst


def build_nc(S=2048, L=2, NSEQ=2):
    assert S % 1024 == 0
    NT = S // 512
    NB = S // 128
    HT = S // 2
    TPH = NT // 2
    co, ncol = _col_layout(L)
    ro, nrow = _row_layout(L)

    nc = bass.Bass("TRN2", target_bir_lowering=False)

    def din(name, shape):
        return nc.dram_tensor(name, shape, F32, kind="ExternalInput").ap()

    x_d = din("x", [NSEQ, S, D])
    w_in = din("w_in", [L, D, NIN])
    rg_wr = din("rg_w_r", [L, 8, 128, 128])
    rg_wi = din("rg_w_i", [L, 8, 128, 128])
    w_br = [din("w_br_rnn", [L, D, D]), din("w_br_diff", [L, 512, D]), din("w_br_fox", [L, 512, D])]
    w_out = din("w_out", [L, D, D])
    ffn_up = din("ffn_up", [L, D, 2 * DFF])
    ffn_dn = din("ffn_down", [L, DFF, D])
    pcol_d = din("pcols", [128, ncol])
    prow_d = din("prows", [128, nrow])
    strip_d = din("strips", [128, 1024])
    cst_d = din("consts", [128, 512])
    out_d = nc.dram_tensor("out", [NSEQ, S, D], F32, kind="ExternalOutput").ap()
    yscr = nc.dram_tensor("yscr", [16, 128, S], BF16, kind="Internal").ap()

    with contextlib.ExitStack() as st:
        def sb(name, shape, dt=F32):
            return st.enter_context(nc.sbuf_tensor(name, shape, dt))

        S_ = Sched(nc)
        xT = sb("xT", [128, KC, S])
        hT = sb("hT", [128, KC, S], BF16)
        ARN = 17424
        arena = sb("arena", [128, ARN])
        NSTG, NWB = 3, 8
        stg = [sb("stg%d" % i, [128, 8, 128]) for i in range(NSTG)]
        wbs = [sb("wb%d" % i, [128, 8, 128], BF16) for i in range(NWB)]
        pcol = sb("pcol_sb", [128, ncol])
        prow = sb("prow_sb", [128, nrow])
        cst = sb("cst_sb", [128, 512])
        strips = sb("strips_sb", [128, 1024])
        dcol = sb("dcol", [128, 16 * L + 8])
        hcol = sb("hcol", [128, L * 48 + 8])
        ones32 = sb("ones32", [128, 128])
        onesb = sb("onesb", [128, 128], BF16)
        phz = sb("phz", [128, 1])
        banks = [st.enter_context(nc.psum_tensor("bank%d" % i, [128, 512], F32)) for i in range(8)]
        ident = cst[:, 0:128]
        tri = cst[:, 128:256]
        cmask = cst[:, 256:384]
        shm = cst[:, 384:512]

        def MM(out, lhsT, rhs, start, stop, r, w):
            S_.op("pe", lambda e: e.matmul(out, lhsT=lhsT, rhs=rhs, start=start, stop=stop), r=r, w=w)

        def TR(out, in_, r, w):
            S_.op("pe", lambda e: e.transpose(out=out, in_=in_, identity=ident), r=tuple(r) + ("cst",), w=w)

        def ACT(out, in_, func, r, w, bias=None, scale=None):
            kw = {}
            if bias is not None:
                kw["bias"] = bias
            if scale is not None:
                kw["scale"] = scale
            S_.op("act", lambda e: e.activation(out=out, in_=in_, func=func, **kw), r=r, w=w)

        def TT(eng, out, in0, in1, op, r, w):
            S_.op(eng, lambda e: e.tensor_tensor(out=out, in0=in0, in1=in1, op=op), r=r, w=w)

        def TS(eng, out, in0, s1, s2, op0, op1, r, w):
            if s2 is None:
                S_.op(eng, lambda e: e.tensor_scalar(out=out, in0=in0, scalar1=s1, scalar2=None, op0=op0), r=r, w=w)
            else:
                S_.op(eng, lambda e: e.tensor_scalar(out=out, in0=in0, scalar1=s1, scalar2=s2, op0=op0, op1=op1), r=r, w=w)

        def STT(out, in0, scalar, in1, op0, op1, r, w):
            S_.op("dve", lambda e: e.scalar_tensor_tensor(out=out, in0=in0, scalar=scalar, in1=in1, op0=op0, op1=op1), r=r, w=w)

        def CP(eng, out, in_, r, w):
            S_.op(eng, lambda e: e.tensor_copy(out=out, in_=in_), r=r, w=w)

        def RECIP(out, in_, r, w):
            S_.op("dve", lambda e: e.reciprocal(out=out, in_=in_), r=r, w=w)

        def MEMSET(eng, ap, val, w):
            S_.op(eng, lambda e: e.memset(ap, val), w=w)

        def DMA(out, in_, r, w):
            S_.op("sp", lambda e: e.dma_start(out=out, in_=in_), r=r, w=w, dma=True)

        def barrier():
            S_.op("dve", lambda e: e.memset(phz[:], 0.0), w=("PHASE", "phz"))

        apos = [0]

        def areset():
            barrier()
            apos[0] = 0

        def aalloc(nelem, dt=F32):
            n32 = nelem if dt == F32 else (nelem + 1) // 2
            n32 = (n32 + 7) // 8 * 8
            a0 = apos[0]
            apos[0] += n32
            assert apos[0] <= ARN, ("arena overflow", apos[0])
            v = arena[:, a0:a0 + n32]
            if dt != F32:
                v = v.bitcast(dt)[:, 0:nelem]
            else:
                v = v[:, 0:nelem]
            return v

        roles = {"mm": [0, 1], "sc": [2, 3], "acc": [4, 5, 6, 7], "pj": [0, 4, 5, 6, 7]}
        rpos = {"mm": 0, "sc": 0, "acc": 0, "pj": 0}

        def bank(role):
            lst = roles[role]
            i = lst[rpos[role] % len(lst)]
            rpos[role] += 1
            return banks[i], "bank%d" % i

        wcnt = [0, 0]

        def LW(src, kc, ncols, dst=None, dkey=None):
            i = wcnt[0] % NSTG
            wcnt[0] += 1
            DMA(stg[i][:, 0:kc, 0:ncols], src, r=(), w=("stg%d" % i,))
            if dst is None:
                j = wcnt[1] % NWB
                wcnt[1] += 1
                dst = wbs[j][:, 0:kc, 0:ncols]
                dkey = "wb%d" % j
                ret = wbs[j]
            else:
                ret = None
            if wcnt[0] % 3 == 0:
                ACT(dst, stg[i][:, 0:kc, 0:ncols], AF.Copy, r=("stg%d" % i,), w=(dkey,))
            else:
                CP("pool", dst, stg[i][:, 0:kc, 0:ncols], r=("stg%d" % i,), w=(dkey,))
            return ret, dkey

        def wview(ap2d, c0, ncols):
            return ap2d.rearrange("(k p) n -> p k n", p=128)[:, :, c0:c0 + ncols]

        DMA(pcol[:], pcol_d, r=(), w=("pcol",))
        DMA(prow[:], prow_d, r=(), w=("prow",))
        DMA(cst[:], cst_d, r=(), w=("cst",))
        DMA(strips[:], strip_d, r=(), w=("strips",))
        MEMSET("dve", ones32[:], 1.0, w=("ones32",))
        MEMSET("dve", onesb[:], 1.0, w=("onesb",))
        DC_EPS, DC_ONE = 16 * L, 16 * L + 1
        MEMSET("dve", dcol[:, DC_EPS:DC_EPS + 1], EPS, w=("dcol",))
        MEMSET("dve", dcol[:, DC_ONE:DC_ONE + 1], 1.0, w=("dcol",))
        epsc = dcol[:, DC_EPS:DC_EPS + 1]
        onec = dcol[:, DC_ONE:DC_ONE + 1]
        for h in range(4):
            TS("dve", strips[:, h * 256:(h + 1) * 256], strips[:, h * 256:(h + 1) * 256],
               prow[:, ro["ch"] + h:ro["ch"] + h + 1], None, ALU.subtract, None, r=("strips", "prow"), w=("strips",))
        HB_R, HB_I, HB_G, HB_A, HB_C = 0, L * 8, L * 16, L * 40, L * 48
        TS("dve", hcol[:, HB_R:HB_R + L * 16], pcol[:, co["br"]:co["br"] + L * 16], 0.5, None, ALU.mult, None, r=("pcol",), w=("hcol",))
        TS("dve", hcol[:, HB_G:HB_G + L * 24], pcol[:, co["gb"]:co["gb"] + L * 24], 0.5, None, ALU.mult, None, r=("pcol",), w=("hcol",))
        MEMSET("dve", hcol[:, HB_C:HB_C + 1], 1.0 / 16, w=("hcol",))
        sixteenth = hcol[:, HB_C:HB_C + 1]
        lam_init = [0.8 - 0.6 * math.exp(-0.3 * l) for l in range(L)]
        for l in range(L):
            b0 = l * 16
            ACT(dcol[:, b0:b0 + 8], pcol[:, co["ra"] + l * 8:co["ra"] + l * 8 + 8], AF.Exp, r=("pcol",), w=("dcol",), scale=-1.0)
            ACT(dcol[:, b0:b0 + 8], dcol[:, b0:b0 + 8], AF.Ln, r=("dcol",), w=("dcol",), bias=onec)
            TS("dve", dcol[:, b0:b0 + 8], dcol[:, b0:b0 + 8], -8.0, None, ALU.mult, None, r=("dcol",), w=("dcol",))
            TS("dve", hcol[:, HB_A + l * 8:HB_A + l * 8 + 8], dcol[:, b0:b0 + 8], 0.5, None, ALU.mult, None, r=("dcol",), w=("hcol",))
            q0 = ro["lq"] + l * 256
            for pair in range(2):
                tmp = arena[:, 0:64]
                TT("dve", tmp, prow[:, q0 + pair * 128:q0 + pair * 128 + 64], prow[:, q0 + pair * 128 + 64:q0 + pair * 128 + 128],
                   ALU.mult, r=("prow",), w=("lamtmp",))
                S_.op("dve", lambda e, o_=dcol[:, b0 + 10 + pair:b0 + 11 + pair], i_=tmp: e.tensor_reduce(
                    out=o_, in_=i_, axis=mybir.AxisListType.X, op=ALU.add), r=("lamtmp",), w=("dcol",))
            ACT(dcol[:, b0 + 10:b0 + 12], dcol[:, b0 + 10:b0 + 12], AF.Exp, r=("dcol",), w=("dcol",))
            TT("dve", dcol[:, b0 + 8:b0 + 9], dcol[:, b0 + 11:b0 + 12], dcol[:, b0 + 10:b0 + 11], ALU.subtract, r=("dcol",), w=("dcol",))
            TS("dve", dcol[:, b0 + 8:b0 + 9], dcol[:, b0 + 8:b0 + 9], -lam_init[l], None, ALU.add, None, r=("dcol",), w=("dcol",))
            TS("dve", dcol[:, b0 + 9:b0 + 10], pcol[:, co["sg"] + l:co["sg"] + l + 1], 1.0 - lam_init[l], None, ALU.mult, None,
               r=("pcol",), w=("dcol",))

        def rstd_all(sq2, ms):
            for t in range(NT):
                pn, pk = bank("mm")
                for f in range(KC):
                    sq = sq2[f % 2]
                    ACT(sq[0], xT[:, f, t * 512:(t + 1) * 512], AF.Square, r=("xT%d_%d" % (f, t),), w=(sq[1],))
                    MM(pn[:], ones32[:], sq[0], f == 0, f == KC - 1, r=(sq[1], "ones32"), w=(pk,))
                ACT(ms[:, t * 512:(t + 1) * 512], pn[:], AF.Copy, r=(pk,), w=("ms",), scale=1.0 / D)
            ACT(ms, ms, AF.Sqrt, r=("ms", "dcol"), w=("ms",), bias=epsc)
            RECIP(ms, ms, r=("ms",), w=("ms",))

        def norm_to_hT(gbase):
            areset()
            sq2 = [(aalloc(512), "nsq0"), (aalloc(512), "nsq1")]
            ms = aalloc(S)
            rstd_all(sq2, ms)
            for t in range(NT):
                for f in range(KC):
                    STT(hT[:, f, t * 512:(t + 1) * 512], xT[:, f, t * 512:(t + 1) * 512], pcol[:, gbase + f:gbase + f + 1],
                        ms[:, t * 512:(t + 1) * 512], ALU.mult, ALU.mult, r=("xT%d_%d" % (f, t), "pcol", "ms"), w=("hT%d_%d" % (f, t),))

        def proj_fm(dst_fn, wt, wkey, kc, rhs_fn, rkeys_fn, ntiles, M=128, tiles=None, role="mm"):
            for t in (tiles if tiles is not None else range(ntiles)):
                p, pk = bank(role)
                for k in range(kc):
                    MM(p[0:M, :], wt[:, k, 0:M], rhs_fn(k, t), k == 0, k == kc - 1, r=(wkey,) + tuple(rkeys_fn(k, t)), w=(pk,))
                dst_fn(t, p, pk)

        def hT_rhs(k, t):
            return hT[:, k, t * 512:(t + 1) * 512]

        def hT_keys(k, t):
            return ("hT%d_%d" % (k, t),)

        def gelu2_from(dst, src, srckeys, tmp, tmpkey, dkey):
            ACT(tmp, src, AF.Square, r=srckeys, w=(tmpkey,), scale=0.21145921595661512)
            STT(tmp, tmp, 1.0, src, ALU.add, ALU.mult, r=(tmpkey,) + tuple(srckeys), w=(tmpkey,))
            ACT(tmp, tmp, AF.Tanh, r=(tmpkey,), w=(tmpkey,), scale=0.7978845608028654)
            STT(dst, tmp, 1.0, src, ALU.add, ALU.mult, r=(tmpkey,) + tuple(srckeys), w=(dkey,))

        for s in range(NSEQ):
            areset()
            xin = [(aalloc(1024), "xin0"), (aalloc(1024), "xin1")]
            for b in range(NB):
                xi = xin[b % 2]
                DMA(xi[0], x_d[s, b * 128:(b + 1) * 128, :], r=(), w=(xi[1],))
                for g in range(2):
                    p, pk = bank("mm")
                    for i in range(4):
                        f = g * 4 + i
                        TR(p[:, i * 128:(i + 1) * 128], xi[0][:, f * 128:(f + 1) * 128], r=(xi[1],), w=(pk,))
                    t = b // 4
                    wk = tuple("xT%d_%d" % (g * 4 + i, t) for i in range(4))
                    dstx = xT[:, g * 4:g * 4 + 4, b * 128:(b + 1) * 128]
                    srcx = p[:].rearrange("p (a b) -> p a b", a=4)
                    if g == 0:
                        CP("dve", dstx, srcx, r=(pk,), w=wk)
                    else:
                        ACT(dstx, srcx, AF.Copy, r=(pk,), w=wk)

            for l in range(L):
                dc0 = l * 16
                norm_to_hT(co["n1"] + l * 8)

                areset()
                roles["mm"] = [0, 1, 2, 3]
                xr = aalloc(3 + S)
                gg = [aalloc(S), aalloc(S)]
                xc = aalloc(S)
                xcb = aalloc(S, BF16)
                rr = aalloc(S)
                ii = aalloc(S)
                a2 = aalloc(S)
                hh = a2
                gtmp = [(aalloc(512), "gtmp0"), (aalloc(512), "gtmp1")]
                ybuf = aalloc(S, BF16)
                carry = aalloc(8)
                MEMSET("dve", xr[:, 0:3], 0.0, w=("xrh",))

                def sl(t):
                    return slice(t * 512, (t + 1) * 512)

                def rnn_W(n):
                    wx_ = LW(wview(w_in[l], O_XR + n * 128, 128), 8, 128)
                    wg_ = LW(wview(w_in[l], O_GR + n * 128, 128), 8, 128)
                    wr_ = LW(rg_wr[l, n].rearrange("(k p) n -> p k n", p=128), 1, 128)
                    wi_ = LW(rg_wi[l, n].rearrange("(k p) n -> p k n", p=128), 1, 128)
                    return wx_, wr_, wi_, wg_

                def rnn_Pg(n, wg_, t):
                    g_ = gg[n % 2]

                    def ev(t_, p, pk):
                        gt = gtmp[t_ % 2]
                        gelu2_from(g_[:, sl(t_)], p[:], (pk,), gt[0], gt[1], "gg%d_%d" % (n % 2, t_))
                    proj_fm(ev, wg_[0], wg_[1], 8, hT_rhs, hT_keys, NT, tiles=[t])

                def rnn_Px(n, wx_, t):
                    wx, wxk = wx_
                    cw = co["cw"] + (l * 4) * 8 + n
                    w3c = pcol[:, cw + 24:cw + 25]
                    cbc = pcol[:, co["cb"] + l * 8 + n:co["cb"] + l * 8 + n + 1]

                    def ev(t_, p, pk):
                        ACT(xr[:, 3 + t_ * 512:3 + (t_ + 1) * 512], p[:], AF.Copy, r=(pk,), w=("xr%d" % t_,))
                        ACT(xc[:, sl(t_)], p[:], AF.Identity, r=(pk, "pcol"), w=("xc%d" % t_,), bias=cbc, scale=w3c)
                    proj_fm(ev, wx, wxk, 8, hT_rhs, hT_keys, NT, tiles=[t])

                def rnn_R1(n, wr_, wi_, t, after_conv):
                    wr, wrk = wr_
                    wi, wik = wi_
                    cw = co["cw"] + (l * 4) * 8 + n
                    xk = ("xr%d" % t, "xr%d" % (t - 1) if t > 0 else "xrh", "pcol", "xc%d" % t)
                    for j in range(3):
                        STT(xc[:, sl(t)], xr[:, j + t * 512:j + (t + 1) * 512], pcol[:, cw + 8 * j:cw + 8 * j + 1], xc[:, sl(t)],
                            ALU.mult, ALU.add, r=xk, w=("xc%d" % t,))
                    ACT(xcb[:, sl(t)], xc[:, sl(t)], AF.Copy, r=("xc%d" % t,), w=("xcb%d" % t,))
                    after_conv()
                    hbr = hcol[:, HB_R + l * 8 + n:HB_R + l * 8 + n + 1]
                    hbi = hcol[:, HB_I + l * 8 + n:HB_I + l * 8 + n + 1]
                    hsa = hcol[:, HB_A + l * 8 + n:HB_A + l * 8 + n + 1]
                    proj_fm(lambda t_, p, pk: ACT(rr[:, sl(t)], p[:], AF.Tanh, r=(pk, "hcol"), w=("rr%d" % t,), bias=hbr, scale=0.5),
                            wr, wrk, 1, lambda k, t_: xcb[:, sl(t)], lambda k, t_: ("xcb%d" % t,), NT, tiles=[t])
                    proj_fm(lambda t_, p, pk: ACT(ii[:, sl(t)], p[:], AF.Tanh, r=(pk, "hcol"), w=("ii%d" % t,), bias=hbi, scale=0.5),
                            wi, wik, 1, lambda k, t_: xcb[:, sl(t)], lambda k, t_: ("xcb%d" % t,), NT, tiles=[t])
                    ACT(rr[:, sl(t)], rr[:, sl(t)], AF.Exp, r=("rr%d" % t, "hcol"), w=("rr%d" % t,), bias=hsa, scale=hsa)
                    TT("pool", a2[:, sl(t)], rr[:, sl(t)], rr[:, sl(t)], ALU.mult, r=("rr%d" % t,), w=("a2_%d" % t,))
                    STT(ii[:, sl(t)], ii[:, sl(t)], 1.0, xc[:, sl(t)], ALU.add, ALU.mult, r=("ii%d" % t, "xc%d" % t), w=("ii%d" % t,))

                def rnn_R2(n, t):
                    g_ = gg[n % 2]
                    TT("pool", ii[:, sl(t)], ii[:, sl(t)], a2[:, sl(t)], ALU.mult, r=("ii%d" % t, "a2_%d" % t), w=("ii%d" % t,))
                    init = 0.0 if t == 0 else carry[:, t - 1:t]
                    rk = ("rr%d" % t, "ii%d" % t, "a2_%d" % t) + (("carry%d" % (t - 1),) if t > 0 else ())
                    S_.op("dve", lambda e, o_=hh[:, sl(t)], d0=rr[:, sl(t)], d1=ii[:, sl(t)], init=init: e.tensor_tensor_scan(
                        out=o_, data0=d0, data1=d1, initial=init, op0=ALU.mult, op1=ALU.add), r=rk, w=("a2_%d" % t,))
                    if t + 1 < NT:
                        CP("pool", carry[:, t:t + 1], hh[:, (t + 1) * 512 - 1:(t + 1) * 512], r=("a2_%d" % t,), w=("carry%d" % t,))
                    TT("pool", ybuf[:, sl(t)], g_[:, sl(t)], hh[:, sl(t)], ALU.mult, r=("gg%d_%d" % (n % 2, t), "a2_%d" % t), w=("ybuf%d" % t,))

                def rnn_sqrt(n):
                    ak = tuple("a2_%d" % t for t in range(NT))
                    ACT(a2, a2, AF.Sqrt, r=ak + ("hcol",), w=ak, bias=sixteenth, scale=-1.0 / 16)

                def rnn_out(n):
                    DMA(yscr[n], ybuf, r=tuple("ybuf%d" % t for t in range(NT)), w=("yscr%d" % n,))

                W = {0: rnn_W(0)}
                for t in range(NT):
                    rnn_Pg(0, W[0][3], t)
                    rnn_Px(0, W[0][0], t)
                for n in range(8):
                    if n + 1 < 8:
                        W[n + 1] = rnn_W(n + 1)
                    for t in range(NT):
                        if n > 0:
                            rnn_R2(n - 1, t)
                        if n + 1 < 8:
                            rnn_Pg(n + 1, W[n + 1][3], t)
                        if n + 1 < 8 and t > 0:
                            rnn_R1(n, W[n][1], W[n][2], t, lambda t=t: rnn_Px(n + 1, W[n + 1][0], t - 1))
                        else:
                            rnn_R1(n, W[n][1], W[n][2], t, lambda: None)
                    if n > 0:
                        rnn_out(n - 1)
                    if n + 1 < 8:
                        rnn_Px(n + 1, W[n + 1][0], NT - 1)
                    rnn_sqrt(n)
                for t in range(NT):
                    rnn_R2(7, t)
                rnn_out(7)

                areset()
                roles["mm"] = [0]
                roles["sc"] = [1, 2, 3]
                Vt = aalloc(NB * 576, BF16).rearrange("p (a b) -> p a b", a=NB)
                MEMSET("pool", Vt[:, :, 512:576], 1.0, w=("Vt",))
                Wv = aalloc(8 * 512, BF16).rearrange("p (a b) -> p a b", a=8)
                qk = [aalloc(S, BF16) for _ in range(3)]
                Et = [(aalloc(512, BF16), "E%d" % i) for i in range(4)]
                rz = aalloc(512)
                oo = [aalloc(512), aalloc(512)]
                sq_ = aalloc(512)
                rs_ = aalloc(512)
                yt = [(aalloc(512, BF16), "yt%d" % i) for i in range(2)]
                ecnt = [0]

                def build_V(colbase):
                    for i in range(4):
                        LW(wview(w_in[l], colbase + i * 128, 128), 8, 128, dst=Wv[:, :, i * 128:(i + 1) * 128], dkey="Wv")
                    for b in range(NB):
                        p, pk = bank("pj")
                        for k in range(KC):
                            MM(p[:], hT[:, k, b * 128:(b + 1) * 128], Wv[:, k, :], k == 0, k == KC - 1,
                               r=("hT%d_%d" % (k, b // 4), "Wv"), w=(pk,))
                        if b % 2 == 0:
                            CP("dve", Vt[:, b, 0:512], p[:], r=(pk,), w=("Vt",))
                        else:
                            ACT(Vt[:, b, 0:512], p[:], AF.Copy, r=(pk,), w=("Vt",))

                MEMSET("dve", qk[0][64:128, :], 0.0, w=("qk0",))
                MEMSET("dve", qk[1][0:64, :], 0.0, w=("qk1",))

                def load_qk(colq, colk):
                    return LW(wview(w_in[l], colq, 128), 8, 128), LW(wview(w_in[l], colk, 128), 8, 128)

                def build_qk3(wq_, wk_):
                    def evq(t, p, pk):
                        ACT(qk[0][0:64, t * 512:(t + 1) * 512], p[0:64, :], AF.Copy, r=(pk,), w=("qk0",), scale=0.125)
                        ACT(qk[1][64:128, t * 512:(t + 1) * 512], p[64:128, :], AF.Copy, r=(pk,), w=("qk1",), scale=0.125)
                    proj_fm(evq, wq_[0], wq_[1], 8, hT_rhs, hT_keys, NT, role="pj")

                    def evk(t, p, pk):
                        CP("dve", qk[2][:, t * 512:(t + 1) * 512], p[:], r=(pk,), w=("qk2",))
                    proj_fm(evk, wk_[0], wk_[1], 8, hT_rhs, hT_keys, NT, role="pj")

                NE = len(Et)

                def attn_tasks(qT, qkey, kT, kkey, j, v_fn, use_pz, bias_fn, fix_fn, fin_fn):
                    nk = 4 * (j + 1)
                    st_ = {}
                    tasks = []
                    for kb in range(nk):
                        def A(kb=kb):
                            m = kb - 4 * j
                            c0 = max(0, 128 * m)
                            ps, psk = bank("sc")
                            MM(ps[:, c0:512], kT[:, kb * 128:(kb + 1) * 128], qT[:, j * 512 + c0:(j + 1) * 512], True, True,
                               r=(qkey, kkey), w=(psk,))
                            fix_fn(ps, psk, m)
                            E = Et[ecnt[0] % NE]
                            ecnt[0] += 1
                            bcol, bkeys = bias_fn(kb)
                            ACT(E[0][:, c0:512], ps[:, c0:512], AF.Exp, r=(psk,) + tuple(bkeys), w=(E[1],), bias=bcol)
                            st_[kb] = (E, c0)

                        def B(kb=kb):
                            if kb == 0:
                                st_["po"] = bank("acc")
                                st_["pz"] = bank("acc") if use_pz else (None, None)
                            po, pok = st_["po"]
                            pz, pzk = st_["pz"]
                            E, c0 = st_.pop(kb)
                            MM(po[:, c0:512], v_fn(kb), E[0][:, c0:512], kb == 0, kb == nk - 1, r=("Vt", E[1]), w=(pok,))
                            if use_pz:
                                MM(pz[:, c0:512], onesb[:], E[0][:, c0:512], kb == 0, kb == nk - 1, r=("onesb", E[1]), w=(pzk,))
                            if kb == nk - 1:
                                fin_fn(po, pok, pz, pzk)
                        tasks.append((A, B))
                    return tasks

                def run_tasks(tasks, LA=3):
                    n = len(tasks)
                    for i in range(n + LA):
                        if i < n:
                            tasks[i][0]()
                        if i - LA >= 0:
                            tasks[i - LA][1]()

                build_V(O_DV)
                wpre = load_qk(O_DQ, O_DK)
                for h in range(4):
                    build_qk3(*wpre)
                    wpre = load_qk(O_DQ + (h + 1) * 128, O_DK + (h + 1) * 128) if h + 1 < 4 else load_qk(O_FQ, O_FK)
                    G = strips[:, h * 256:(h + 1) * 256]
                    chc = prow[:, ro["ch"] + h:ro["ch"] + h + 1]

                    def fix_diff(ps, psk, m, G=G):
                        if m == -1:
                            TT("dve", ps[:, 0:128], ps[:, 0:128], G[:, 128:256], ALU.add, r=(psk, "strips"), w=(psk,))
                        elif m >= 0:
                            a_ = 128 * m
                            b_ = min(a_ + 256, 512)
                            TT("dve", ps[:, a_:b_], ps[:, a_:b_], G[:, 0:b_ - a_], ALU.add, r=(psk, "strips"), w=(psk,))

                    tasks = []
                    for j in range(NT):
                        for c in range(2):
                            def fin_d(po, pok, pz, pzk, j=j, c=c, h=h):
                                RECIP(rz, pz[:], r=(pzk,), w=("rz",))
                                TT("dve", oo[c], po[:], rz, ALU.mult, r=(pok, "rz"), w=("oo%d" % c,))
                                if c == 1:
                                    STT(oo[0], oo[1], dcol[:, dc0 + 8:dc0 + 9], oo[0], ALU.mult, ALU.add, r=("oo0", "oo1", "dcol"), w=("oo0",))
                                    ACT(sq_, oo[0], AF.Square, r=("oo0",), w=("sq_",))
                                    pn, pnk = bank("mm")
                                    MM(pn[:], ones32[:], sq_, True, True, r=("sq_", "ones32"), w=(pnk,))
                                    ACT(rs_, pn[:], AF.Sqrt, r=(pnk, "dcol"), w=("rs_",), bias=epsc, scale=1.0 / 128)
                                    RECIP(rs_, rs_, r=("rs_",), w=("rs_",))
                                    y_ = yt[j % 2]
                                    STT(y_[0], oo[0], dcol[:, dc0 + 9:dc0 + 10], rs_, ALU.mult, ALU.mult, r=("oo0", "dcol", "rs_"), w=(y_[1],))
                                    DMA(yscr[8 + h][:, j * 512:(j + 1) * 512], y_[0], r=(y_[1],), w=("yscr%d" % (8 + h),))
                            tasks += attn_tasks(qk[c], "qk%d" % c, qk[2], "qk2", j,
                                                lambda kb, h=h: Vt[:, kb, h * 128:(h + 1) * 128], True,
                                                lambda kb, chc=chc: (chc, ("prow",)), fix_diff, fin_d)
                    run_tasks(tasks)

                build_V(O_FV)
                wf, wfk = LW(wview(w_in[l], O_FL, 8), 8, 8)
                lf = aalloc(128)
                Dk = aalloc(128)
                pref = aalloc(128 + 8)
                ball = aalloc(NT * NB * 8)
                pf, pfk = bank("mm")
                for b in range(NB):
                    for k in range(KC):
                        MM(pf[:, b * 8:(b + 1) * 8], hT[:, k, b * 128:(b + 1) * 128], wf[:, k, 0:8], k == 0, k == KC - 1,
                           r=("hT%d_%d" % (k, b // 4), wfk), w=(pfk,))
                nb8 = NB * 8
                bf0 = ro["bf"] + l * 128
                TT("dve", lf[:, 0:nb8], pf[:, 0:nb8], prow[:, bf0:bf0 + nb8], ALU.add, r=(pfk, "prow"), w=("lf",))
                ACT(lf[:, 0:nb8], lf[:, 0:nb8], AF.Exp, r=("lf",), w=("lf",), scale=-1.0)
                ACT(lf[:, 0:nb8], lf[:, 0:nb8], AF.Ln, r=("lf", "dcol"), w=("lf",), bias=onec)
                pc_, pck = bank("sc")
                MM(pc_[:, 0:nb8], tri, lf[:, 0:nb8], True, True, r=("cst", "lf"), w=(pck,))
                ptot, ptk = bank("sc")
                MM(ptot[:, 0:nb8], ones32[:], lf[:, 0:nb8], True, True, r=("ones32", "lf"), w=(ptk,))
                MEMSET("dve", pref[:, 0:8], 0.0, w=("pref",))
                for b in range(1, NB + 1):
                    TT("dve", pref[:, b * 8:(b + 1) * 8], pref[:, (b - 1) * 8:b * 8], ptot[:, (b - 1) * 8:b * 8], ALU.add,
                       r=("pref", ptk), w=("pref",))
                TT("dve", Dk[:, 0:nb8], pc_[:, 0:nb8], pref[:, 0:nb8], ALU.add, r=(pck, "pref"), w=("Dk",))
                for j in range(NT):
                    rb = 4 * j + 2
                    for kb in range(4 * (j + 1)):
                        TT("dve", ball[:, (j * NB + kb) * 8:(j * NB + kb) * 8 + 8], Dk[:, kb * 8:kb * 8 + 8], pref[:, rb * 8:rb * 8 + 8],
                           ALU.subtract, r=("Dk", "pref"), w=("ball",))

                def fix_fox(ps, psk, m):
                    if m >= 0:
                        a_ = 128 * m
                        TT("dve", ps[:, a_:a_ + 128], ps[:, a_:a_ + 128], cmask, ALU.add, r=(psk, "cst"), w=(psk,))

                for pr_ in range(4):
                    build_qk3(*wpre)
                    if pr_ + 1 < 4:
                        wpre = load_qk(O_FQ + (pr_ + 1) * 128, O_FK + (pr_ + 1) * 128)
                    tasks = []
                    for j in range(NT):
                        for hx in range(2):
                            hd = 2 * pr_ + hx

                            def fin_f(po, pok, pz, pzk, j=j, hx=hx, pr_=pr_):
                                y_ = yt[j % 2]
                                lo, hi = hx * 64, hx * 64 + 64
                                RECIP(rz[lo:hi, :], pz[lo:hi, :], r=(pzk,), w=("rz",))
                                TT("dve", y_[0][lo:hi, :], po[lo:hi, :], rz[lo:hi, :], ALU.mult, r=(pok, "rz"), w=(y_[1],))
                                if hx == 1:
                                    DMA(yscr[12 + pr_][:, j * 512:(j + 1) * 512], y_[0], r=(y_[1],), w=("yscr%d" % (12 + pr_),))
                            tasks += attn_tasks(
                                qk[hx], "qk%d" % hx, qk[2], "qk2", j,
                                lambda kb, pr_=pr_: Vt[:, kb, pr_ * 128:(pr_ + 1) * 128], True,
                                lambda kb, hd=hd, j=j: (ball[:, (j * NB + kb) * 8 + hd:(j * NB + kb) * 8 + hd + 1], ("ball",)), fix_fox, fin_f)
                    run_tasks(tasks)

                areset()
                roles["mm"] = [0, 1, 2, 3, 4, 5, 6, 7]
                ymh = aalloc(16 * HT, BF16).rearrange("p (a b) -> p a b", a=16)
                mT = aalloc(8 * HT, BF16).rearrange("p (a b) -> p a b", a=8)
                gt_ = [(aalloc(512), "gt0"), (aalloc(512), "gt1")]
                acc = aalloc(512)
                tmpm = aalloc(512)
                broff = [0, 8, 12]
                brkc = [8, 4, 4]
                for hf in range(2):
                    for c in range(16):
                        DMA(ymh[:, c, :], yscr[c][:, hf * HT:(hf + 1) * HT], r=("yscr%d" % c,), w=("ymh%d" % c,))
                    for f in range(8):
                        wg_ = [LW(wview(w_in[l], O_GT + b * 1024 + f * 128, 128), 8, 128) for b in range(3)]
                        wb_ = [LW(wview(w_br[b][l], f * 128, 128), brkc[b], 128) for b in range(3)]
                        for tt in range(TPH):
                            t = hf * TPH + tt
                            for b in range(3):
                                g_ = gt_[b % 2]
                                gbc = pcol[:, co["gb"] + (l * 3 + b) * 8 + f:co["gb"] + (l * 3 + b) * 8 + f + 1]
                                hgb = hcol[:, HB_G + (l * 3 + b) * 8 + f:HB_G + (l * 3 + b) * 8 + f + 1]
                                proj_fm(lambda t_, p, pk: ACT(g_[0], p[:], AF.Tanh, r=(pk, "hcol"), w=(g_[1],), bias=hgb, scale=0.5),
                                        wg_[b][0], wg_[b][1], 8, hT_rhs, hT_keys, NT, tiles=[t])
                                dst = acc if b == 0 else tmpm
                                dkey = "acc" if b == 0 else "tmpm"
                                proj_fm(lambda t_, p, pk: STT(dst, g_[0], 1.0, p[:], ALU.add, ALU.mult, r=(pk, g_[1]), w=(dkey,)),
                                        wb_[b][0], wb_[b][1], brkc[b],
                                        lambda k, t_, b=b, tt=tt: ymh[:, broff[b] + k, tt * 512:(tt + 1) * 512],
                                        lambda k, t_, b=b: ("ymh%d" % (broff[b] + k),), NT, tiles=[t])
                                if b == 1:
                                    TT("dve", acc, acc, tmpm, ALU.add, r=("acc", "tmpm"), w=("acc",))
                                elif b == 2:
                                    TT("dve", mT[:, f, tt * 512:(tt + 1) * 512], acc, tmpm, ALU.add, r=("acc", "tmpm"), w=("mT%d" % f,))
                    for f in range(8):
                        wo, wok = LW(wview(w_out[l], f * 128, 128), 8, 128)
                        for tt in range(TPH):
                            t = hf * TPH + tt
                            xk = "xT%d_%d" % (f, t)
                            proj_fm(lambda t_, p, pk: STT(xT[:, f, t * 512:(t + 1) * 512], p[:], 0.5, xT[:, f, t * 512:(t + 1) * 512], ALU.mult, ALU.add,
                                                          r=(pk, xk), w=(xk,)),
                                    wo, wok, 8, lambda k, t_, tt=tt: mT[:, k, tt * 512:(tt + 1) * 512], lambda k, t_: ("mT%d" % k,), NT, tiles=[t])

                norm_to_hT(co["n2"] + l * 8)
                areset()
                zT = aalloc(NFC * HT, BF16).rearrange("p (a b) -> p a b", a=NFC)
                ur = [(aalloc(2 + HT), "urg"), (aalloc(2 + HT), "urv")]
                uc = [(aalloc(HT), "ucg"), (aalloc(HT), "ucv")]
                ftmp = aalloc(HT)
                for hf in range(2):
                    for c in range(NFC):
                        for gv in range(2):
                            col = gv * DFF + c * 128
                            wt, wk = LW(wview(ffn_up[l], col, 128), 8, 128)
                            u_, uk = ur[gv]
                            if hf == 0:
                                MEMSET("dve", u_[:, 0:2], 0.0, w=(uk,))
                            else:
                                p, pk = bank("mm")
                                t0 = hf * HT
                                for k in range(KC):
                                    MM(p[:, 0:2], wt[:, k, :], hT[:, k, t0 - 2:t0], k == 0, k == KC - 1,
                                       r=(wk, "hT%d_%d" % (k, (t0 - 2) // 512)), w=(pk,))
                                ACT(u_[:, 0:2], p[:, 0:2], AF.Copy, r=(pk,), w=(uk,))
                            fc = gv * NFC + c
                            w0 = co["fw"] + (l * 3) * 44 + fc
                            o_, ok = uc[gv]
                            w2c = pcol[:, w0 + 88:w0 + 89]
                            bbc = pcol[:, co["fb"] + l * 44 + fc:co["fb"] + l * 44 + fc + 1]

                            def ffn_evac(t, p, pk, u_=u_, uk=uk, o_=o_, ok=ok, w2c=w2c, bbc=bbc):
                                tl = t - hf * TPH
                                ACT(u_[:, 2 + tl * 512:2 + (tl + 1) * 512], p[:], AF.Copy, r=(pk,), w=(uk,))
                                ACT(o_[:, tl * 512:(tl + 1) * 512], p[:], AF.Identity, r=(pk, "pcol"), w=(ok,), bias=bbc, scale=w2c)
                            proj_fm(ffn_evac, wt, wk, 8, hT_rhs, hT_keys, NT, tiles=[hf * TPH + tt for tt in range(TPH)])
                            for j in range(2):
                                STT(o_, u_[:, j:j + HT], pcol[:, w0 + 44 * j:w0 + 44 * j + 1], o_, ALU.mult, ALU.add, r=(uk, "pcol", ok), w=(ok,))
                        gelu2_from(ftmp, uc[0][0], ("ucg",), ftmp, "ftmp", "ftmp")
                        STT(zT[:, c, :], ftmp, 0.5, uc[1][0], ALU.mult, ALU.mult, r=("ftmp", "ucv"), w=("zT%d" % c,))
                    for f in range(8):
                        wd = [LW(ffn_dn[l][k0 * 128:(k0 + n_) * 128, :].rearrange("(k p) n -> p k n", p=128)[:, :, f * 128:(f + 1) * 128], n_, 128)
                              for (k0, n_) in ((0, 8), (8, 8), (16, 6))]
                        for tt in range(TPH):
                            t = hf * TPH + tt
                            xk = "xT%d_%d" % (f, t)
                            p, pk = bank("mm")
                            for kk in range(NFC):
                                wt_, wk_ = wd[kk // 8]
                                MM(p[:], wt_[:, kk % 8, :], zT[:, kk, tt * 512:(tt + 1) * 512], kk == 0, kk == NFC - 1, r=(wk_, "zT%d" % kk), w=(pk,))
                            TT("dve", xT[:, f, t * 512:(t + 1) * 512], p[:], xT[:, f, t * 512:(t + 1) * 512], ALU.add, r=(pk, xk), w=(xk,))
                roles["mm"] = [0, 1]

            areset()
            sq2 = [(aalloc(512), "nsq0"), (aalloc(512), "nsq1")]
            ms = aalloc(S)
            of = aalloc(8 * 512).rearrange("p (a b) -> p a b", a=8)
            ot = [(aalloc(1024), "ot0"), (aalloc(1024), "ot1")]
            ocnt = 0
            rstd_all(sq2, ms)
            for t in range(NT):
                for f in range(KC):
                    STT(of[:, f, :], xT[:, f, t * 512:(t + 1) * 512], pcol[:, co["fin"] + f:co["fin"] + f + 1], ms[:, t * 512:(t + 1) * 512],
                        ALU.mult, ALU.mult, r=("xT%d_%d" % (f, t), "pcol", "ms"), w=("of%d" % f,))
                for blk in range(4):
                    o_, ok = ot[ocnt % 2]
                    ocnt += 1
                    for g in range(2):
                        p, pk = bank("mm")
                        for i in range(4):
                            f = g * 4 + i
                            TR(p[:, i * 128:(i + 1) * 128], of[:, f, blk * 128:(blk + 1) * 128], r=("of%d" % f,), w=(pk,))
                        if g == 0:
                            CP("dve", o_[:, 0:512], p[:], r=(pk,), w=(ok,))
                        else:
                            ACT(o_[:, 512:1024], p[:], AF.Copy, r=(pk,), w=(ok,))
                    r0 = t * 512 + blk * 128
                    DMA(out_d[s, r0:r0 + 128, :], o_, r=(ok,), w=("out",))
        S_.emit()
    return nc


_NC_CACHE = {}


def _make_in_maps(inputs, L, nseq, ncores, S):
    pc, pr, strips, cst = _host_params(inputs, L)
    shared = {
        "w_in": np.ascontiguousarray(inputs["w_in"][:L], np.float32),
        "rg_w_r": np.ascontiguousarray(inputs["rg_w_r"][:L], np.float32),
        "rg_w_i": np.ascontiguousarray(inputs["rg_w_i"][:L], np.float32),
        "w_br_rnn": np.ascontiguousarray(inputs["w_br_rnn"][:L], np.float32),
        "w_br_diff": np.ascontiguousarray(inputs["w_br_diff"][:L], np.float32),
        "w_br_fox": np.ascontiguousarray(inputs["w_br_fox"][:L], np.float32),
        "w_out": np.ascontiguousarray(inputs["w_out"][:L], np.float32),
        "ffn_up": np.ascontiguousarray(inputs["ffn_up"][:L], np.float32),
        "ffn_down": np.ascontiguousarray(inputs["ffn_down"][:L], np.float32),
        "pcols": pc, "prows": pr, "strips": strips, "consts": cst,
    }
    x = np.asarray(inputs["x"], np.float32)
    maps = []
    for c in range(ncores):
        m = dict(shared)
        m["x"] = np.ascontiguousarray(x[c * nseq:(c + 1) * nseq, :S])
        maps.append(m)
    return maps


def kernel(**inputs):
    inputs = {k: np.asarray(v) for k, v in inputs.items()}
    B, S, _ = inputs["x"].shape
    L = inputs["w_in"].shape[0]
    ncores = 8
    nseq = B // ncores
    key = (S, L, nseq)
    if key not in _NC_CACHE:
        _NC_CACHE[key] = build_nc(S=S, L=L, NSEQ=nseq)
    nc = _NC_CACHE[key]
    maps = _make_in_maps(inputs, L, nseq, ncores, S)
    res = run_bass_kernel_spmd(nc, maps, core_ids=list(range(ncores)))
    out = np.concatenate([np.asarray(r["out"]) for r in res.results], axis=0)
    return out.astype(np.float32)
```

```python
import contextlib
import math
import numpy as np
import concourse.bass as bass
import concourse.mybir as mybir
from concourse.bass_utils import run_bass_kernel_spmd

F32 = mybir.dt.float32
BF16 = mybir.dt.bfloat16
AF = mybir.ActivationFunctionType
ALU = mybir.AluOpType

D = 1024
KC = 8
NIN = 8200
DFF = 2816
NFC = 22
EPS = 1e-6
NEG = -30000.0
O_XR, O_GR, O_DQ, O_DK, O_DV, O_FQ, O_FK, O_FV, O_FL, O_GT = 0, 1024, 2048, 2560, 3072, 3584, 4096, 4608, 5120, 5128


class _Op:
    __slots__ = ("eng", "fn", "waits", "signal", "idx", "count", "dma", "dsem", "dcount", "clock")

    def __init__(self, eng, fn, dma):
        self.eng = eng; self.fn = fn; self.waits = []; self.signal = False; self.idx = -1
        self.count = 0; self.dma = dma; self.dsem = None; self.dcount = 0; self.clock = None


class Sched:
    ENG = ("pe", "act", "dve", "pool", "sp")
    NDSEM = 24

    def __init__(self, nc):
        self.nc = nc
        self.ops = {e: [] for e in self.ENG}
        self.lastw = {}
        self.readers = {}
        self.seen = {e: {f: -1 for f in self.ENG} for e in self.ENG}
        self.seen_d = {e: [0] * self.NDSEM for e in self.ENG}
        self.ndma = 0
        self.dma_ops = []

    def op(self, eng, fn, r=(), w=(), dma=False):
        o = _Op(eng, fn, dma)
        o.idx = len(self.ops[eng])
        r = tuple(r) + ("PHASE",)
        deps = []
        for k in r:
            x = self.lastw.get(k)
            if x is not None:
                deps.append(x)
        for k in w:
            x = self.lastw.get(k)
            if x is not None:
                deps.append(x)
            deps.extend(self.readers.get(k, ()))
        if dma:
            j = self.ndma
            self.ndma += 1
            o.dsem = j % self.NDSEM
            o.dcount = 16 * (j // self.NDSEM + 1)
            if j >= self.NDSEM:
                deps.append(self.dma_ops[j - self.NDSEM])
            self.dma_ops.append(o)
        seen = self.seen[eng]
        sd = self.seen_d[eng]
        best_e = {}
        best_d = {}
        for d in deps:
            if d is o:
                continue
            if d.dma:
                if d.dcount > sd[d.dsem] and (d.dsem not in best_d or d.dcount > best_d[d.dsem].dcount):
                    best_d[d.dsem] = d
            else:
                if d.eng == eng and eng == "pe":
                    continue
                if d.idx > seen[d.eng] and (d.eng not in best_e or d.idx > best_e[d.eng].idx):
                    best_e[d.eng] = d
        for d in best_e.values():
            if seen[d.eng] >= d.idx:
                continue
            d.signal = True
            o.waits.append(d)
            seen[d.eng] = d.idx
            if d.clock is not None:
                for f, v in d.clock.items():
                    if f != eng and v > seen[f]:
                        seen[f] = v
        for d in best_d.values():
            sd[d.dsem] = d.dcount
            o.waits.append(d)
        if not dma:
            o.clock = dict(seen)
            o.clock[eng] = o.idx
        for k in r:
            self.readers.setdefault(k, []).append(o)
        for k in w:
            self.lastw[k] = o
            self.readers[k] = []
        self.ops[eng].append(o)
        return o

    def emit(self):
        nc = self.nc
        with contextlib.ExitStack() as st:
            esem = {e: st.enter_context(nc.semaphore("s_" + e)) for e in self.ENG}
            dsem = [st.enter_context(nc.semaphore("d%d" % i)) for i in range(self.NDSEM)]
            for e in self.ENG:
                c = 0
                for o in self.ops[e]:
                    if o.signal and not o.dma:
                        c += 1
                        o.count = c
            last = {}
            for o in self.dma_ops:
                last[o.dsem] = o
            final = list(last.values())
            block = st.enter_context(nc.Block())

            def run(e, eng):
                for o in self.ops[e]:
                    for d in o.waits:
                        if d.dma:
                            eng.wait_ge(dsem[d.dsem], d.dcount)
                        else:
                            eng.wait_ge(esem[d.eng], d.count)
                    ins = o.fn(eng)
                    if o.dma:
                        ins.then_inc(dsem[o.dsem], 16)
                    elif o.signal:
                        ins.then_inc(esem[e], 1)
                if e == "sp":
                    for o in final:
                        eng.wait_ge(dsem[o.dsem], o.dcount)

            @block.tensor
            def _(eng):
                run("pe", eng)

            @block.scalar
            def _(eng):
                run("act", eng)

            @block.vector
            def _(eng):
                run("dve", eng)

            @block.gpsimd
            def _(eng):
                run("pool", eng)

            @block.sync
            def _(eng):
                run("sp", eng)


def _col_layout(L):
    off = {}
    n = 0
    for name, w in (("n1", L * 8), ("n2", L * 8), ("fin", 8), ("cw", L * 4 * 8), ("cb", L * 8), ("br", L * 8),
                    ("bi", L * 8), ("ra", L * 8), ("gb", L * 3 * 8), ("fw", L * 3 * 44), ("fb", L * 44), ("sg", L)):
        off[name] = n
        n += w
    return off, n


def _row_layout(L):
    off = {}
    n = 0
    for name, w in (("lq", L * 4 * 64), ("bf", L * 128), ("ch", 4)):
        off[name] = n
        n += w
    return off, n


def _t5_bucket_np(dist):
    n = np.maximum(dist, 0)
    nf = np.maximum(n, 1).astype(np.float32)
    large = 16 + (np.log(nf / np.float32(16)) / np.float32(math.log(128 / 16)) * np.float32(16)).astype(np.int32)
    large = np.minimum(large, 31)
    return np.where(n < 16, n, large)


def _host_params(inp, L):
    co, nc_ = _col_layout(L)
    ro, nr = _row_layout(L)
    pc = np.zeros((128, nc_), np.float32)

    def cols(v):
        v = np.asarray(v, np.float32)
        lead = int(np.prod(v.shape[:-1])) if v.ndim > 1 else 1
        k = v.shape[-1] // 128
        return v.reshape(lead, k, 128).transpose(2, 0, 1).reshape(128, lead * k)

    pc[:, co["n1"]:co["n1"] + L * 8] = cols(inp["norm1_g"][:L])
    pc[:, co["n2"]:co["n2"] + L * 8] = cols(inp["norm2_g"][:L])
    pc[:, co["fin"]:co["fin"] + 8] = cols(inp["final_g"])
    pc[:, co["cw"]:co["cw"] + L * 32] = cols(inp["rnn_conv_w"][:L])
    pc[:, co["cb"]:co["cb"] + L * 8] = cols(inp["rnn_conv_b"][:L])
    pc[:, co["br"]:co["br"] + L * 8] = cols(inp["rg_b_r"][:L])
    pc[:, co["bi"]:co["bi"] + L * 8] = cols(inp["rg_b_i"][:L])
    pc[:, co["ra"]:co["ra"] + L * 8] = cols(inp["rg_a"][:L])
    pc[:, co["gb"]:co["gb"] + L * 24] = cols(inp["gate_b"][:L])
    pc[:, co["fw"]:co["fw"] + L * 132] = cols(inp["ffn_conv_w"][:L])
    pc[:, co["fb"]:co["fb"] + L * 44] = cols(inp["ffn_conv_b"][:L])
    pc[:, co["sg"]:co["sg"] + L] = np.asarray(inp["diff_subln_g"][:L], np.float32).T
    pr = np.zeros((128, nr), np.float32)
    lq = np.stack([np.asarray(inp[k][:L], np.float32) for k in ("diff_lq1", "diff_lk1", "diff_lq2", "diff_lk2")], 1)
    pr[:, ro["lq"]:ro["lq"] + L * 256] = np.broadcast_to(lq.reshape(1, L * 256), (128, L * 256))
    bf = np.tile(np.asarray(inp["fox_b_f"][:L], np.float32).reshape(L, 1, 8), (1, 16, 1)).reshape(1, L * 128)
    pr[:, ro["bf"]:ro["bf"] + L * 128] = np.broadcast_to(bf, (128, L * 128))
    rel = np.asarray(inp["rel_bias"], np.float32)
    pr[:, ro["ch"]:ro["ch"] + 4] = np.broadcast_to(rel[31:32, :], (128, 4))
    kl = np.arange(128)[:, None]
    sx = np.arange(256)[None, :]
    dist = sx - kl
    bidx = _t5_bucket_np(dist)
    strips = np.zeros((128, 4 * 256), np.float32)
    for h in range(4):
        g = rel[bidx, h]
        strips[:, h * 256:(h + 1) * 256] = np.where(dist >= 0, g, np.float32(NEG))
    cst = np.zeros((128, 4 * 128), np.float32)
    cst[:, 0:128] = np.eye(128, dtype=np.float32)
    cst[:, 128:256] = (np.arange(128)[:, None] <= np.arange(128)[None, :]).astype(np.float32)
    cst[:, 256:384] = np.where(np.arange(128)[:, None] <= np.arange(128)[None, :], 0.0, NEG)
    cst[:, 384:512] = (np.arange(128)[:, None] == (np.arange(128)[None, :] + 64) % 128).astype(np.float32)
    return pc, pr, strips, cst


def build_nc(S=2048, L=2, NSEQ=2):
    assert S % 1024 == 0
    NT = S // 512
    NB = S // 128
    HT = S // 2
    TPH = NT // 2
    co, ncol = _col_layout(L)
    ro, nrow = _row_layout(L)

    nc = bass.Bass("TRN2", target_bir_lowering=False)

    def din(name, shape):
        return nc.dram_tensor(name, shape, F32, kind="ExternalInput").ap()

    x_d = din("x", [NSEQ, S, D])
    w_in = din("w_in", [L, D, NIN])
    rg_wr = din("rg_w_r", [L, 8, 128, 128])
    rg_wi = din("rg_w_i", [L, 8, 128, 128])
    w_br = [din("w_br_rnn", [L, D, D]), din("w_br_diff", [L, 512, D]), din("w_br_fox", [L, 512, D])]
    w_out = din("w_out", [L, D, D])
    ffn_up = din("ffn_up", [L, D, 2 * DFF])
    ffn_dn = din("ffn_down", [L, DFF, D])
    pcol_d = din("pcols", [128, ncol])
    prow_d = din("prows", [128, nrow])
    strip_d = din("strips", [128, 1024])
    cst_d = din("consts", [128, 512])
    out_d = nc.dram_tensor("out", [NSEQ, S, D], F32, kind="ExternalOutput").ap()
    yscr = nc.dram_tensor("yscr", [16, 128, S], BF16, kind="Internal").ap()

    with contextlib.ExitStack() as st:
        def sb(name, shape, dt=F32):
            return st.enter_context(nc.sbuf_tensor(name, shape, dt))

        S_ = Sched(nc)
        xT = sb("xT", [128, KC, S])
        hT = sb("hT", [128, KC, S], BF16)
        ARN = 17424
        arena = sb("arena", [128, ARN])
        NSTG, NWB = 3, 9
        stg = [sb("stg%d" % i, [128, 8, 128]) for i in range(NSTG)]
        wbs = [sb("wb%d" % i, [128, 8, 128], BF16) for i in range(NWB)]
        pcol = sb("pcol_sb", [128, ncol])
        prow = sb("prow_sb", [128, nrow])
        cst = sb("cst_sb", [128, 512])
        strips = sb("strips_sb", [128, 1024])
        dcol = sb("dcol", [128, 16 * L + 8])
        hcol = sb("hcol", [128, L * 48 + 8])
        ones32 = sb("ones32", [128, 128])
        onesb = sb("onesb", [128, 128], BF16)
        phz = sb("phz", [128, 1])
        banks = [st.enter_context(nc.psum_tensor("bank%d" % i, [128, 512], F32)) for i in range(8)]
        ident = cst[:, 0:128]
        tri = cst[:, 128:256]
        cmask = cst[:, 256:384]
        shm = cst[:, 384:512]

        def MM(out, lhsT, rhs, start, stop, r, w):
            S_.op("pe", lambda e: e.matmul(out, lhsT=lhsT, rhs=rhs, start=start, stop=stop), r=r, w=w)

        def TR(out, in_, r, w):
            S_.op("pe", lambda e: e.transpose(out=out, in_=in_, identity=ident), r=tuple(r) + ("cst",), w=w)

        def ACT(out, in_, func, r, w, bias=None, scale=None):
            kw = {}
            if bias is not None:
                kw["bias"] = bias
            if scale is not None:
                kw["scale"] = scale
            S_.op("act", lambda e: e.activation(out=out, in_=in_, func=func, **kw), r=r, w=w)

        def TT(eng, out, in0, in1, op, r, w):
            S_.op(eng, lambda e: e.tensor_tensor(out=out, in0=in0, in1=in1, op=op), r=r, w=w)

        def TS(eng, out, in0, s1, s2, op0, op1, r, w):
            if s2 is None:
                S_.op(eng, lambda e: e.tensor_scalar(out=out, in0=in0, scalar1=s1, scalar2=None, op0=op0), r=r, w=w)
            else:
                S_.op(eng, lambda e: e.tensor_scalar(out=out, in0=in0, scalar1=s1, scalar2=s2, op0=op0, op1=op1), r=r, w=w)

        def STT(out, in0, scalar, in1, op0, op1, r, w):
            S_.op("dve", lambda e: e.scalar_tensor_tensor(out=out, in0=in0, scalar=scalar, in1=in1, op0=op0, op1=op1), r=r, w=w)

        def CP(eng, out, in_, r, w):
            S_.op(eng, lambda e: e.tensor_copy(out=out, in_=in_), r=r, w=w)

        def RECIP(out, in_, r, w):
            S_.op("dve", lambda e: e.reciprocal(out=out, in_=in_), r=r, w=w)

        def MEMSET(eng, ap, val, w):
            S_.op(eng, lambda e: e.memset(ap, val), w=w)

        def DMA(out, in_, r, w):
            S_.op("sp", lambda e: e.dma_start(out=out, in_=in_), r=r, w=w, dma=True)

        def barrier():
            S_.op("dve", lambda e: e.memset(phz[:], 0.0), w=("PHASE", "phz"))

        apos = [0]

        def areset():
            barrier()
            apos[0] = 0

        def aalloc(nelem, dt=F32):
            n32 = nelem if dt == F32 else (nelem + 1) // 2
            n32 = (n32 + 7) // 8 * 8
            a0 = apos[0]
            apos[0] += n32
            assert apos[0] <= ARN, ("arena overflow", apos[0])
            v = arena[:, a0:a0 + n32]
            if dt != F32:
                v = v.bitcast(dt)[:, 0:nelem]
            else:
                v = v[:, 0:nelem]
            return v

        roles = {"mm": [0, 1], "sc": [2, 3], "acc": [4, 5, 6, 7], "pj": [0, 4, 5, 6, 7]}
        rpos = {"mm": 0, "sc": 0, "acc": 0, "pj": 0}

        def bank(role):
            lst = roles[role]
            i = lst[rpos[role] % len(lst)]
            rpos[role] += 1
            return banks[i], "bank%d" % i

        wcnt = [0, 0]

        def LW(src, kc, ncols, dst=None, dkey=None):
            i = wcnt[0] % NSTG
            wcnt[0] += 1
            DMA(stg[i][:, 0:kc, 0:ncols], src, r=(), w=("stg%d" % i,))
            if dst is None:
                j = wcnt[1] % NWB
                wcnt[1] += 1
                dst = wbs[j][:, 0:kc, 0:ncols]
                dkey = "wb%d" % j
                ret = wbs[j]
            else:
                ret = None
            if wcnt[0] % 3 == 0:
                ACT(dst, stg[i][:, 0:kc, 0:ncols], AF.Copy, r=("stg%d" % i,), w=(dkey,))
            else:
                CP("pool", dst, stg[i][:, 0:kc, 0:ncols], r=("stg%d" % i,), w=(dkey,))
            return ret, dkey

        def wview(ap2d, c0, ncols):
            return ap2d.rearrange("(k p) n -> p k n", p=128)[:, :, c0:c0 + ncols]

        DMA(pcol[:], pcol_d, r=(), w=("pcol",))
        DMA(prow[:], prow_d, r=(), w=("prow",))
        DMA(cst[:], cst_d, r=(), w=("cst",))
        DMA(strips[:], strip_d, r=(), w=("strips",))
        MEMSET("dve", ones32[:], 1.0, w=("ones32",))
        MEMSET("dve", onesb[:], 1.0, w=("onesb",))
        DC_EPS, DC_ONE = 16 * L, 16 * L + 1
        MEMSET("dve", dcol[:, DC_EPS:DC_EPS + 1], EPS, w=("dcol",))
        MEMSET("dve", dcol[:, DC_ONE:DC_ONE + 1], 1.0, w=("dcol",))
        epsc = dcol[:, DC_EPS:DC_EPS + 1]
        onec = dcol[:, DC_ONE:DC_ONE + 1]
        for h in range(4):
            TS("dve", strips[:, h * 256:(h + 1) * 256], strips[:, h * 256:(h + 1) * 256],
               prow[:, ro["ch"] + h:ro["ch"] + h + 1], None, ALU.subtract, None, r=("strips", "prow"), w=("strips",))
        HB_R, HB_I, HB_G, HB_A, HB_C = 0, L * 8, L * 16, L * 40, L * 48
        TS("dve", hcol[:, HB_R:HB_R + L * 16], pcol[:, co["br"]:co["br"] + L * 16], 0.5, None, ALU.mult, None, r=("pcol",), w=("hcol",))
        TS("dve", hcol[:, HB_G:HB_G + L * 24], pcol[:, co["gb"]:co["gb"] + L * 24], 0.5, None, ALU.mult, None, r=("pcol",), w=("hcol",))
        MEMSET("dve", hcol[:, HB_C:HB_C + 1], 1.0 / 16, w=("hcol",))
        sixteenth = hcol[:, HB_C:HB_C + 1]
        lam_init = [0.8 - 0.6 * math.exp(-0.3 * l) for l in range(L)]
        for l in range(L):
            b0 = l * 16
            ACT(dcol[:, b0:b0 + 8], pcol[:, co["ra"] + l * 8:co["ra"] + l * 8 + 8], AF.Exp, r=("pcol",), w=("dcol",), scale=-1.0)
            ACT(dcol[:, b0:b0 + 8], dcol[:, b0:b0 + 8], AF.Ln, r=("dcol",), w=("dcol",), bias=onec)
            TS("dve", dcol[:, b0:b0 + 8], dcol[:, b0:b0 + 8], -8.0, None, ALU.mult, None, r=("dcol",), w=("dcol",))
            TS("dve", hcol[:, HB_A + l * 8:HB_A + l * 8 + 8], dcol[:, b0:b0 + 8], 0.5, None, ALU.mult, None, r=("dcol",), w=("hcol",))
            q0 = ro["lq"] + l * 256
            for pair in range(2):
                tmp = arena[:, 0:64]
                TT("dve", tmp, prow[:, q0 + pair * 128:q0 + pair * 128 + 64], prow[:, q0 + pair * 128 + 64:q0 + pair * 128 + 128],
                   ALU.mult, r=("prow",), w=("lamtmp",))
                S_.op("dve", lambda e, o_=dcol[:, b0 + 10 + pair:b0 + 11 + pair], i_=tmp: e.tensor_reduce(
                    out=o_, in_=i_, axis=mybir.AxisListType.X, op=ALU.add), r=("lamtmp",), w=("dcol",))
            ACT(dcol[:, b0 + 10:b0 + 12], dcol[:, b0 + 10:b0 + 12], AF.Exp, r=("dcol",), w=("dcol",))
            TT("dve", dcol[:, b0 + 8:b0 + 9], dcol[:, b0 + 11:b0 + 12], dcol[:, b0 + 10:b0 + 11], ALU.subtract, r=("dcol",), w=("dcol",))
            TS("dve", dcol[:, b0 + 8:b0 + 9], dcol[:, b0 + 8:b0 + 9], -lam_init[l], None, ALU.add, None, r=("dcol",), w=("dcol",))
            TS("dve", dcol[:, b0 + 9:b0 + 10], pcol[:, co["sg"] + l:co["sg"] + l + 1], 1.0 - lam_init[l], None, ALU.mult, None,
               r=("pcol",), w=("dcol",))

        def rstd_all(sq2, ms):
            for t in range(NT):
                pn, pk = bank("mm")
                for f in range(KC):
                    sq = sq2[f % 2]
                    ACT(sq[0], xT[:, f, t * 512:(t + 1) * 512], AF.Square, r=("xT%d_%d" % (f, t),), w=(sq[1],))
                    MM(pn[:], ones32[:], sq[0], f == 0, f == KC - 1, r=(sq[1], "ones32"), w=(pk,))
                ACT(ms[:, t * 512:(t + 1) * 512], pn[:], AF.Copy, r=(pk,), w=("ms",), scale=1.0 / D)
            ACT(ms, ms, AF.Sqrt, r=("ms", "dcol"), w=("ms",), bias=epsc)
            RECIP(ms, ms, r=("ms",), w=("ms",))

        def norm_to_hT(gbase):
            areset()
            sq2 = [(aalloc(512), "nsq0"), (aalloc(512), "nsq1")]
            ms = aalloc(S)
            rstd_all(sq2, ms)
            for t in range(NT):
                for f in range(KC):
                    STT(hT[:, f, t * 512:(t + 1) * 512], xT[:, f, t * 512:(t + 1) * 512], pcol[:, gbase + f:gbase + f + 1],
                        ms[:, t * 512:(t + 1) * 512], ALU.mult, ALU.mult, r=("xT%d_%d" % (f, t), "pcol", "ms"), w=("hT%d_%d" % (f, t),))

        def proj_fm(dst_fn, wt, wkey, kc, rhs_fn, rkeys_fn, ntiles, M=128, tiles=None, role="mm"):
            for t in (tiles if tiles is not None else range(ntiles)):
                p, pk = bank(role)
                for k in range(kc):
                    MM(p[0:M, :], wt[:, k, 0:M], rhs_fn(k, t), k == 0, k == kc - 1, r=(wkey,) + tuple(rkeys_fn(k, t)), w=(pk,))
                dst_fn(t, p, pk)

        def hT_rhs(k, t):
            return hT[:, k, t * 512:(t + 1) * 512]

        def hT_keys(k, t):
            return ("hT%d_%d" % (k, t),)

        def gelu2_from(dst, src, srckeys, tmp, tmpkey, dkey):
            ACT(tmp, src, AF.Square, r=srckeys, w=(tmpkey,), scale=0.21145921595661512)
            STT(tmp, tmp, 1.0, src, ALU.add, ALU.mult, r=(tmpkey,) + tuple(srckeys), w=(tmpkey,))
            ACT(tmp, tmp, AF.Tanh, r=(tmpkey,), w=(tmpkey,), scale=0.7978845608028654)
            STT(dst, tmp, 1.0, src, ALU.add, ALU.mult, r=(tmpkey,) + tuple(srckeys), w=(dkey,))

        for s in range(NSEQ):
            areset()
            xin = [(aalloc(1024), "xin0"), (aalloc(1024), "xin1")]
            for b in range(NB):
                xi = xin[b % 2]
                DMA(xi[0], x_d[s, b * 128:(b + 1) * 128, :], r=(), w=(xi[1],))
                for g in range(2):
                    p, pk = bank("mm")
                    for i in range(4):
                        f = g * 4 + i
                        TR(p[:, i * 128:(i + 1) * 128], xi[0][:, f * 128:(f + 1) * 128], r=(xi[1],), w=(pk,))
                    t = b // 4
                    wk = tuple("xT%d_%d" % (g * 4 + i, t) for i in range(4))
                    dstx = xT[:, g * 4:g * 4 + 4, b * 128:(b + 1) * 128]
                    srcx = p[:].rearrange("p (a b) -> p a b", a=4)
                    if g == 0:
                        CP("dve", dstx, srcx, r=(pk,), w=wk)
                    else:
                        ACT(dstx, srcx, AF.Copy, r=(pk,), w=wk)

            for l in range(L):
                dc0 = l * 16
                norm_to_hT(co["n1"] + l * 8)

                areset()
                roles["mm"] = [0, 1, 2, 3]
                xr = aalloc(3 + S)
                gg = [aalloc(S), aalloc(S)]
                xc = aalloc(S)
                xcb = aalloc(S, BF16)
                rr = aalloc(S)
                ii = aalloc(S)
                a2 = aalloc(S)
                hh = a2
                gtmp = [(aalloc(512), "gtmp0"), (aalloc(512), "gtmp1")]
                ybuf = aalloc(S, BF16)
                carry = aalloc(8)
                MEMSET("dve", xr[:, 0:3], 0.0, w=("xrh",))

                def sl(t):
                    return slice(t * 512, (t + 1) * 512)

                def rnn_W(n):
                    wx_ = LW(wview(w_in[l], O_XR + n * 128, 128), 8, 128)
                    wg_ = LW(wview(w_in[l], O_GR + n * 128, 128), 8, 128)
                    wr_ = LW(rg_wr[l, n].rearrange("(k p) n -> p k n", p=128), 1, 128)
                    wi_ = LW(rg_wi[l, n].rearrange("(k p) n -> p k n", p=128), 1, 128)
                    return wx_, wr_, wi_, wg_

                def rnn_Pg(n, wg_, t):
                    g_ = gg[n % 2]

                    def ev(t_, p, pk):
                        gt = gtmp[t_ % 2]
                        gelu2_from(g_[:, sl(t_)], p[:], (pk,), gt[0], gt[1], "gg%d_%d" % (n % 2, t_))
                    proj_fm(ev, wg_[0], wg_[1], 8, hT_rhs, hT_keys, NT, tiles=[t])

                def rnn_Px(n, wx_, t):
                    wx, wxk = wx_
                    cw = co["cw"] + (l * 4) * 8 + n
                    w3c = pcol[:, cw + 24:cw + 25]
                    cbc = pcol[:, co["cb"] + l * 8 + n:co["cb"] + l * 8 + n + 1]

                    def ev(t_, p, pk):
                        ACT(xr[:, 3 + t_ * 512:3 + (t_ + 1) * 512], p[:], AF.Copy, r=(pk,), w=("xr%d" % t_,))
                        ACT(xc[:, sl(t_)], p[:], AF.Identity, r=(pk, "pcol"), w=("xc%d" % t_,), bias=cbc, scale=w3c)
                    proj_fm(ev, wx, wxk, 8, hT_rhs, hT_keys, NT, tiles=[t])

                def rnn_R1(n, wr_, wi_, t, after_conv):
                    wr, wrk = wr_
                    wi, wik = wi_
                    cw = co["cw"] + (l * 4) * 8 + n
                    xk = ("xr%d" % t, "xr%d" % (t - 1) if t > 0 else "xrh", "pcol", "xc%d" % t)
                    for j in range(3):
                        STT(xc[:, sl(t)], xr[:, j + t * 512:j + (t + 1) * 512], pcol[:, cw + 8 * j:cw + 8 * j + 1], xc[:, sl(t)],
                            ALU.mult, ALU.add, r=xk, w=("xc%d" % t,))
                    ACT(xcb[:, sl(t)], xc[:, sl(t)], AF.Copy, r=("xc%d" % t,), w=("xcb%d" % t,))
                    after_conv()
                    hbr = hcol[:, HB_R + l * 8 + n:HB_R + l * 8 + n + 1]
                    hbi = hcol[:, HB_I + l * 8 + n:HB_I + l * 8 + n + 1]
                    hsa = hcol[:, HB_A + l * 8 + n:HB_A + l * 8 + n + 1]
                    proj_fm(lambda t_, p, pk: ACT(rr[:, sl(t)], p[:], AF.Tanh, r=(pk, "hcol"), w=("rr%d" % t,), bias=hbr, scale=0.5),
                            wr, wrk, 1, lambda k, t_: xcb[:, sl(t)], lambda k, t_: ("xcb%d" % t,), NT, tiles=[t])
                    proj_fm(lambda t_, p, pk: ACT(ii[:, sl(t)], p[:], AF.Tanh, r=(pk, "hcol"), w=("ii%d" % t,), bias=hbi, scale=0.5),
                            wi, wik, 1, lambda k, t_: xcb[:, sl(t)], lambda k, t_: ("xcb%d" % t,), NT, tiles=[t])
                    ACT(rr[:, sl(t)], rr[:, sl(t)], AF.Exp, r=("rr%d" % t, "hcol"), w=("rr%d" % t,), bias=hsa, scale=hsa)
                    TT("pool", a2[:, sl(t)], rr[:, sl(t)], rr[:, sl(t)], ALU.mult, r=("rr%d" % t,), w=("a2_%d" % t,))
                    STT(ii[:, sl(t)], ii[:, sl(t)], 1.0, xc[:, sl(t)], ALU.add, ALU.mult, r=("ii%d" % t, "xc%d" % t), w=("ii%d" % t,))

                def rnn_R2(n, t):
                    g_ = gg[n % 2]
                    TT("pool", ii[:, sl(t)], ii[:, sl(t)], a2[:, sl(t)], ALU.mult, r=("ii%d" % t, "a2_%d" % t), w=("ii%d" % t,))
                    init = 0.0 if t == 0 else carry[:, t - 1:t]
                    rk = ("rr%d" % t, "ii%d" % t, "a2_%d" % t) + (("carry%d" % (t - 1),) if t > 0 else ())
                    S_.op("dve", lambda e, o_=hh[:, sl(t)], d0=rr[:, sl(t)], d1=ii[:, sl(t)], init=init: e.tensor_tensor_scan(
                        out=o_, data0=d0, data1=d1, initial=init, op0=ALU.mult, op1=ALU.add), r=rk, w=("a2_%d" % t,))
                    if t + 1 < NT:
                        CP("pool", carry[:, t:t + 1], hh[:, (t + 1) * 512 - 1:(t + 1) * 512], r=("a2_%d" % t,), w=("carry%d" % t,))
                    TT("pool", ybuf[:, sl(t)], g_[:, sl(t)], hh[:, sl(t)], ALU.mult, r=("gg%d_%d" % (n % 2, t), "a2_%d" % t), w=("ybuf%d" % t,))

                def rnn_sqrt(n):
                    ak = tuple("a2_%d" % t for t in range(NT))
                    ACT(a2, a2, AF.Sqrt, r=ak + ("hcol",), w=ak, bias=sixteenth, scale=-1.0 / 16)

                def rnn_out(n):
                    DMA(yscr[n], ybuf, r=tuple("ybuf%d" % t for t in range(NT)), w=("yscr%d" % n,))

                W = {0: rnn_W(0)}
                for t in range(NT):
                    rnn_Pg(0, W[0][3], t)
                    rnn_Px(0, W[0][0], t)
                for n in range(8):
                    if n + 1 < 8:
                        W[n + 1] = rnn_W(n + 1)
                    for t in range(NT):
                        if n > 0:
                            rnn_R2(n - 1, t)
                        if n + 1 < 8:
                            rnn_Pg(n + 1, W[n + 1][3], t)
                        if n + 1 < 8 and t > 0:
                            rnn_R1(n, W[n][1], W[n][2], t, lambda t=t: rnn_Px(n + 1, W[n + 1][0], t - 1))
                        else:
                            rnn_R1(n, W[n][1], W[n][2], t, lambda: None)
                    if n > 0:
                        rnn_out(n - 1)
                    if n + 1 < 8:
                        rnn_Px(n + 1, W[n + 1][0], NT - 1)
                    rnn_sqrt(n)
                for t in range(NT):
                    rnn_R2(7, t)
                rnn_out(7)

                areset()
                roles["mm"] = [0]
                roles["sc"] = [1, 2, 3]
                Vt = aalloc(NB * 576, BF16).rearrange("p (a b) -> p a b", a=NB)
                MEMSET("pool", Vt[:, :, 512:576], 1.0, w=("Vt",))
                Wv = aalloc(8 * 512, BF16).rearrange("p (a b) -> p a b", a=8)
                qk = [aalloc(S, BF16) for _ in range(3)]
                Et = [(aalloc(512, BF16), "E%d" % i) for i in range(4)]
                rz = aalloc(512)
                oo = [aalloc(512), aalloc(512)]
                sq_ = aalloc(512)
                rs_ = aalloc(512)
                yt = [(aalloc(512, BF16), "yt%d" % i) for i in range(2)]
                ecnt = [0]

                def build_V(colbase):
                    for i in range(4):
                        LW(wview(w_in[l], colbase + i * 128, 128), 8, 128, dst=Wv[:, :, i * 128:(i + 1) * 128], dkey="Wv")
                    for b in range(NB):
                        p, pk = bank("pj")
                        for k in range(KC):
                            MM(p[:], hT[:, k, b * 128:(b + 1) * 128], Wv[:, k, :], k == 0, k == KC - 1,
                               r=("hT%d_%d" % (k, b // 4), "Wv"), w=(pk,))
                        if b % 2 == 0:
                            CP("dve", Vt[:, b, 0:512], p[:], r=(pk,), w=("Vt",))
                        else:
                            ACT(Vt[:, b, 0:512], p[:], AF.Copy, r=(pk,), w=("Vt",))

                MEMSET("dve", qk[0][64:128, :], 0.0, w=("qk0",))
                MEMSET("dve", qk[1][0:64, :], 0.0, w=("qk1",))

                def load_qk(colq, colk):
                    return LW(wview(w_in[l], colq, 128), 8, 128), LW(wview(w_in[l], colk, 128), 8, 128)

                def build_qk3(wq_, wk_):
                    def evq(t, p, pk):
                        ACT(qk[0][0:64, t * 512:(t + 1) * 512], p[0:64, :], AF.Copy, r=(pk,), w=("qk0",), scale=0.125)
                        ACT(qk[1][64:128, t * 512:(t + 1) * 512], p[64:128, :], AF.Copy, r=(pk,), w=("qk1",), scale=0.125)
                    proj_fm(evq, wq_[0], wq_[1], 8, hT_rhs, hT_keys, NT, role="pj")

                    def evk(t, p, pk):
                        CP("dve", qk[2][:, t * 512:(t + 1) * 512], p[:], r=(pk,), w=("qk2",))
                    proj_fm(evk, wk_[0], wk_[1], 8, hT_rhs, hT_keys, NT, role="pj")

                NE = len(Et)

                def attn_tasks(qT, qkey, kT, kkey, j, v_fn, use_pz, bias_fn, fix_fn, fin_fn):
                    nk = 4 * (j + 1)
                    st_ = {}
                    tasks = []
                    for kb in range(nk):
                        def A(kb=kb):
                            m = kb - 4 * j
                            c0 = max(0, 128 * m)
                            ps, psk = bank("sc")
                            MM(ps[:, c0:512], kT[:, kb * 128:(kb + 1) * 128], qT[:, j * 512 + c0:(j + 1) * 512], True, True,
                               r=(qkey, kkey), w=(psk,))
                            fix_fn(ps, psk, m)
                            E = Et[ecnt[0] % NE]
                            ecnt[0] += 1
                            bcol, bkeys = bias_fn(kb)
                            ACT(E[0][:, c0:512], ps[:, c0:512], AF.Exp, r=(psk,) + tuple(bkeys), w=(E[1],), bias=bcol)
                            st_[kb] = (E, c0)

                        def B(kb=kb):
                            if kb == 0:
                                st_["po"] = bank("acc")
                                st_["pz"] = bank("acc") if use_pz else (None, None)
                            po, pok = st_["po"]
                            pz, pzk = st_["pz"]
                            E, c0 = st_.pop(kb)
                            MM(po[:, c0:512], v_fn(kb), E[0][:, c0:512], kb == 0, kb == nk - 1, r=("Vt", E[1]), w=(pok,))
                            if use_pz:
                                MM(pz[:, c0:512], onesb[:], E[0][:, c0:512], kb == 0, kb == nk - 1, r=("onesb", E[1]), w=(pzk,))
                            if kb == nk - 1:
                                fin_fn(po, pok, pz, pzk)
                        tasks.append((A, B))
                    return tasks

                def run_tasks(tasks, LA=3):
                    n = len(tasks)
                    for i in range(n + LA):
                        if i < n:
                            tasks[i][0]()
                        if i - LA >= 0:
                            tasks[i - LA][1]()

                build_V(O_DV)
                wpre = load_qk(O_DQ, O_DK)
                for h in range(4):
                    build_qk3(*wpre)
                    wpre = load_qk(O_DQ + (h + 1) * 128, O_DK + (h + 1) * 128) if h + 1 < 4 else load_qk(O_FQ, O_FK)
                    G = strips[:, h * 256:(h + 1) * 256]
                    chc = prow[:, ro["ch"] + h:ro["ch"] + h + 1]

                    def fix_diff(ps, psk, m, G=G):
                        if m == -1:
                            TT("dve", ps[:, 0:128], ps[:, 0:128], G[:, 128:256], ALU.add, r=(psk, "strips"), w=(psk,))
                        elif m >= 0:
                            a_ = 128 * m
                            b_ = min(a_ + 256, 512)
                            TT("dve", ps[:, a_:b_], ps[:, a_:b_], G[:, 0:b_ - a_], ALU.add, r=(psk, "strips"), w=(psk,))

                    tasks = []
                    for j in range(NT):
                        for c in range(2):
                            def fin_d(po, pok, pz, pzk, j=j, c=c, h=h):
                                RECIP(rz, pz[:], r=(pzk,), w=("rz",))
                                TT("dve", oo[c], po[:], rz, ALU.mult, r=(pok, "rz"), w=("oo%d" % c,))
                                if c == 1:
                                    STT(oo[0], oo[1], dcol[:, dc0 + 8:dc0 + 9], oo[0], ALU.mult, ALU.add, r=("oo0", "oo1", "dcol"), w=("oo0",))
                                    ACT(sq_, oo[0], AF.Square, r=("oo0",), w=("sq_",))
                                    pn, pnk = bank("mm")
                                    MM(pn[:], ones32[:], sq_, True, True, r=("sq_", "ones32"), w=(pnk,))
                                    ACT(rs_, pn[:], AF.Sqrt, r=(pnk, "dcol"), w=("rs_",), bias=epsc, scale=1.0 / 128)
                                    RECIP(rs_, rs_, r=("rs_",), w=("rs_",))
                                    y_ = yt[j % 2]
                                    STT(y_[0], oo[0], dcol[:, dc0 + 9:dc0 + 10], rs_, ALU.mult, ALU.mult, r=("oo0", "dcol", "rs_"), w=(y_[1],))
                                    DMA(yscr[8 + h][:, j * 512:(j + 1) * 512], y_[0], r=(y_[1],), w=("yscr%d" % (8 + h),))
                            tasks += attn_tasks(qk[c], "qk%d" % c, qk[2], "qk2", j,
                                                lambda kb, h=h: Vt[:, kb, h * 128:(h + 1) * 128], True,
                                                lambda kb, chc=chc: (chc, ("prow",)), fix_diff, fin_d)
                    run_tasks(tasks)

                build_V(O_FV)
                wf, wfk = LW(wview(w_in[l], O_FL, 8), 8, 8)
                lf = aalloc(128)
                Dk = aalloc(128)
                pref = aalloc(128 + 8)
                ball = aalloc(NT * NB * 8)
                pf, pfk = bank("mm")
                for b in range(NB):
                    for k in range(KC):
                        MM(pf[:, b * 8:(b + 1) * 8], hT[:, k, b * 128:(b + 1) * 128], wf[:, k, 0:8], k == 0, k == KC - 1,
                           r=("hT%d_%d" % (k, b // 4), wfk), w=(pfk,))
                nb8 = NB * 8
                bf0 = ro["bf"] + l * 128
                TT("dve", lf[:, 0:nb8], pf[:, 0:nb8], prow[:, bf0:bf0 + nb8], ALU.add, r=(pfk, "prow"), w=("lf",))
                ACT(lf[:, 0:nb8], lf[:, 0:nb8], AF.Exp, r=("lf",), w=("lf",), scale=-1.0)
                ACT(lf[:, 0:nb8], lf[:, 0:nb8], AF.Ln, r=("lf", "dcol"), w=("lf",), bias=onec)
                pc_, pck = bank("sc")
                MM(pc_[:, 0:nb8], tri, lf[:, 0:nb8], True, True, r=("cst", "lf"), w=(pck,))
                ptot, ptk = bank("sc")
                MM(ptot[:, 0:nb8], ones32[:], lf[:, 0:nb8], True, True, r=("ones32", "lf"), w=(ptk,))
                MEMSET("dve", pref[:, 0:8], 0.0, w=("pref",))
                for b in range(1, NB + 1):
                    TT("dve", pref[:, b * 8:(b + 1) * 8], pref[:, (b - 1) * 8:b * 8], ptot[:, (b - 1) * 8:b * 8], ALU.add,
                       r=("pref", ptk), w=("pref",))
                TT("dve", Dk[:, 0:nb8], pc_[:, 0:nb8], pref[:, 0:nb8], ALU.add, r=(pck, "pref"), w=("Dk",))
                for j in range(NT):
                    rb = 4 * j + 2
                    for kb in range(4 * (j + 1)):
                        TT("dve", ball[:, (j * NB + kb) * 8:(j * NB + kb) * 8 + 8], Dk[:, kb * 8:kb * 8 + 8], pref[:, rb * 8:rb * 8 + 8],
                           ALU.subtract, r=("Dk", "pref"), w=("ball",))

                def fix_fox(ps, psk, m):
                    if m >= 0:
                        a_ = 128 * m
                        TT("dve", ps[:, a_:a_ + 128], ps[:, a_:a_ + 128], cmask, ALU.add, r=(psk, "cst"), w=(psk,))

                for pr_ in range(4):
                    build_qk3(*wpre)
                    if pr_ + 1 < 4:
                        wpre = load_qk(O_FQ + (pr_ + 1) * 128, O_FK + (pr_ + 1) * 128)
                    tasks = []
                    for j in range(NT):
                        for hx in range(2):
                            hd = 2 * pr_ + hx

                            def fin_f(po, pok, pz, pzk, j=j, hx=hx, pr_=pr_):
                                y_ = yt[j % 2]
                                lo, hi = hx * 64, hx * 64 + 64
                                RECIP(rz[lo:hi, :], pz[lo:hi, :], r=(pzk,), w=("rz",))
                                TT("dve", y_[0][lo:hi, :], po[lo:hi, :], rz[lo:hi, :], ALU.mult, r=(pok, "rz"), w=(y_[1],))
                                if hx == 1:
                                    DMA(yscr[12 + pr_][:, j * 512:(j + 1) * 512], y_[0], r=(y_[1],), w=("yscr%d" % (12 + pr_),))
                            tasks += attn_tasks(
                                qk[hx], "qk%d" % hx, qk[2], "qk2", j,
                                lambda kb, pr_=pr_: Vt[:, kb, pr_ * 128:(pr_ + 1) * 128], True,
                                lambda kb, hd=hd, j=j: (ball[:, (j * NB + kb) * 8 + hd:(j * NB + kb) * 8 + hd + 1], ("ball",)), fix_fox, fin_f)
                    run_tasks(tasks)

                areset()
                roles["mm"] = [0, 1, 2, 3, 4, 5, 6, 7]
                ymh = aalloc(16 * HT, BF16).rearrange("p (a b) -> p a b", a=16)
                mT = aalloc(8 * HT, BF16).rearrange("p (a b) -> p a b", a=8)
                gt_ = [(aalloc(512), "gt0"), (aalloc(512), "gt1")]
                acc = aalloc(512)
                tmpm = aalloc(512)
                broff = [0, 8, 12]
                brkc = [8, 4, 4]
                for hf in range(2):
                    for c in range(16):
                        DMA(ymh[:, c, :], yscr[c][:, hf * HT:(hf + 1) * HT], r=("yscr%d" % c,), w=("ymh%d" % c,))
                    for f in range(8):
                        wg_ = [LW(wview(w_in[l], O_GT + b * 1024 + f * 128, 128), 8, 128) for b in range(3)]
                        wb_ = [LW(wview(w_br[b][l], f * 128, 128), brkc[b], 128) for b in range(3)]
                        for tt in range(TPH):
                            t = hf * TPH + tt
                            for b in range(3):
                                g_ = gt_[b % 2]
                                gbc = pcol[:, co["gb"] + (l * 3 + b) * 8 + f:co["gb"] + (l * 3 + b) * 8 + f + 1]
                                hgb = hcol[:, HB_G + (l * 3 + b) * 8 + f:HB_G + (l * 3 + b) * 8 + f + 1]
                                proj_fm(lambda t_, p, pk: ACT(g_[0], p[:], AF.Tanh, r=(pk, "hcol"), w=(g_[1],), bias=hgb, scale=0.5),
                                        wg_[b][0], wg_[b][1], 8, hT_rhs, hT_keys, NT, tiles=[t])
                                dst = acc if b == 0 else tmpm
                                dkey = "acc" if b == 0 else "tmpm"
                                proj_fm(lambda t_, p, pk: STT(dst, g_[0], 1.0, p[:], ALU.add, ALU.mult, r=(pk, g_[1]), w=(dkey,)),
                                        wb_[b][0], wb_[b][1], brkc[b],
                                        lambda k, t_, b=b, tt=tt: ymh[:, broff[b] + k, tt * 512:(tt + 1) * 512],
                                        lambda k, t_, b=b: ("ymh%d" % (broff[b] + k),), NT, tiles=[t])
                                if b == 1:
                                    TT("dve", acc, acc, tmpm, ALU.add, r=("acc", "tmpm"), w=("acc",))
                                elif b == 2:
                                    TT("dve", mT[:, f, tt * 512:(tt + 1) * 512], acc, tmpm, ALU.add, r=("acc", "tmpm"), w=("mT%d" % f,))
                    for f in range(8):
                        wo, wok = LW(wview(w_out[l], f * 128, 128), 8, 128)
                        for tt in range(TPH):
                            t = hf * TPH + tt
                            xk = "xT%d_%d" % (f, t)
                            proj_fm(lambda t_, p, pk: STT(xT[:, f, t * 512:(t + 1) * 512], p[:], 0.5, xT[:, f, t * 512:(t + 1) * 512], ALU.mult, ALU.add,
                                                          r=(pk, xk), w=(xk,)),
                                    wo, wok, 8, lambda k, t_, tt=tt: mT[:, k, tt * 512:(tt + 1) * 512], lambda k, t_: ("mT%d" % k,), NT, tiles=[t])

                norm_to_hT(co["n2"] + l * 8)
                areset()
                zT = aalloc(NFC * HT, BF16).rearrange("p (a b) -> p a b", a=NFC)
                ur = [(aalloc(2 + HT), "urg"), (aalloc(2 + HT), "urv")]
                uc = [(aalloc(HT), "ucg"), (aalloc(HT), "ucv")]
                ftmp = aalloc(HT)
                for hf in range(2):
                    for c in range(NFC):
                        for gv in range(2):
                            col = gv * DFF + c * 128
                            wt, wk = LW(wview(ffn_up[l], col, 128), 8, 128)
                            u_, uk = ur[gv]
                            if hf == 0:
                                MEMSET("dve", u_[:, 0:2], 0.0, w=(uk,))
                            else:
                                p, pk = bank("mm")
                                t0 = hf * HT
                                for k in range(KC):
                                    MM(p[:, 0:2], wt[:, k, :], hT[:, k, t0 - 2:t0], k == 0, k == KC - 1,
                                       r=(wk, "hT%d_%d" % (k, (t0 - 2) // 512)), w=(pk,))
                                ACT(u_[:, 0:2], p[:, 0:2], AF.Copy, r=(pk,), w=(uk,))
                            fc = gv * NFC + c
                            w0 = co["fw"] + (l * 3) * 44 + fc
                            o_, ok = uc[gv]
                            w2c = pcol[:, w0 + 88:w0 + 89]
                            bbc = pcol[:, co["fb"] + l * 44 + fc:co["fb"] + l * 44 + fc + 1]

                            def ffn_evac(t, p, pk, u_=u_, uk=uk, o_=o_, ok=ok, w2c=w2c, bbc=bbc):
                                tl = t - hf * TPH
                                ACT(u_[:, 2 + tl * 512:2 + (tl + 1) * 512], p[:], AF.Copy, r=(pk,), w=(uk,))
                                ACT(o_[:, tl * 512:(tl + 1) * 512], p[:], AF.Identity, r=(pk, "pcol"), w=(ok,), bias=bbc, scale=w2c)
                            proj_fm(ffn_evac, wt, wk, 8, hT_rhs, hT_keys, NT, tiles=[hf * TPH + tt for tt in range(TPH)])
                            for j in range(2):
                                STT(o_, u_[:, j:j + HT], pcol[:, w0 + 44 * j:w0 + 44 * j + 1], o_, ALU.mult, ALU.add, r=(uk, "pcol", ok), w=(ok,))
                        gelu2_from(ftmp, uc[0][0], ("ucg",), ftmp, "ftmp", "ftmp")
                        STT(zT[:, c, :], ftmp, 0.5, uc[1][0], ALU.mult, ALU.mult, r=("ftmp", "ucv"), w=("zT%d" % c,))
                    for f in range(8):
                        wd = [LW(ffn_dn[l][k0 * 128:(k0 + n_) * 128, :].rearrange("(k p) n -> p k n", p=128)[:, :, f * 128:(f + 1) * 128], n_, 128)
                              for (k0, n_) in ((0, 8), (8, 8), (16, 6))]
                        for tt in range(TPH):
                            t = hf * TPH + tt
                            xk = "xT%d_%d" % (f, t)
                            p, pk = bank("mm")
                            for kk in range(NFC):
                                wt_, wk_ = wd[kk // 8]
                                MM(p[:], wt_[:, kk % 8, :], zT[:, kk, tt * 512:(tt + 1) * 512], kk == 0, kk == NFC - 1, r=(wk_, "zT%d" % kk), w=(pk,))
                            TT("dve", xT[:, f, t * 512:(t + 1) * 512], p[:], xT[:, f, t * 512:(t + 1) * 512], ALU.add, r=(pk, xk), w=(xk,))
                roles["mm"] = [0, 1]

            areset()
            sq2 = [(aalloc(512), "nsq0"), (aalloc(512), "nsq1")]
            ms = aalloc(S)
            of = aalloc(8 * 512).rearrange("p (a b) -> p a b", a=8)
            ot = [(aalloc(1024), "ot0"), (aalloc(1024), "ot1")]
            ocnt = 0
            rstd_all(sq2, ms)
            for t in range(NT):
                for f in range(KC):
                    STT(of[:, f, :], xT[:, f, t * 512:(t + 1) * 512], pcol[:, co["fin"] + f:co["fin"] + f + 1], ms[:, t * 512:(t + 1) * 512],
                        ALU.mult, ALU.mult, r=("xT%d_%d" % (f, t), "pcol", "ms"), w=("of%d" % f,))
                for blk in range(4):
                    o_, ok = ot[ocnt % 2]
                    ocnt += 1
                    for g in range(2):
                        p, pk = bank("mm")
                        for i in range(4):
                            f = g * 4 + i
                            TR(p[:, i * 128:(i + 1) * 128], of[:, f, blk * 128:(blk + 1) * 128], r=("of%d" % f,), w=(pk,))
                        if g == 0:
                            CP("dve", o_[:, 0:512], p[:], r=(pk,), w=(ok,))
                        else:
                            ACT(o_[:, 512:1024], p[:], AF.Copy, r=(pk,), w=(ok,))
                    r0 = t * 512 + blk * 128
                    DMA(out_d[s, r0:r0 + 128, :], o_, r=(ok,), w=("out",))
        S_.emit()
    return nc


_NC_CACHE = {}


def _make_in_maps(inputs, L, nseq, ncores, S):
    pc, pr, strips, cst = _host_params(inputs, L)
    shared = {
        "w_in": np.ascontiguousarray(inputs["w_in"][:L], np.float32),
        "rg_w_r": np.ascontiguousarray(inputs["rg_w_r"][:L], np.float32),
        "rg_w_i": np.ascontiguousarray(inputs["rg_w_i"][:L], np.float32),
        "w_br_rnn": np.ascontiguousarray(inputs["w_br_rnn"][:L], np.float32),
        "w_br_diff": np.ascontiguousarray(inputs["w_br_diff"][:L], np.float32),
        "w_br_fox": np.ascontiguousarray(inputs["w_br_fox"][:L], np.float32),
        "w_out": np.ascontiguousarray(inputs["w_out"][:L], np.float32),
        "ffn_up": np.ascontiguousarray(inputs["ffn_up"][:L], np.float32),
        "ffn_down": np.ascontiguousarray(inputs["ffn_down"][:L], np.float32),
        "pcols": pc, "prows": pr, "strips": strips, "consts": cst,
    }
    x = np.asarray(inputs["x"], np.float32)
    maps = []
    for c in range(ncores):
        m = dict(shared)
        m["x"] = np.ascontiguousarray(x[c * nseq:(c + 1) * nseq, :S])
        maps.append(m)
    return maps


def kernel(**inputs):
    inputs = {k: np.asarray(v) for k, v in inputs.items()}
    B, S, _ = inputs["x"].shape
    L = inputs["w_in"].shape[0]
    ncores = 8
    nseq = B // ncores
    key = (S, L, nseq)
    if key not in _NC_CACHE:
        _NC_CACHE[key] = build_nc(S=S, L=L, NSEQ=nseq)
    nc = _NC_CACHE[key]
    maps = _make_in_maps(inputs, L, nseq, ncores, S)
    res = run_bass_kernel_spmd(nc, maps, core_ids=list(range(ncores)))
    out = np.concatenate([np.asarray(r["out"]) for r in res.results], axis=0)
    return out.astype(np.float32)
```
